# Optimizing a Trainium2 kernel written in Bass

```python
import math
import jax, jax.numpy as jnp
from jax import lax
import numpy as np

D_MODEL = 2048
BATCH = 2
SEQ = 4096
DEPTH = 1

EPS = 1e-6
D_MIX = 2 * D_MODEL
CONV_WIDTH = 4
LRU_WIDTH = D_MIX // 2
LRU_HEADS = 16
LRU_HEAD_DIM = LRU_WIDTH // LRU_HEADS
LRU_C = 8.0
SSD_WIDTH = D_MIX - LRU_WIDTH
SSD_HEAD_DIM = 64
SSD_HEADS = SSD_WIDTH // SSD_HEAD_DIM
SSD_GROUPS = 4
HEADS_PER_GROUP = SSD_HEADS // SSD_GROUPS
SSD_STATE = 128
SSD_CHUNK = 128
SSD_GN = SSD_GROUPS * SSD_STATE
XBC_WIDTH = SSD_WIDTH + 2 * SSD_GN
O_LRU_GATE = 0
O_LRU_X = O_LRU_GATE + LRU_WIDTH
O_SSD_Z = O_LRU_X + LRU_WIDTH
O_SSD_XBC = O_SSD_Z + SSD_WIDTH
O_SSD_DT = O_SSD_XBC + XBC_WIDTH
IN_WIDTH = O_SSD_DT + SSD_HEADS
PEER_HEADS = 8
PEER_KEYS = 128
PEER_EXPERTS = PEER_KEYS * PEER_KEYS
PEER_TOPK = 16
PEER_QDIM = 256
PEER_HALF = PEER_QDIM // 2
PEER_BLOCK = 128

kernel_name = "hybrid_rglru_ssd_peer_adaln"


def rms_norm(x, g):
    xf = x.astype(jnp.float32)
    y = xf * lax.rsqrt(jnp.mean(xf * xf, axis=-1, keepdims=True) + EPS)
    return (y * g.astype(jnp.float32)).astype(x.dtype)


def causal_depthwise_conv(x, w, b):
    y = lax.conv_general_dilated(
        x, w[:, None, :].astype(x.dtype), window_strides=(1,),
        padding=[(CONV_WIDTH - 1, 0)], dimension_numbers=("NWC", "WIO", "NWC"),
        feature_group_count=x.shape[-1])
    return y + b.astype(x.dtype)


def _linear_combine(left, right):
    a_l, b_l = left
    a_r, b_r = right
    return a_l * a_r, a_r * b_l + b_r


def rg_lru(x, w_a, b_a, w_i, b_i, lam):
    bsz, seq, _ = x.shape
    xf = x.astype(jnp.float32).reshape(bsz, seq, LRU_HEADS, LRU_HEAD_DIM)
    r = jax.nn.sigmoid(jnp.einsum("bshi,hij->bshj", xf, w_a.astype(jnp.float32)) + b_a.astype(jnp.float32))
    i = jax.nn.sigmoid(jnp.einsum("bshi,hij->bshj", xf, w_i.astype(jnp.float32)) + b_i.astype(jnp.float32))
    log_a = -LRU_C * jax.nn.softplus(-lam.astype(jnp.float32)).reshape(LRU_HEADS, LRU_HEAD_DIM) * r
    a = jnp.exp(log_a).reshape(bsz, seq, LRU_WIDTH)
    bt = (jnp.sqrt(-jnp.expm1(2.0 * log_a)) * (i * xf)).reshape(bsz, seq, LRU_WIDTH)
    _, h = lax.associative_scan(_linear_combine, (a, bt), axis=1)
    return h.astype(x.dtype)


def ssd_chunked(x, dt, a, bm, cm):
    bsz, seq = x.shape[:2]
    nc = seq // SSD_CHUNK
    shp = (bsz, nc, SSD_CHUNK, SSD_GROUPS, HEADS_PER_GROUP)
    xc = (x * dt[..., None]).reshape(*shp, SSD_HEAD_DIM)
    a_cs = jnp.cumsum((dt * a).reshape(shp), axis=2)
    bc = bm.reshape(bsz, nc, SSD_CHUNK, SSD_GROUPS, SSD_STATE)
    cc = cm.reshape(bsz, nc, SSD_CHUNK, SSD_GROUPS, SSD_STATE)
    a_t = jnp.moveaxis(a_cs, 2, -1)
    seg = a_t[..., :, None] - a_t[..., None, :]
    causal = jnp.tril(jnp.ones((SSD_CHUNK, SSD_CHUNK), dtype=bool))
    decay_in = jnp.exp(jnp.where(causal, seg, -jnp.inf))
    cb = jnp.einsum("bclgn,bcsgn->bcgls", cc, bc)
    y_diag = jnp.einsum("bcgrls,bcsgrp->bclgrp", cb[:, :, :, None] * decay_in, xc)
    decay_to_end = jnp.exp(a_cs[:, :, -1:] - a_cs)
    states = jnp.einsum("bclgn,bclgrp->bcgrpn", bc, xc * decay_to_end[..., None])
    chunk_decay = jnp.exp(a_cs[:, :, -1])

    def carry_state(h, inp):
        dec, st = inp
        return dec[..., None, None] * h + st, h

    h0 = jnp.zeros((bsz, SSD_GROUPS, HEADS_PER_GROUP, SSD_HEAD_DIM, SSD_STATE), jnp.float32)
    _, prev = lax.scan(carry_state, h0, (jnp.moveaxis(chunk_decay, 1, 0), jnp.moveaxis(states, 1, 0)))
    prev = jnp.moveaxis(prev, 0, 1)
    y_off = jnp.einsum("bclgn,bcgrpn->bclgrp", cc, prev) * jnp.exp(a_cs)[..., None]
    return (y_diag + y_off).reshape(bsz, seq, SSD_HEADS, SSD_HEAD_DIM)


def mamba2_heads(z, xbc, dt_raw, conv_w, conv_b, dt_bias, a_log, d_skip, norm_g):
    bsz, seq, _ = z.shape
    xbc = jax.nn.silu(causal_depthwise_conv(xbc, conv_w, conv_b)).astype(jnp.float32)
    xs = xbc[..., :SSD_WIDTH].reshape(bsz, seq, SSD_HEADS, SSD_HEAD_DIM)
    bm = xbc[..., SSD_WIDTH:SSD_WIDTH + SSD_GN].reshape(bsz, seq, SSD_GROUPS, SSD_STATE)
    cm = xbc[..., SSD_WIDTH + SSD_GN:].reshape(bsz, seq, SSD_GROUPS, SSD_STATE)
    dt = jax.nn.softplus(dt_raw.astype(jnp.float32) + dt_bias.astype(jnp.float32))
    a = -jnp.exp(a_log.astype(jnp.float32))
    y = ssd_chunked(xs, dt, a, bm, cm) + d_skip.astype(jnp.float32)[:, None] * xs
    gshape = (bsz, seq, SSD_GROUPS, SSD_WIDTH // SSD_GROUPS)
    y = y.reshape(gshape) * jax.nn.silu(z.astype(jnp.float32)).reshape(gshape)
    y = y * lax.rsqrt(jnp.mean(y * y, axis=-1, keepdims=True) + EPS)
    return (y.reshape(bsz, seq, SSD_WIDTH) * norm_g.astype(jnp.float32)).astype(z.dtype)


def peer(h, w_q, sub_keys, u_tab, v_tab):
    bsz, seq, d = h.shape
    q = jnp.einsum("bsd,de->bse", h, w_q).astype(jnp.float32).reshape(bsz, seq, PEER_HEADS, 2, PEER_HALF)
    sub = jnp.einsum("bshtd,htkd->bshtk", q, sub_keys.astype(jnp.float32))
    sub_val, sub_idx = lax.top_k(sub, PEER_TOPK)
    cand_val = (sub_val[..., 0, :, None] + sub_val[..., 1, None, :]).reshape(bsz, seq, PEER_HEADS, PEER_TOPK * PEER_TOPK)
    cand_idx = (sub_idx[..., 0, :, None] * PEER_KEYS + sub_idx[..., 1, None, :]).reshape(bsz, seq, PEER_HEADS, PEER_TOPK * PEER_TOPK)
    top_val, top_pos = lax.top_k(cand_val, PEER_TOPK)
    expert = jnp.take_along_axis(cand_idx, top_pos, axis=-1)
    gate = jax.nn.softmax(top_val, axis=-1)
    nblk = (bsz * seq) // PEER_BLOCK
    hb = h.reshape(nblk, PEER_BLOCK, d)
    eb = expert.reshape(nblk, PEER_BLOCK, PEER_HEADS, PEER_TOPK)
    gb = gate.reshape(nblk, PEER_BLOCK, PEER_HEADS, PEER_TOPK)

    def block(args):
        xt, e, g = args
        u = jnp.take(u_tab, e, axis=0)
        act = jax.nn.gelu(jnp.einsum("td,thkd->thk", xt, u).astype(jnp.float32))
        v = jnp.take(v_tab, e, axis=0)
        return jnp.einsum("thk,thkd->td", (g * act).astype(v.dtype), v)

    out = lax.map(block, (hb, eb, gb))
    return out.reshape(bsz, seq, d).astype(h.dtype)


def hybrid_layer(x, mod, norm1_g, w_in, lru_conv_w, lru_conv_b, lru_w_a, lru_b_a, lru_w_i, lru_b_i,
                 lru_lambda, ssd_conv_w, ssd_conv_b, ssd_dt_bias, ssd_a_log, ssd_d, ssd_norm_g, w_out,
                 norm2_g, peer_w_q, peer_sub_keys, peer_u, peer_v):
    shift1, scale1, gate1, shift2, scale2, gate2 = jnp.split(mod[:, None, :], 6, axis=-1)
    h = rms_norm(x, norm1_g) * (1.0 + scale1) + shift1
    proj = jnp.einsum("bsd,de->bse", h, w_in)
    lru_gate = proj[..., O_LRU_GATE:O_LRU_GATE + LRU_WIDTH]
    lru_x = causal_depthwise_conv(proj[..., O_LRU_X:O_LRU_X + LRU_WIDTH], lru_conv_w, lru_conv_b)
    lru_out = jax.nn.gelu(lru_gate) * rg_lru(lru_x, lru_w_a, lru_b_a, lru_w_i, lru_b_i, lru_lambda)
    ssd_out = mamba2_heads(proj[..., O_SSD_Z:O_SSD_Z + SSD_WIDTH],
                           proj[..., O_SSD_XBC:O_SSD_XBC + XBC_WIDTH],
                           proj[..., O_SSD_DT:O_SSD_DT + SSD_HEADS],
                           ssd_conv_w, ssd_conv_b, ssd_dt_bias, ssd_a_log, ssd_d, ssd_norm_g)
    mix = jnp.einsum("bse,ed->bsd", jnp.concatenate([lru_out, ssd_out], axis=-1), w_out)
    x = x + gate1 * mix
    h2 = rms_norm(x, norm2_g) * (1.0 + scale2) + shift2
    return x + gate2 * peer(h2, peer_w_q, peer_sub_keys, peer_u, peer_v)


def setup_inputs(seed: int = 0) -> dict:
    key = jax.random.key(seed)
    ks = jax.random.split(key, 32)
    f32 = jnp.float32

    def nrm(k, shape, scale):
        return jax.random.normal(k, shape, f32) * scale

    L = DEPTH
    a0 = jax.random.uniform(ks[12], (L, LRU_WIDTH), f32, minval=0.9, maxval=0.999)
    p = a0 ** (1.0 / LRU_C)
    dt0 = jnp.exp(jax.random.uniform(ks[15], (L, SSD_HEADS), f32, minval=math.log(1e-3), maxval=math.log(1e-1)))
    return {
        "x": nrm(ks[0], (BATCH, SEQ, D_MODEL), 1.0),
        "c": nrm(ks[1], (BATCH, D_MODEL), 1.0),
        "w_ada": nrm(ks[2], (L, D_MODEL, 6 * D_MODEL), 0.5 * D_MODEL ** -0.5),
        "b_ada": nrm(ks[3], (L, 6 * D_MODEL), 0.02),
        "norm1_g": 1.0 + nrm(ks[4], (L, D_MODEL), 0.02),
        "w_in": nrm(ks[5], (L, D_MODEL, IN_WIDTH), D_MODEL ** -0.5),
        "lru_conv_w": nrm(ks[6], (L, CONV_WIDTH, LRU_WIDTH), CONV_WIDTH ** -0.5),
        "lru_conv_b": nrm(ks[7], (L, LRU_WIDTH), 0.02),
        "lru_w_a": nrm(ks[8], (L, LRU_HEADS, LRU_HEAD_DIM, LRU_HEAD_DIM), LRU_HEAD_DIM ** -0.5),
        "lru_b_a": nrm(ks[9], (L, LRU_HEADS, LRU_HEAD_DIM), 0.02),
        "lru_w_i": nrm(ks[10], (L, LRU_HEADS, LRU_HEAD_DIM, LRU_HEAD_DIM), LRU_HEAD_DIM ** -0.5),
        "lru_b_i": nrm(ks[11], (L, LRU_HEADS, LRU_HEAD_DIM), 0.02),
        "lru_lambda": jnp.log(p) - jnp.log1p(-p),
        "ssd_conv_w": nrm(ks[13], (L, CONV_WIDTH, XBC_WIDTH), CONV_WIDTH ** -0.5),
        "ssd_conv_b": nrm(ks[14], (L, XBC_WIDTH), 0.02),
        "ssd_dt_bias": dt0 + jnp.log(-jnp.expm1(-dt0)),
        "ssd_a_log": jnp.log(jax.random.uniform(ks[16], (L, SSD_HEADS), f32, minval=1.0, maxval=16.0)),
        "ssd_d": 1.0 + nrm(ks[17], (L, SSD_HEADS), 0.02),
        "ssd_norm_g": 1.0 + nrm(ks[18], (L, SSD_WIDTH), 0.02),
        "w_out": nrm(ks[19], (L, D_MIX, D_MODEL), D_MIX ** -0.5),
        "norm2_g": 1.0 + nrm(ks[20], (L, D_MODEL), 0.02),
        "peer_w_q": nrm(ks[21], (L, D_MODEL, PEER_HEADS * PEER_QDIM), D_MODEL ** -0.5),
        "peer_sub_keys": nrm(ks[22], (L, PEER_HEADS, 2, PEER_KEYS, PEER_HALF), PEER_HALF ** -0.5),
        "peer_u": nrm(ks[23], (L, PEER_EXPERTS, D_MODEL), D_MODEL ** -0.5),
        "peer_v": nrm(ks[24], (L, PEER_EXPERTS, D_MODEL), PEER_HEADS ** -0.5),
        "final_norm_g": 1.0 + nrm(ks[25], (D_MODEL,), 0.02),
    }


def reference(x, c, w_ada, b_ada, norm1_g, w_in, lru_conv_w, lru_conv_b, lru_w_a, lru_b_a, lru_w_i,
              lru_b_i, lru_lambda, ssd_conv_w, ssd_conv_b, ssd_dt_bias, ssd_a_log, ssd_d, ssd_norm_g,
              w_out, norm2_g, peer_w_q, peer_sub_keys, peer_u, peer_v, final_norm_g):
    c_act = jax.nn.silu(c)
    for l in range(DEPTH):
        mod = jnp.einsum("bd,de->be", c_act, w_ada[l]) + b_ada[l]
        x = hybrid_layer(x, mod, norm1_g[l], w_in[l], lru_conv_w[l], lru_conv_b[l], lru_w_a[l], lru_b_a[l],
                         lru_w_i[l], lru_b_i[l], lru_lambda[l], ssd_conv_w[l], ssd_conv_b[l], ssd_dt_bias[l],
                         ssd_a_log[l], ssd_d[l], ssd_norm_g[l], w_out[l], norm2_g[l], peer_w_q[l],
                         peer_sub_keys[l], peer_u[l], peer_v[l])
    return rms_norm(x, final_norm_g)
```

```python
import numpy as np
import concourse.bass as bass
import concourse.mybir as mybir
from concourse.bass_utils import run_bass_kernel_spmd
from contextlib import ExitStack

F32 = mybir.dt.float32
BF16 = mybir.dt.bfloat16
U32 = mybir.dt.uint32
AF = mybir.ActivationFunctionType
ALU = mybir.AluOpType
AX = mybir.AxisListType

D = 2048
SEQ = 4096
NSL = 4
TT = 512
NTILE = SEQ // TT
WSL = 2312
G0, X0, Z0, SX0, B0, C0, DT0 = 0, 512, 1024, 1536, 2048, 2176, 2304
EPS = 1e-6
NEG = -1.0e30
GC1 = 0.044715
GC2 = 1.5957691216057308
DEBUG = {}


class Buf:
    __slots__ = ("name", "w", "r")

    def __init__(self, name=""):
        self.name = name
        self.w = []
        self.r = []


class Sched:
    ENG = ("tensor", "vector", "scalar", "gpsimd", "sync")

    def __init__(self, nc, es, ndma=14):
        self.nc = nc
        self.es = es
        self.q = {e: [] for e in self.ENG}
        self.sems = []
        self.cur = {}
        self.cnt = {}
        for e in self.ENG:
            self._new_eng_sem(e)
        self.seen = {e: {} for e in self.ENG}
        self.dpool = {}
        for e in ("sync", "gpsimd", "scalar"):
            self.dpool[e] = [[self._sem("d%s%d" % (e, i)), 0] for i in range(ndma)]
        self.dnext = {e: 0 for e in self.dpool}
        self.nops = {e: 0 for e in self.ENG}

    def _sem(self, name):
        s = self.es.enter_context(self.nc.semaphore(name))
        self.sems.append(s)
        return len(self.sems) - 1

    def _new_eng_sem(self, e):
        self.cur[e] = self._sem("e%s%d" % (e, len(self.sems)))
        self.cnt[e] = 0

    def _waits(self, eng, reads, writes, extra=(), par=False):
        deps = {}

        def add(tok):
            if tok is None:
                return
            k, v = tok
            if deps.get(k, 0) < v:
                deps[k] = v
        for b in reads:
            for t in b.w:
                add(t)
        for b in writes:
            if not par:
                for t in b.w:
                    add(t)
            for t in b.r:
                add(t)
        for t in extra:
            add(t)
        for k, v in deps.items():
            if eng == "tensor" and k == self.cur["tensor"]:
                continue
            if self.seen[eng].get(k, 0) >= v:
                continue
            self.seen[eng][k] = v
            sem = self.sems[k]
            self.q[eng].append(lambda e, sem=sem, v=v: e.wait_ge(sem, v))

    def _mark(self, tok, reads, writes, par=False):
        for b in reads:
            if len(b.r) > 24:
                mx = {}
                for k, v in b.r:
                    if mx.get(k, 0) < v:
                        mx[k] = v
                b.r = list(mx.items())
            b.r.append(tok)
        for b in writes:
            if par:
                if len(b.w) > 24:
                    mx = {}
                    for k, v in b.w:
                        if mx.get(k, 0) < v:
                            mx[k] = v
                    b.w = list(mx.items())
                b.w.append(tok)
            else:
                b.w = [tok]
            b.r = []

    def begin_chain(self):
        self.rec = []

    def end_chain(self):
        r = self.rec
        self.rec = None
        return r

    def interleave(self, chains):
        n = max(len(c) for c in chains) if chains else 0
        for i in range(n):
            for c in chains:
                if i < len(c):
                    kind, eng, fn, reads, writes, par = c[i]
                    if kind == "op":
                        self.op(eng, fn, reads, writes)
                    else:
                        self.dma(eng, fn, reads, writes, par)

    def op(self, eng, fn, reads=(), writes=()):
        if getattr(self, "rec", None) is not None:
            self.rec.append(("op", eng, fn, tuple(reads), tuple(writes), False))
            return None
        self._waits(eng, reads, writes)
        if self.cnt[eng] >= 30000:
            self._new_eng_sem(eng)
        self.cnt[eng] += 1
        self.nops[eng] += 1
        k = self.cur[eng]
        tok = (k, self.cnt[eng])
        sem = self.sems[k]
        self.q[eng].append(lambda e, fn=fn, sem=sem: fn(e).then_inc(sem, 1))
        self._mark(tok, reads, writes)
        return tok

    def dma(self, eng, fn, reads=(), writes=(), par=False):
        if getattr(self, "rec", None) is not None:
            self.rec.append(("dma", eng, fn, tuple(reads), tuple(writes), par))
            return None
        pool = self.dpool[eng]
        i = self.dnext[eng]
        self.dnext[eng] = (i + 1) % len(pool)
        k, val = pool[i]
        extra = [(k, val)] if val > 0 else []
        self._waits(eng, reads, writes, extra, par)
        pool[i][1] = val + 16
        tok = (k, val + 16)
        sem = self.sems[k]
        self.q[eng].append(lambda e, fn=fn, sem=sem: fn(e).then_inc(sem, 16))
        self._mark(tok, reads, writes, par)
        return tok

    def wait_all(self, eng, bufs):
        self._waits(eng, bufs, bufs)

    def emit(self, name=None):
        self.nemit = getattr(self, "nemit", 0) + 1
        with self.nc.named_scope(name or ("ph%d" % self.nemit)), self.nc.Block() as block:
            for e in self.ENG:
                lst = self.q[e]

                def body(engine, lst=lst):
                    for f in lst:
                        f(engine)
                getattr(block, e)(body)
        self.q = {e: [] for e in self.ENG}


class Ops:
    def __init__(self, S):
        self.S = S

    def tt(self, eng, out, in0, in1, op, r, w):
        self.S.op(eng, lambda e: e.tensor_tensor(out, in0, in1, op), r, w)

    def ts(self, eng, out, in0, s1, s2, op0, op1, r, w):
        if op1 is None:
            self.S.op(eng, lambda e: e.tensor_scalar(out, in0, s1, None, op0), r, w)
        else:
            self.S.op(eng, lambda e: e.tensor_scalar(out, in0, s1, s2, op0, op1), r, w)

    def stt(self, out, in0, sc, in1, op0, op1, r, w):
        self.S.op("vector", lambda e: e.scalar_tensor_tensor(out, in0, sc, in1, op0, op1), r, w)

    def act(self, out, in_, func, r, w, **kw):
        self.S.op("scalar", lambda e: e.activation(out, in_, func, **kw), r, w)

    def cp(self, eng, out, in_, r, w):
        if eng == "scalar":
            self.S.op(eng, lambda e: e.activation(out, in_, AF.Identity), r, w)
        else:
            self.S.op(eng, lambda e: e.tensor_copy(out, in_), r, w)

    def mm(self, out, lhsT, rhs, start, stop, r, w):
        self.S.op("tensor", lambda e: e.matmul(out, lhsT=lhsT, rhs=rhs, start=start, stop=stop), r, w)

    def tr(self, out, in_, ident, r, w):
        self.S.op("tensor", lambda e: e.transpose(out, in_, ident), r, w)

    def dma(self, eng, out, in_, r, w, par=False):
        self.S.dma(eng, lambda e: e.dma_start(out=out, in_=in_), r, w, par)

    def memset(self, eng, out, val, w):
        self.S.op(eng, lambda e: e.memset(out, val), (), w)

    def red(self, out, in_, r, w):
        self.S.op("vector", lambda e: e.reduce_sum(out, in_, axis=AX.X), r, w)

    def recip(self, out, in_, r, w):
        self.S.op("vector", lambda e: e.reciprocal(out, in_), r, w)

    def scan(self, out, d0, d1, init, r, w):
        self.S.op("vector", lambda e: e.tensor_tensor_scan(out, d0, d1, init, ALU.mult, ALU.add), r, w)


def mkap(base, dims):
    return bass.AP(base.tensor, base.offset, [list(base.ap[0])] + [list(d) for d in dims])


class Pool:
    def __init__(self, nc, es, name, n, shape, dt, psum=False):
        self.t = []
        self.b = []
        for i in range(n):
            if psum:
                t = es.enter_context(nc.psum_tensor("%s%d" % (name, i), shape, dt))
            else:
                t = es.enter_context(nc.sbuf_tensor("%s%d" % (name, i), shape, dt))
            self.t.append(t)
            self.b.append(Buf("%s%d" % (name, i)))
        self.i = 0

    def get(self):
        i = self.i
        self.i = (i + 1) % len(self.t)
        return self.t[i], self.b[i]


class SubPool:
    def __init__(self, pool, idxs):
        self.t = [pool.t[i] for i in idxs]
        self.b = [pool.b[i] for i in idxs]
        self.i = 0

    def get(self):
        i = self.i
        self.i = (i + 1) % len(self.t)
        return self.t[i], self.b[i]


def build():
    nc = bass.Bass("TRN2", target_bir_lowering=False)

    def din(name, shape, dt=F32):
        return nc.dram_tensor(name, list(shape), dt, kind="ExternalInput").ap()

    xb = din("xb", [SEQ, D])
    xq = din("xq", [1024, D])
    cfm = din("cfm", [128, 16])
    w_ada = din("w_ada", [D, 6 * D])
    bada_bc = din("bada_bc", [128, 6 * D])
    g1_bc = din("g1_bc", [128, D])
    g2_bc = din("g2_bc", [128, D])
    gf_bc = din("gf_bc", [128, D])
    sel = din("sel", [128, 4])
    win = din("win", [NSL, D, WSL])
    lru_cw = din("lru_cw", [NSL, 128, 16])
    lru_cb = din("lru_cb", [NSL, 128, 4])
    lru_wa = din("lru_wa", [NSL, 4, 128, 128])
    lru_wi = din("lru_wi", [NSL, 4, 128, 128])
    lru_ba = din("lru_ba", [NSL, 128, 4])
    lru_bi = din("lru_bi", [NSL, 128, 4])
    lru_lam = din("lru_lam", [NSL, 128, 4])
    ssd_cw = din("ssd_cw", [NSL, 128, 24])
    ssd_cb = din("ssd_cb", [NSL, 128, 6])
    ssd_dtb = din("ssd_dtb", [NSL, 128, 8])
    ssd_alog = din("ssd_alog", [NSL, 128, 8])
    ssd_d = din("ssd_d", [NSL, 128, 512])
    ssd_ng = din("ssd_ng", [NSL, 128, 512])
    wout = din("wout", [4096, D])
    wq = din("wq", [D, D])
    keysT = din("keysT", [16, 128, 128])
    UT = din("UT", [D, 16384])
    Vt = din("V", [16384, D])
    y = nc.dram_tensor("y", [1024, D], F32, kind="ExternalOutput").ap()
    dbg_out = None
    if DEBUG.get("shape"):
        dbg_out = nc.dram_tensor("dbg", list(DEBUG["shape"]), F32, kind="ExternalOutput").ap()

    modsc = nc.dram_tensor("modsc", [128, 6 * D], F32).ap()
    mixd = nc.dram_tensor("mixd", [4, 8, 128, SEQ], BF16).ap()
    x1d = nc.dram_tensor("x1d", [1024, D], F32).ap()
    Gd = nc.dram_tensor("Gd", [128, 128, 1024], BF16).ap()

    stop_after = DEBUG.get("stop_after", 99)

    with ExitStack() as ges:
        S = Sched(nc, ges)
        O = Ops(S)

        def gsb(name, shape, dt):
            return ges.enter_context(nc.sbuf_tensor(name, shape, dt))

        ident32 = gsb("ident32", [128, 128], F32); Bc = Buf("consts")
        identb = gsb("identb", [128, 128], BF16)
        ones32 = gsb("ones32", [128, 128], F32)
        tri32 = gsb("tri32", [128, 128], F32)
        ustr32 = gsb("ustr32", [128, 128], F32)
        iotaf = gsb("iotaf", [128, 128], F32)
        fm4 = gsb("fm4", [128, 4, 16], F32); Bfm4 = Buf("fm4")
        selt = gsb("selt", [128, 4], F32)
        dfl = gsb("dfl", [128, 128], F32); Bdfl = Buf("dfl")

        O.S.op("gpsimd", lambda e: e.iota(dfl[:], pattern=[[1, 128]], base=0, channel_multiplier=-1,
                                          allow_small_or_imprecise_dtypes=True), (), [Bdfl])
        O.S.op("gpsimd", lambda e: e.iota(iotaf[:], pattern=[[1, 128]], base=0, channel_multiplier=0,
                                          allow_small_or_imprecise_dtypes=True), (), [Bc])
        O.ts("vector", ident32[:], dfl[:], 0.0, None, ALU.is_equal, None, [Bdfl], [Bc])
        O.ts("vector", tri32[:], dfl[:], 0.0, None, ALU.is_ge, None, [Bdfl], [Bc])
        O.ts("vector", ustr32[:], dfl[:], 0.0, None, ALU.is_lt, None, [Bdfl], [Bc])
        O.cp("vector", identb[:], ident32[:], [Bc], [Bc])
        O.memset("vector", ones32[:], 1.0, [Bc])
        O.dma("sync", selt[:], sel, [], [Bc])

        with ExitStack() as es:
            def sb(name, shape, dt):
                return es.enter_context(nc.sbuf_tensor(name, shape, dt))
            psp = Pool(nc, es, "p0ps", 4, [128, 512], F32, psum=True)
            wa = Pool(nc, es, "wa", 2, [128, 16, 512], F32)
            bad = Pool(nc, es, "bad", 2, [128, 512], F32)
            modbc = sb("modbc", [128, 6 * D], F32); Bmod = Buf("modbc")
            g12 = sb("g12", [128, D], F32); Bg12 = Buf("g12")
            crep = sb("crep", [128, 16, 128], F32); Bcrep = Buf("crep")
            cin = sb("cin", [128, 16], F32); Bcin = Buf("cin")
            cact = sb("cact", [128, 16], F32); Bcact = Buf("cact")
            tmpd = Pool(nc, es, "tmpd", 2, [128, 128], F32)

            O.dma("sync", cin[:], cfm, [], [Bcin])
            O.act(cact[:], cin[:], AF.Silu, [Bcin], [Bcact])
            for k in range(16):
                O.ts("vector", crep[:, k, :], ones32[:], cact[:, k:k + 1], None, ALU.mult, None, [Bc, Bcact], [Bcrep])
            wav = w_ada.rearrange("(k p) e -> p k e", p=128)
            for cb in range(24):
                wt, wb_ = wa.get()
                bt_, bb_ = bad.get()
                cs = slice(cb * 512, (cb + 1) * 512)
                for k in range(16):
                    O.dma("sync", wt[:, k, :], w_ada[k * 128:(k + 1) * 128, cs], [], [wb_], par=(k > 0))
                O.dma("sync", bt_[:], bada_bc[:, cs], [], [bb_])
                pt, pb = psp.get()
                for k in range(16):
                    O.mm(pt[:], crep[:, k, :], wt[:, k, :], k == 0, k == 15, [Bcrep, wb_], [pb])
                O.tt("vector", modbc[:, cs], pt[:], bt_[:], ALU.add, [pb, bb_], [Bmod])
            O.dma("sync", g12[:], g1_bc, [], [Bg12])
            O.stt(modbc[:, 2048:4096], modbc[:, 2048:4096], 1.0, g12[:], ALU.add, ALU.mult, [Bmod, Bg12], [Bmod])
            O.dma("sync", g12[:], g2_bc, [Bg12], [Bg12])
            O.stt(modbc[:, 8192:10240], modbc[:, 8192:10240], 1.0, g12[:], ALU.add, ALU.mult, [Bmod, Bg12], [Bmod])
            for v, sec in enumerate((0, 1, 3, 4)):
                for k in range(16):
                    td, tb_ = tmpd.get()
                    c0 = sec * 2048 + k * 128
                    O.tt("vector", td[:], modbc[:, c0:c0 + 128], ident32[:], ALU.mult, [Bmod, Bc], [tb_])
                    O.red(fm4[:, v, k:k + 1], td[:], [tb_], [Bfm4])
            Bmodsc = Buf("modsc")
            for i6 in range(6):
                O.dma("sync", modsc[:, i6 * 2048:(i6 + 1) * 2048], modbc[:, i6 * 2048:(i6 + 1) * 2048], [Bmod], [Bmodsc])
            if stop_after == 0:
                for i6 in range(6):
                    O.dma("sync", dbg_out[:, i6 * 2048:(i6 + 1) * 2048], modbc[:, i6 * 2048:(i6 + 1) * 2048], [Bmod], [Bmodsc])
            S.wait_all("sync", [Bmodsc])
            S.emit()

        Bmix = Buf("mixd")
        if stop_after >= 1:
            for j in range(DEBUG.get("nsl", NSL)):
                phase1_slice(nc, S, O, j, locals())
        if stop_after == 1:
            with ExitStack() as es:
                dt_ = es.enter_context(nc.sbuf_tensor("dbgt", [128, 4096], BF16)); bdt = Buf()
                df_ = es.enter_context(nc.sbuf_tensor("dbgf", [128, 4096], F32)); bdf = Buf()
                for cc in range(8):
                    O.dma("sync", dt_[:], mixd[0, cc], [Bmix], [bdt])
                    O.cp("vector", df_[:], dt_[:], [bdt], [bdf])
                    O.dma("sync", dbg_out[cc * 128:(cc + 1) * 128, :], df_[:], [bdf], [Bmix])
                S.wait_all("sync", [Bmix])
                S.emit()
        if stop_after >= 2:
            phase2(nc, S, O, locals())
    return nc


def phase1_slice(nc, S, O, j, G):
    xb, win, mixd = G["xb"], G["win"], G["mixd"]
    ident32, identb, ones32, tri32, ustr32 = G["ident32"], G["identb"], G["ones32"], G["tri32"], G["ustr32"]
    fm4, Bc, Bfm4, Bmix = G["fm4"], G["Bc"], G["Bfm4"], G["Bmix"]
    ntile = DEBUG.get("ntile", NTILE)
    with ExitStack() as es:
        def sb(name, shape, dt):
            return es.enter_context(nc.sbuf_tensor("%s_%d" % (name, j), shape, dt))
        psp = Pool(nc, es, "p1ps%d" % j, 8, [128, 512], F32, psum=True)
        tmp = Pool(nc, es, "p1t%d" % j, 16, [128, 512], F32)
        sm = Pool(nc, es, "p1s%d" % j, 24, [128, 16], F32)
        smw = Pool(nc, es, "p1w%d" % j, 8, [128, 64], F32)
        s16 = Pool(nc, es, "p1h%d" % j, 6, [128, 128], BF16)
        pgen, pyP, pzP, phg = SubPool(psp, [0, 1, 2]), SubPool(psp, [3, 4]), SubPool(psp, [5]), SubPool(psp, [6, 7])
        W = sb("W", [128, 16, WSL], BF16); BW = Buf("W")
        xin = Pool(nc, es, "xin%d" % j, 1, [128, D], F32)
        xn = Pool(nc, es, "xn%d" % j, 1, [128, D], BF16)
        hbuf = [sb("h0", [128, 16, TT], BF16), sb("h1", [128, 16, TT], BF16)]
        Bhs = [Buf("h0"), Buf("h1")]
        h, Bh = hbuf[0], Bhs[0]
        xh = [sb("xh%d" % c, [128, TT + 3], F32) for c in range(10)]
        Bxh = [Buf("xh%d" % c) for c in range(10)]
        hst = sb("hst", [128, 4], F32); Bhst = [Buf() for _ in range(4)]
        prm = sb("prm", [128, 16 + 4 + 4 + 4 + 4 + 24 + 6 + 8 + 8], F32); Bprm = Buf("prm")
        cvec = sb("cvec", [128, 8], F32); Bcv = Buf("cvec")
        abc = sb("abc", [128, 8], F32); Babc = Buf("abc")
        dbc = sb("dbc", [128, 512], F32)
        ngb = sb("ngb", [128, 512], F32)
        wab = sb("wab", [128, 8, 128], BF16); Bwab = Buf("wab")
        xsf = [sb("xsf%d" % c, [128, TT], F32) for c in range(4)]
        Bxsf = [Buf() for _ in range(4)]
        BCf = sb("BCf", [128, 2, TT], BF16); BBC = [Buf(), Buf()]
        Lh = Pool(nc, es, "Lh%d" % j, 1, [128, 8, 128], F32)
        prev = sb("prev", [128, 512], F32); Bprev = Buf("prev")
        prevb = sb("prevb", [128, 512], BF16); Bprevb = Buf("prevb")
        mos = sb("mos", [128, 4, TT], BF16); Bmos = Buf("mos")
        P_LCW, P_LCB, P_LBA, P_LBI, P_LAM, P_SCW, P_SCB, P_DTB, P_ALOG = 0, 16, 20, 24, 28, 32, 56, 62, 70

        for k in range(16):
            O.dma("gpsimd", W[:, k, :], win[j, k * 128:(k + 1) * 128, :], [], [BW], par=(k > 0))
        for h_ in range(4):
            O.dma("gpsimd", wab[:, h_, :], G["lru_wa"][j, h_], [], [Bwab])
            O.dma("gpsimd", wab[:, 4 + h_, :], G["lru_wi"][j, h_], [], [Bwab])
        for (off, n, src) in ((P_LCW, 16, "lru_cw"), (P_LCB, 4, "lru_cb"), (P_LBA, 4, "lru_ba"), (P_LBI, 4, "lru_bi"),
                              (P_LAM, 4, "lru_lam"), (P_SCW, 24, "ssd_cw"), (P_SCB, 6, "ssd_cb"), (P_DTB, 8, "ssd_dtb"),
                              (P_ALOG, 8, "ssd_alog")):
            O.dma("sync", prm[:, off:off + n], G[src][j], [], [Bprm])
        O.dma("sync", dbc[:], G["ssd_d"][j], [], [Bprm])
        O.dma("sync", ngb[:], G["ssd_ng"][j], [], [Bprm])
        O.act(cvec[:, 0:4], prm[:, P_LAM:P_LAM + 4], AF.Exp, [Bprm], [Bcv], scale=-1.0)
        O.act(cvec[:, 0:4], cvec[:, 0:4], AF.Ln, [Bcv], [Bcv], bias=1.0)
        O.ts("vector", cvec[:, 4:8], cvec[:, 0:4], -16.0, None, ALU.mult, None, [Bcv], [Bcv])
        O.ts("vector", cvec[:, 0:4], cvec[:, 0:4], -8.0, None, ALU.mult, None, [Bcv], [Bcv])
        O.act(abc[:], prm[:, P_ALOG:P_ALOG + 8], AF.Exp, [Bprm], [Babc])
        O.ts("vector", abc[:], abc[:], -1.0, None, ALU.mult, None, [Babc], [Babc])
        for c in range(10):
            O.memset("gpsimd", xh[c][:, 0:3], 0.0, [Bxh[c]])
        O.memset("gpsimd", prev[:], 0.0, [Bprev])
        O.memset("gpsimd", prevb[:], 0.0, [Bprevb])

        def inproj(col, n=128):
            pt, pb = psp.get()
            for k in range(16):
                O.mm(pt[0:n, :], W[:, k, col:col + n], h[:, k, :], k == 0, k == 15, [BW, Bh], [pb])
            return pt, pb

        def conv(c, pt, pb, wcol, bcol):
            O.cp("scalar", xh[c][:, 3:TT + 3], pt[:], [pb], [Bxh[c]])
            r, rb = tmp.get()
            O.ts("vector", r[:], xh[c][:, 0:TT], prm[:, wcol:wcol + 1], prm[:, bcol:bcol + 1], ALU.mult, ALU.add,
                 [Bxh[c], Bprm], [rb])
            for k in range(1, 4):
                O.stt(r[:], xh[c][:, k:k + TT], prm[:, wcol + k:wcol + k + 1], r[:], ALU.mult, ALU.add,
                      [Bxh[c], Bprm, rb], [rb])
            O.cp("gpsimd", xh[c][:, 0:3], xh[c][:, TT:TT + 3], [Bxh[c]], [Bxh[c]])
            return r, rb

        def hgen(tix):
            hh_, Bhh_ = hbuf[tix % 2], Bhs[tix % 2]
            tt0 = tix * TT
            for tb in range(4):
                xt, xtb = xin.get()
                xnt, xnb = xn.get()
                O.dma("sync", xt[:], xb[tt0 + tb * 128:tt0 + (tb + 1) * 128, :], [], [xtb])
                st, stb = sm.get()
                O.memset("gpsimd", st[:, 0:1], 0.0, [stb])
                O.act(xnt[:], xt[:], AF.Square, [xtb], [xnb, stb], accum_out=st[:, 0:1])
                O.act(st[:, 1:2], st[:, 0:1], AF.Sqrt, [stb], [stb], scale=1.0 / D, bias=EPS)
                O.recip(st[:, 2:3], st[:, 1:2], [stb], [stb])
                O.ts("vector", xnt[:], xt[:], st[:, 2:3], None, ALU.mult, None, [xtb, stb], [xnb])
                for half in range(2):
                    pt, pb = phg.get()
                    ptb = pt[:].bitcast(BF16)
                    for kk in range(8):
                        k = half * 8 + kk
                        O.tr(ptb[:, kk * 128:(kk + 1) * 128], xnt[:, k * 128:(k + 1) * 128], identb[:], [xnb, Bc], [pb])
                    for kk in range(8):
                        k = half * 8 + kk
                        dst = hh_[:, k, tb * 128:(tb + 1) * 128]
                        src = ptb[:, kk * 128:(kk + 1) * 128]
                        if kk % 2 == 0:
                            O.act(dst, src, AF.Identity, [pb, Bfm4], [Bhh_], scale=fm4[:, 1, k:k + 1], bias=fm4[:, 0, k:k + 1])
                        else:
                            O.ts("vector", dst, src, fm4[:, 1, k:k + 1], fm4[:, 0, k:k + 1], ALU.mult, ALU.add,
                                 [pb, Bfm4], [Bhh_])

        hgen(0)
        for ti in range(ntile):
            t0 = ti * TT
            h, Bh = hbuf[ti % 2], Bhs[ti % 2]
            def lru_chain(cc):
                pg, pgb = inproj(G0 + cc * 128)
                px, pxb = inproj(X0 + cc * 128)
                sA, sAb = tmp.get()
                sB, sBb = tmp.get()
                O.act(sA[:], pg[:], AF.Square, [pgb], [sAb])
                O.ts("vector", sA[:], sA[:], GC1, 1.0, ALU.mult, ALU.add, [sAb], [sAb])
                O.tt("vector", sA[:], sA[:], pg[:], ALU.mult, [sAb, pgb], [sAb])
                O.act(sB[:], sA[:], AF.Sigmoid, [sAb], [sBb], scale=GC2)
                O.tt("vector", sB[:], sB[:], pg[:], ALU.mult, [sBb, pgb], [sBb])
                xc, xcb_ = conv(cc, px, pxb, P_LCW + cc * 4, P_LCB + cc)
                xb16v = sA[:].bitcast(BF16)[:, 0:TT]
                O.cp("scalar", xb16v, xc[:], [xcb_], [sAb])
                pa, pab = psp.get()
                pi, pib = psp.get()
                O.mm(pa[:], wab[:, cc, :], xb16v, True, True, [Bwab, sAb], [pab])
                O.mm(pi[:], wab[:, 4 + cc, :], xb16v, True, True, [Bwab, sAb], [pib])
                r_, rb_ = tmp.get()
                i_, ib_ = tmp.get()
                m_, mb_ = tmp.get()
                a_, ab_ = sA, sAb
                O.act(r_[:], pa[:], AF.Sigmoid, [pab, Bprm], [rb_], bias=prm[:, P_LBA + cc:P_LBA + cc + 1])
                O.act(i_[:], pi[:], AF.Sigmoid, [pib, Bprm], [ib_], bias=prm[:, P_LBI + cc:P_LBI + cc + 1])
                O.act(a_[:], r_[:], AF.Exp, [rb_, Bcv], [ab_], scale=cvec[:, cc:cc + 1])
                O.act(m_[:], r_[:], AF.Exp, [rb_, Bcv], [mb_], scale=cvec[:, 4 + cc:5 + cc])
                O.act(m_[:], m_[:], AF.Sqrt, [mb_], [mb_], scale=-1.0, bias=1.0)
                O.tt("vector", i_[:], i_[:], xc[:], ALU.mult, [ib_, xcb_], [ib_])
                O.tt("vector", i_[:], i_[:], m_[:], ALU.mult, [ib_, mb_], [ib_])
                init = 0.0 if ti == 0 else hst[:, cc:cc + 1]
                O.scan(r_[:], a_[:], i_[:], init, [ab_, ib_, Bhst[cc]], [rb_])
                O.cp("vector", hst[:, cc:cc + 1], r_[:, TT - 1:TT], [rb_], [Bhst[cc]])
                mov = xc[:].bitcast(BF16)[:, 0:TT]
                O.tt("vector", mov, sB[:], r_[:], ALU.mult, [sBb, rb_, xcb_], [xcb_])
                O.dma("sync", mixd[j, cc, :, t0:t0 + TT], mov, [xcb_], [Bmix], par=True)

            for pr in range(2):
                chains = []
                for cc in (2 * pr, 2 * pr + 1):
                    S.begin_chain()
                    lru_chain(cc)
                    chains.append(S.end_chain())
                S.interleave(chains)
            chains = []
            for c in range(6):
                S.begin_chain()
                col = SX0 + c * 128 if c < 4 else (B0 if c == 4 else C0)
                pp, ppb = inproj(col)
                cv, cvb = conv(4 + c, pp, ppb, P_SCW + c * 4, P_SCB + c)
                sg, sgb = tmp.get()
                O.act(sg[:], cv[:], AF.Sigmoid, [cvb], [sgb])
                if c < 4:
                    O.tt("vector", xsf[c][:], cv[:], sg[:], ALU.mult, [cvb, sgb], [Bxsf[c]])
                else:
                    O.tt("vector", BCf[:, c - 4, :], cv[:], sg[:], ALU.mult, [cvb, sgb], [BBC[c - 4]])
                chains.append(S.end_chain())
            S.interleave(chains)
            pd, pdb = psp.get()
            for tb in range(4):
                ts_ = slice(tb * 128, (tb + 1) * 128)
                for k in range(16):
                    O.mm(pd[:, tb * 8:(tb + 1) * 8], h[:, k, ts_], W[:, k, DT0:DT0 + 8], k == 0, k == 15, [Bh, BW], [pdb])
            q, qb = smw.get()
            O.tt("vector", mkap(q[:, 0:1], [[8, 4], [1, 8]]), mkap(pd[:, 0:1], [[8, 4], [1, 8]]),
                 mkap(prm[:, P_DTB:P_DTB + 1], [[0, 4], [1, 8]]), ALU.add, [pdb, Bprm], [qb])
            O.act(q[:, 0:32], q[:, 0:32], AF.Exp, [qb], [qb])
            O.act(q[:, 0:32], q[:, 0:32], AF.Ln, [qb], [qb], bias=1.0)
            O.tt("vector", mkap(q[:, 32:33], [[8, 4], [1, 8]]), mkap(q[:, 0:1], [[8, 4], [1, 8]]),
                 mkap(abc[:, 0:1], [[0, 4], [1, 8]]), ALU.mult, [qb, Babc], [qb])
            pa2, pa2b = psp.get()
            O.mm(pa2[:, 0:32], tri32[:], q[:, 32:64], True, True, [Bc, qb], [pa2b])
            O.mm(pa2[:, 32:64], ones32[:], q[:, 32:64], True, True, [Bc, qb], [pa2b])
            e_, eb_ = smw.get()
            f_, fb_ = smw.get()
            g_, gb_ = smw.get()
            O.cp("scalar", e_[:], pa2[:, 0:64], [pa2b], [eb_])
            O.act(f_[:], e_[:], AF.Exp, [eb_], [fb_])
            O.tt("vector", g_[:, 0:32], e_[:, 32:64], e_[:, 0:32], ALU.subtract, [eb_], [gb_])
            O.act(g_[:, 0:32], g_[:, 0:32], AF.Exp, [gb_], [gb_])
            O.tt("vector", g_[:, 0:32], g_[:, 0:32], q[:, 0:32], ALU.mult, [gb_, qb], [gb_])

            def stage1(tb):
                ts_ = slice(tb * 128, (tb + 1) * 128)
                o8 = tb * 8
                pxs, pxsb = pgen.get()
                for c in range(4):
                    O.tr(pxs[:, c * 128:(c + 1) * 128], xsf[c][:, ts_], ident32[:], [Bxsf[c], Bc], [pxsb])
                xstm, xstmb = tmp.get()
                O.cp("scalar", xstm[:], pxs[:], [pxsb], [xstmb])
                pbt, pbtb = pgen.get()
                pbtv = pbt[:].bitcast(BF16)
                O.tr(pbtv[:, 0:128], BCf[:, 0, ts_], identb[:], [BBC[0], Bc], [pbtb])
                btm, btmb = s16.get()
                O.cp("vector", btm[:], pbtv[:, 0:128], [pbtb], [btmb])
                xcx, xcxb = tmp.get()
                xcv = xcx[:].bitcast(BF16)
                x3 = mkap(xstm[:, 0:1], [[64, 8], [1, 64]])
                O.tt("vector", mkap(xcv[:, 0:1], [[64, 8], [1, 64]]), x3, mkap(q[:, o8:o8 + 1], [[1, 8], [0, 64]]), ALU.mult,
                     [xstmb, qb], [xcxb])
                O.tt("gpsimd", mkap(xcv[:, 512:513], [[64, 8], [1, 64]]), x3, mkap(g_[:, o8:o8 + 1], [[1, 8], [0, 64]]), ALU.mult,
                     [xstmb, gb_], [xcxb])
                lh, lhb = Lh.get()
                O.tt("vector", lh[:], mkap(ustr32[:, 0:1], [[0, 8], [1, 128]]), mkap(q[:, 32 + o8:33 + o8], [[1, 8], [0, 128]]),
                     ALU.mult, [Bc, qb], [lhb])
                ps1, ps1b = pgen.get()
                ps2, ps2b = pgen.get()
                for hh in range(8):
                    pdst = (ps1 if hh < 4 else ps2)
                    pbuf = (ps1b if hh < 4 else ps2b)
                    O.mm(pdst[:, (hh % 4) * 128:(hh % 4 + 1) * 128], lh[:, hh, :], tri32[:], True, True, [lhb, Bc], [pbuf])
                E, Eb = tmp.get()
                Ev = E[:].bitcast(BF16)
                O.act(Ev[:, 0:512], ps1[:], AF.Exp, [ps1b], [Eb])
                O.act(Ev[:, 512:1024], ps2[:], AF.Exp, [ps2b], [Eb])
                pc, pcb = pgen.get()
                O.mm(pc[:, 0:128], BCf[:, 0, ts_], BCf[:, 1, ts_], True, True, [BBC[0], BBC[1]], [pcb])
                cbm, cbmb = s16.get()
                O.tt("vector", cbm[:], pc[:, 0:128], tri32[:], ALU.mult, [pcb, Bc], [cbmb])
                O.tt("vector", mkap(Ev[:, 0:1], [[128, 8], [1, 128]]), mkap(Ev[:, 0:1], [[128, 8], [1, 128]]),
                     mkap(cbm[:, 0:1], [[0, 8], [1, 128]]), ALU.mult, [Eb, cbmb], [Eb])
                py, pyb = pyP.get()
                for hh in range(8):
                    O.mm(py[:, hh * 64:(hh + 1) * 64], Ev[:, hh * 128:(hh + 1) * 128], xcv[:, hh * 64:(hh + 1) * 64],
                         True, True, [Eb, xcxb], [pyb])
                pz, pzb = pzP.get()
                for k in range(16):
                    O.mm(pz[:], h[:, k, ts_], W[:, k, Z0:Z0 + 512], k == 0, k == 15, [Bh, BW], [pzb])
                sgz, sgzb = tmp.get()
                O.act(sgz[:], pz[:], AF.Sigmoid, [pzb], [sgzb])
                O.tt("vector", sgz[:], sgz[:], pz[:], ALU.mult, [sgzb, pzb], [sgzb])
                return dict(xstm=xstm, xstmb=xstmb, btm=btm, btmb=btmb, xcv=xcv, xcxb=xcxb, py=py, pyb=pyb, sgz=sgz, sgzb=sgzb)

            def stage2(tb, st):
                ts_ = slice(tb * 128, (tb + 1) * 128)
                o8 = tb * 8
                xstm, xstmb, xcv, xcxb = st["xstm"], st["xstmb"], st["xcv"], st["xcxb"]
                py, pyb, sgz, sgzb = st["py"], st["pyb"], st["sgz"], st["sgzb"]
                pyo, pyob = pgen.get()
                O.mm(pyo[:], BCf[:, 1, ts_], prevb[:], True, True, [BBC[1], Bprevb], [pyob])
                yv, yvb = tmp.get()
                O.tt("vector", mkap(yv[:, 0:1], [[64, 8], [1, 64]]), mkap(pyo[:, 0:1], [[64, 8], [1, 64]]),
                     mkap(f_[:, o8:o8 + 1], [[1, 8], [0, 64]]), ALU.mult, [pyob, fb_], [yvb])
                O.tt("vector", yv[:], yv[:], py[:], ALU.add, [yvb, pyb], [yvb])
                t2, t2b = tmp.get()
                O.tt("gpsimd", t2[:], xstm[:], dbc[:], ALU.mult, [xstmb, Bprm], [t2b])
                O.tt("vector", yv[:], yv[:], t2[:], ALU.add, [yvb, t2b], [yvb])
                pst, pstb = pgen.get()
                O.mm(pst[:], st["btm"][:], xcv[:, 512:1024], True, True, [st["btmb"], xcxb], [pstb])
                O.tt("vector", mkap(prev[:, 0:1], [[64, 8], [1, 64]]), mkap(prev[:, 0:1], [[64, 8], [1, 64]]),
                     mkap(f_[:, 32 + o8:33 + o8], [[1, 8], [0, 64]]), ALU.mult, [Bprev, fb_], [Bprev])
                O.tt("vector", prev[:], prev[:], pst[:], ALU.add, [Bprev, pstb], [Bprev])
                O.cp("scalar", prevb[:], prev[:], [Bprev], [Bprevb])
                O.tt("vector", yv[:], yv[:], sgz[:], ALU.mult, [yvb, sgzb], [yvb])
                n_, nb_ = sm.get()
                O.memset("gpsimd", n_[:, 0:1], 0.0, [nb_])
                O.act(sgz[:], yv[:], AF.Square, [yvb], [sgzb, nb_], accum_out=n_[:, 0:1])
                O.act(n_[:, 1:2], n_[:, 0:1], AF.Sqrt, [nb_], [nb_], scale=1.0 / 512, bias=EPS)
                O.recip(n_[:, 2:3], n_[:, 1:2], [nb_], [nb_])
                ob, obb = tmp.get()
                obv = ob[:].bitcast(BF16)[:, 0:512]
                O.stt(obv, yv[:], n_[:, 2:3], ngb[:], ALU.mult, ALU.mult, [yvb, nb_, Bprm], [obb])
                pot, potb = pgen.get()
                potv = pot[:].bitcast(BF16)
                for c in range(4):
                    O.tr(potv[:, c * 128:(c + 1) * 128], obv[:, c * 128:(c + 1) * 128], identb[:], [obb, Bc], [potb])
                O.cp("scalar", mos[:, :, ts_], mkap(potv[:, 0:1], [[128, 4], [1, 128]]), [potb], [Bmos])

            S.begin_chain()
            sts = {0: stage1(0)}
            for tb in range(4):
                if tb + 1 < 4:
                    sts[tb + 1] = stage1(tb + 1)
                stage2(tb, sts.pop(tb))
            for c in range(4):
                O.dma("sync", mixd[j, 4 + c, :, t0:t0 + TT], mos[:, c, :], [Bmos], [Bmix], par=True)
            chT = S.end_chain()
            chains = [chT]
            if ti + 1 < ntile:
                S.begin_chain()
                hgen(ti + 1)
                chains.append(S.end_chain())
            S.interleave(chains)
        S.wait_all("sync", [Bmix])
        S.emit()


def phase2(nc, S, O, G):
    mixd, modsc, x1d, Gd, xq, y = G["mixd"], G["modsc"], G["x1d"], G["Gd"], G["xq"], G["y"]
    ident32, identb, iotaf, fm4, selt = G["ident32"], G["identb"], G["iotaf"], G["fm4"], G["selt"]
    Bc, Bfm4, Bmix = G["Bc"], G["Bfm4"], G["Bmix"]
    wout, wq, keysT, UT, Vt, gf_bc = G["wout"], G["wq"], G["keysT"], G["UT"], G["Vt"], G["gf_bc"]
    dbg_out = G["dbg_out"]
    stop_after = G["stop_after"]
    NTB = 8
    Bx1d = Buf("x1d")
    BGd = Buf("Gd")
    By = Buf("y")

    with ExitStack() as es:
        def sb(name, shape, dt):
            return es.enter_context(nc.sbuf_tensor(name, shape, dt))
        psp = Pool(nc, es, "p2aps", 4, [128, 512], F32, psum=True)
        msel = sb("msel", [128, 32, 1024], BF16); Bmsel = [Buf() for _ in range(32)]
        tq = Pool(nc, es, "tq", 8, [128, 1024], BF16)
        wo = Pool(nc, es, "wo", 2, [128, 32, 512], BF16)
        gate1 = sb("gate1", [128, D], F32); Bg1 = Buf()
        xqb = Pool(nc, es, "xqb", 3, [128, 512], F32)
        x1b = Pool(nc, es, "x1b", 3, [128, 512], F32)
        for i4 in range(4):
            O.dma("sync", gate1[:, i4 * 512:(i4 + 1) * 512], modsc[:, 4096 + i4 * 512:4096 + (i4 + 1) * 512], [], [Bg1], par=(i4 > 0))
        for ch in range(32):
            jj, cc = ch // 8, ch % 8
            tqs = []
            for Q in range(4):
                t_, tb_ = tq.get()
                O.dma("sync", t_[:], mixd[jj, cc, :, Q * 1024:(Q + 1) * 1024], [Bmix], [tb_])
                tqs.append((t_, tb_))
            O.ts("vector", msel[:, ch, :], tqs[0][0][:], selt[:, 0:1], None, ALU.mult, None, [tqs[0][1], Bc], [Bmsel[ch]])
            for Q in range(1, 4):
                O.stt(msel[:, ch, :], tqs[Q][0][:], selt[:, Q:Q + 1], msel[:, ch, :], ALU.mult, ALU.add,
                      [tqs[Q][1], Bc, Bmsel[ch]], [Bmsel[ch]])
        first = True
        for cb in range(4):
            cs = slice(cb * 512, (cb + 1) * 512)
            wt, wb_ = wo.get()
            for ch in range(32):
                O.dma("gpsimd", wt[:, ch, :], wout[ch * 128:(ch + 1) * 128, cs], [], [wb_], par=(ch > 0))
            for tb in range(NTB):
                ts_ = slice(tb * 128, (tb + 1) * 128)
                pt, pb = psp.get()
                for ch in range(32):
                    O.mm(pt[:], msel[:, ch, ts_], wt[:, ch, :], ch == 0, ch == 31, [Bmsel[ch], wb_], [pb])
                xt, xtb = xqb.get()
                O.dma("sync", xt[:], xq[ts_, cs], [], [xtb])
                ot, otb = x1b.get()
                O.tt("vector", ot[:], pt[:], gate1[:, cs], ALU.mult, [pb, Bg1], [otb])
                O.tt("gpsimd", ot[:], ot[:], xt[:], ALU.add, [otb, xtb], [otb])
                O.dma("sync", x1d[ts_, cs], ot[:], [otb], [Bx1d], par=(not first))
                first = False
        S.wait_all("sync", [Bx1d])
        S.emit()
    if stop_after == 2:
        with ExitStack() as es:
            df_ = es.enter_context(nc.sbuf_tensor("dbgf2", [128, D], F32)); bdf = Buf()
            bo = Buf()
            for tb in range(NTB):
                O.dma("sync", df_[:], x1d[tb * 128:(tb + 1) * 128, :], [Bx1d], [bdf])
                O.dma("sync", dbg_out[tb * 128:(tb + 1) * 128, :], df_[:], [bdf], [bo])
            S.wait_all("sync", [bo])
            S.emit()
        return

    with ExitStack() as hes:
        h2 = hes.enter_context(nc.sbuf_tensor("h2", [128, 16, 1024], BF16)); Bh2 = Buf("h2")
        ies = ExitStack()
        ITt = ies.enter_context(nc.sbuf_tensor("ITt", [128, 3, 1024], F32)); BIT = Buf("IT")
        with ExitStack() as es:
            def sb(name, shape, dt):
                return es.enter_context(nc.sbuf_tensor(name, shape, dt))
            psp = Pool(nc, es, "p2bps", 8, [128, 512], F32, psum=True)
            sm = Pool(nc, es, "p2bs", 8, [128, 16], F32)
            xin = Pool(nc, es, "x1in", 1, [128, D], F32)
            xn = Pool(nc, es, "x1n", 1, [128, D], BF16)
            wqb = sb("wqb", [128, 16, D], BF16); Bwq = Buf()
            qfm = sb("qfm", [128, 16, 512], F32); Bqf = Buf()
            kT = sb("kT", [128, 16, 128], F32); BkT = Buf()
            sub = Pool(nc, es, "sub", 1, [128, D], F32)
            wk = Pool(nc, es, "wk", 2, [128, 256], F32)
            v16 = sb("v16", [128, 16, 16], F32); Bv16 = Buf()
            idx = sb("idx", [128, 16, 16], U32); Bidx = Buf()
            idxf = sb("idxf", [128, 16, 16], F32); Bidxf = Buf()
            cand = sb("cand", [128, 8, 256], F32); Bcand = Buf()
            tv = sb("tv", [128, 8, 16], F32); Btv = Buf()
            pos = sb("pos", [128, 8, 16], U32); Bpos = Buf()
            posf = sb("posf", [128, 8, 16], F32); Bposf = Buf()
            thr = sb("thr", [128, 16], F32); Bthr = Buf()
            big = Pool(nc, es, "big", 2, [128, 2048], F32)
            d1 = sb("d1", [128, 8, 16], F32); Bd1 = Buf()
            IJW = sb("IJW", [128, 3, 128], F32); BIJW = Buf()
            asel = sb("asel", [128, 128], F32); Basel = Buf()
            bsel = sb("bsel", [128, 128], F32); Bbsel = Buf()
            ew = sb("ew", [128, 128], F32); Bew = Buf()

            for k in range(16):
                O.dma("gpsimd", wqb[:, k, :], wq[k * 128:(k + 1) * 128, :], [], [Bwq], par=(k > 0))
                O.dma("sync", kT[:, k, :], keysT[k], [], [BkT], par=(k > 0))
            O.ts("vector", thr[:], iotaf[:, 0:16], 16.0, None, ALU.mult, None, [Bc], [Bthr])
            for tb in range(NTB):
                ts_ = slice(tb * 128, (tb + 1) * 128)
                xt, xtb = xin.get()
                xnt, xnb = xn.get()
                for i4 in range(4):
                    O.dma("sync", xt[:, i4 * 512:(i4 + 1) * 512], x1d[ts_, i4 * 512:(i4 + 1) * 512], [Bx1d], [xtb], par=(i4 > 0))
                st, stb = sm.get()
                O.memset("gpsimd", st[:, 0:1], 0.0, [stb])
                O.act(xnt[:], xt[:], AF.Square, [xtb], [xnb, stb], accum_out=st[:, 0:1])
                O.act(st[:, 1:2], st[:, 0:1], AF.Sqrt, [stb], [stb], scale=1.0 / D, bias=EPS)
                O.recip(st[:, 2:3], st[:, 1:2], [stb], [stb])
                O.ts("vector", xnt[:], xt[:], st[:, 2:3], None, ALU.mult, None, [xtb, stb], [xnb])
                for half in range(2):
                    pt, pb = psp.get()
                    ptb = pt[:].bitcast(BF16)
                    for kk in range(8):
                        k = half * 8 + kk
                        O.tr(ptb[:, kk * 128:(kk + 1) * 128], xnt[:, k * 128:(k + 1) * 128], identb[:], [xnb, Bc], [pb])
                    for kk in range(8):
                        k = half * 8 + kk
                        dst = h2[:, k, ts_]
                        src = ptb[:, kk * 128:(kk + 1) * 128]
                        if kk % 2 == 0:
                            O.act(dst, src, AF.Identity, [pb, Bfm4], [Bh2], scale=fm4[:, 3, k:k + 1], bias=fm4[:, 2, k:k + 1])
                        else:
                            O.ts("vector", dst, src, fm4[:, 3, k:k + 1], fm4[:, 2, k:k + 1], ALU.mult, ALU.add,
                                 [pb, Bfm4], [Bh2])
            for half in range(2):
              for qc in range(16):
                pt, pb = psp.get()
                for k in range(16):
                    O.mm(pt[:], wqb[:, k, qc * 128:(qc + 1) * 128], h2[:, k, half * 512:(half + 1) * 512],
                         k == 0, k == 15, [Bwq, Bh2], [pb])
                O.cp("scalar" if qc % 2 == 0 else "vector", qfm[:, qc, :], pt[:], [pb], [Bqf])
              for tb in range(half * 4, half * 4 + 4):
                ts_ = slice(tb * 128, (tb + 1) * 128)
                tl_ = slice((tb % 4) * 128, (tb % 4 + 1) * 128)
                sbt, sbb = sub.get()
                for g4 in range(4):
                    pt, pb = psp.get()
                    for q4 in range(4):
                        qc = g4 * 4 + q4
                        O.mm(pt[:, q4 * 128:(q4 + 1) * 128], qfm[:, qc, tl_], kT[:, qc, :], True, True, [Bqf, BkT], [pb])
                    O.cp("scalar", sbt[:, g4 * 512:(g4 + 1) * 512], pt[:], [pb], [sbb])
                for qc in range(16):
                    sv = sbt[:, qc * 128:(qc + 1) * 128]
                    w_, wb2 = wk.get()
                    S.op("vector", lambda e, o=v16[:, qc, 0:8], i=sv: e.max(out=o, in_=i), [sbb], [Bv16])
                    S.op("vector", lambda e, o=w_[:, 0:128], r=v16[:, qc, 0:8], i=sv: e.match_replace(out=o, in_to_replace=r, in_values=i, imm_value=NEG),
                         [sbb, Bv16], [wb2])
                    S.op("vector", lambda e, o=v16[:, qc, 8:16], i=w_[:, 0:128]: e.max(out=o, in_=i), [wb2], [Bv16])
                    S.op("vector", lambda e, o=idx[:, qc, 0:8], m=v16[:, qc, 0:8], i=sv: e.max_index(out=o, in_max=m, in_values=i), [sbb, Bv16], [Bidx])
                    S.op("vector", lambda e, o=idx[:, qc, 8:16], m=v16[:, qc, 8:16], i=sv: e.max_index(out=o, in_max=m, in_values=i), [sbb, Bv16], [Bidx])
                O.cp("vector", idxf[:], idx[:], [Bidx], [Bidxf])
                O.tt("vector", mkap(cand[:, 0, 0:1], [[256, 8], [16, 16], [1, 16]]),
                     mkap(v16[:, 0, 0:1], [[32, 8], [1, 16], [0, 16]]),
                     mkap(v16[:, 1, 0:1], [[32, 8], [0, 16], [1, 16]]), ALU.add, [Bv16], [Bcand])
                for hh in range(8):
                    cvw = cand[:, hh, :]
                    w_, wb2 = wk.get()
                    S.op("vector", lambda e, o=tv[:, hh, 0:8], i=cvw: e.max(out=o, in_=i), [Bcand], [Btv])
                    S.op("vector", lambda e, o=w_[:], r=tv[:, hh, 0:8], i=cvw: e.match_replace(out=o, in_to_replace=r, in_values=i, imm_value=NEG),
                         [Bcand, Btv], [wb2])
                    S.op("vector", lambda e, o=tv[:, hh, 8:16], i=w_[:]: e.max(out=o, in_=i), [wb2], [Btv])
                    S.op("vector", lambda e, o=pos[:, hh, 0:8], m=tv[:, hh, 0:8], i=cvw: e.max_index(out=o, in_max=m, in_values=i), [Bcand, Btv], [Bpos])
                    S.op("vector", lambda e, o=pos[:, hh, 8:16], m=tv[:, hh, 8:16], i=cvw: e.max_index(out=o, in_max=m, in_values=i), [Bcand, Btv], [Bpos])
                O.cp("vector", posf[:], pos[:], [Bpos], [Bposf])
                ge, geb = big.get()
                pr, prb = big.get()
                A4 = [[16, 8], [1, 16], [0, 16]]
                O.tt("vector", mkap(ge[:, 0:1], [[256, 8], [16, 16], [1, 16]]), mkap(posf[:, 0, 0:1], A4),
                     mkap(thr[:, 0:1], [[0, 8], [0, 16], [1, 16]]), ALU.is_ge, [Bposf, Bthr], [geb])
                O.cp("vector", mkap(d1[:, 0, 0:1], [[16, 8], [1, 1]]), mkap(idxf[:, 0, 0:1], [[32, 8], [1, 1]]), [Bidxf], [Bd1])
                O.tt("vector", mkap(d1[:, 0, 1:2], [[16, 8], [1, 15]]), mkap(idxf[:, 0, 1:2], [[32, 8], [1, 15]]),
                     mkap(idxf[:, 0, 0:1], [[32, 8], [1, 15]]), ALU.subtract, [Bidxf], [Bd1])
                O.tt("vector", mkap(pr[:, 0:1], [[256, 8], [16, 16], [1, 16]]), mkap(ge[:, 0:1], [[256, 8], [16, 16], [1, 16]]),
                     mkap(d1[:, 0, 0:1], [[16, 8], [0, 16], [1, 16]]), ALU.mult, [geb, Bd1], [prb])
                O.red(IJW[:, 0, :], mkap(pr[:, 0:1], [[16, 128], [1, 16]]), [prb], [BIJW])
                O.red(asel[:], mkap(ge[:, 0:1], [[16, 128], [1, 16]]), [geb], [Basel])
                O.ts("vector", asel[:], asel[:], -16.0, 16.0, ALU.mult, ALU.add, [Basel], [Basel])
                O.tt("vector", bsel[:], asel[:], mkap(posf[:, 0, 0:1], [[1, 128]]), ALU.add, [Basel, Bposf], [Bbsel])
                eq, eqb = big.get()
                O.tt("vector", mkap(eq[:, 0:1], [[256, 8], [16, 16], [1, 16]]), mkap(bsel[:, 0:1], A4),
                     mkap(iotaf[:, 0:1], [[0, 8], [0, 16], [1, 16]]), ALU.is_equal, [Bbsel, Bc], [eqb])
                O.tt("vector", mkap(eq[:, 0:1], [[256, 8], [16, 16], [1, 16]]), mkap(eq[:, 0:1], [[256, 8], [16, 16], [1, 16]]),
                     mkap(idxf[:, 1, 0:1], [[32, 8], [0, 16], [1, 16]]), ALU.mult, [eqb, Bidxf], [eqb])
                O.red(IJW[:, 1, :], mkap(eq[:, 0:1], [[16, 128], [1, 16]]), [eqb], [BIJW])
                O.tt("vector", mkap(ew[:, 0:1], [[16, 8], [1, 16]]), mkap(tv[:, 0, 0:1], [[16, 8], [1, 16]]),
                     mkap(tv[:, 0, 0:1], [[16, 8], [0, 16]]), ALU.subtract, [Btv], [Bew])
                O.act(ew[:], ew[:], AF.Exp, [Bew], [Bew])
                z_, zb_ = sm.get()
                O.red(z_[:, 0:8], mkap(ew[:, 0:1], [[16, 8], [1, 16]]), [Bew], [zb_])
                O.recip(z_[:, 8:16], z_[:, 0:8], [zb_], [zb_])
                O.tt("vector", mkap(IJW[:, 2, 0:1], [[16, 8], [1, 16]]), mkap(ew[:, 0:1], [[16, 8], [1, 16]]),
                     mkap(z_[:, 8:9], [[1, 8], [0, 16]]), ALU.mult, [Bew, zb_], [BIJW])
                pt, pb = psp.get()
                for i3 in range(3):
                    O.tr(pt[:, i3 * 128:(i3 + 1) * 128], IJW[:, i3, :], ident32[:], [BIJW, Bc], [pb])
                O.cp("scalar", ITt[:, :, ts_], mkap(pt[:, 0:1], [[128, 3], [1, 128]]), [pb], [BIT])
            S.emit()
        if stop_after == 3:
            with ExitStack() as es:
                bo = Buf()
                for i3 in range(3):
                    O.dma("sync", dbg_out[:, i3 * 1024:(i3 + 1) * 1024], ITt[:, i3, :], [BIT], [bo])
                S.wait_all("sync", [bo])
                S.emit()
            ies.close()
            return
        with ExitStack() as es:
            psp = Pool(nc, es, "p2cps", 4, [128, 512], F32, psum=True)
            Gs = Pool(nc, es, "Gs", 2, [128, 128, 128], BF16)
            EJ = Pool(nc, es, "EJ", 6, [128, 128], BF16)
            WI = Pool(nc, es, "WI", 6, [128, 128], BF16)
            firstg = True
            for tb in range(NTB):
                gs, gsb = Gs.get()
                for t4 in range(32):
                    pt, pb = psp.get()
                    for tl in range(4):
                        t = tb * 128 + t4 * 4 + tl
                        ej, ejb = EJ.get()
                        wi, wib = WI.get()
                        O.ts("vector", ej[:], iotaf[:], ITt[:, 1, t:t + 1], None, ALU.is_equal, None, [Bc, BIT], [ejb])
                        O.ts("vector", wi[:], iotaf[:], ITt[:, 0, t:t + 1], ITt[:, 2, t:t + 1], ALU.is_equal, ALU.mult,
                             [Bc, BIT], [wib])
                        O.mm(pt[:, tl * 128:(tl + 1) * 128], ej[:], wi[:], True, True, [ejb, wib], [pb])
                    dst = mkap(gs[:, 0, t4 * 4:t4 * 4 + 1], [[128, 128], [1, 4]])
                    src = mkap(pt[:, 0:1], [[1, 128], [128, 4]])
                    if t4 % 2 == 0:
                        O.cp("scalar", dst, src, [pb], [gsb])
                    else:
                        O.cp("vector", dst, src, [pb], [gsb])
                for g16 in range(16):
                    O.dma("sync", Gd[g16 * 8:(g16 + 1) * 8, :, tb * 128:(tb + 1) * 128].rearrange("i j t -> j i t"),
                          gs[:, g16 * 8:(g16 + 1) * 8, :], [gsb], [BGd], par=(not firstg))
                    firstg = False
            S.wait_all("sync", [BGd])
            S.emit()
        ies.close()
        with ExitStack() as aes:
            acc = aes.enter_context(nc.sbuf_tensor("acc", [128, 8, D], F32)); Bacc = [Buf() for _ in range(8)]
            with ExitStack() as es:
                psp = Pool(nc, es, "p2dps", 8, [128, 512], F32, psum=True)
                UTp = Pool(nc, es, "UTp", 2, [128, 16, 512], BF16)
                Vp = Pool(nc, es, "Vp", 2, [128, 4, D], BF16)
                Gp = Pool(nc, es, "Gp", 2, [128, 4, 1024], BF16)
                ATp = Pool(nc, es, "ATp", 2, [128, 4, 1024], BF16)
                tmp = Pool(nc, es, "p2dt", 4, [128, 512], F32)
                NG = DEBUG.get("ngroups", 32)
                for g in range(NG):
                    e0 = g * 512
                    ut, utb = UTp.get()
                    vt, vtb = Vp.get()
                    gg, ggb = Gp.get()
                    at, atb = ATp.get()
                    for k in range(16):
                        O.dma("gpsimd", ut[:, k, :], UT[k * 128:(k + 1) * 128, e0:e0 + 512], [], [utb], par=(k > 0))
                    for ii in range(4):
                        O.dma("sync", gg[:, ii, :], Gd[4 * g + ii], [BGd], [ggb], par=(ii > 0))
                    for ii in range(4):
                        r0 = (4 * g + ii) * 128
                        for hf in range(2):
                            O.dma("gpsimd", vt[:, ii, hf * 1024:(hf + 1) * 1024], Vt[r0:r0 + 128, hf * 1024:(hf + 1) * 1024], [], [vtb],
                                  par=(ii > 0 or hf > 0))
                    for ii in range(4):
                        for half in range(2):
                            hs = slice(half * 512, (half + 1) * 512)
                            ps_, psb = psp.get()
                            for k in range(16):
                                O.mm(ps_[:], ut[:, k, ii * 128:(ii + 1) * 128], h2[:, k, hs], k == 0, k == 15, [utb, Bh2], [psb])
                            t0_, t0b = tmp.get()
                            t1_, t1b = tmp.get()
                            O.act(t0_[:], ps_[:], AF.Square, [psb], [t0b])
                            O.ts("vector", t0_[:], t0_[:], GC1, 1.0, ALU.mult, ALU.add, [t0b], [t0b])
                            O.tt("vector", t0_[:], t0_[:], ps_[:], ALU.mult, [t0b, psb], [t0b])
                            O.act(t1_[:], t0_[:], AF.Sigmoid, [t0b], [t1b], scale=GC2)
                            O.tt("vector", t1_[:], t1_[:], ps_[:], ALU.mult, [t1b, psb], [t1b])
                            O.tt("gpsimd", at[:, ii, hs], t1_[:], gg[:, ii, hs], ALU.mult, [t1b, ggb], [atb])
                    for tb in range(NTB):
                        ts_ = slice(tb * 128, (tb + 1) * 128)
                        for cb in range(4):
                            cs = slice(cb * 512, (cb + 1) * 512)
                            pv, pvb = psp.get()
                            for ii in range(4):
                                O.mm(pv[:], at[:, ii, ts_], vt[:, ii, cs], ii == 0, ii == 3, [atb, vtb], [pvb])
                            if g == 0:
                                O.cp("vector", acc[:, tb, cs], pv[:], [pvb], [Bacc[tb]])
                            else:
                                O.tt("vector", acc[:, tb, cs], acc[:, tb, cs], pv[:], ALU.add, [Bacc[tb], pvb], [Bacc[tb]])
                S.emit()
            with ExitStack() as es:
                def sb(name, shape, dt):
                    return es.enter_context(nc.sbuf_tensor(name, shape, dt))
                sm = Pool(nc, es, "p2es", 4, [128, 16], F32)
                xin = Pool(nc, es, "x1f", 2, [128, D], F32)
                gate2 = sb("gate2", [128, D], F32); Bg2 = Buf()
                gft = sb("gft", [128, D], F32); Bgf = Buf()
                jk = sb("jk", [128, D], BF16); Bjk = Buf()
                for i4 in range(4):
                    O.dma("sync", gate2[:, i4 * 512:(i4 + 1) * 512], modsc[:, 10240 + i4 * 512:10240 + (i4 + 1) * 512], [], [Bg2], par=(i4 > 0))
                    O.dma("sync", gft[:, i4 * 512:(i4 + 1) * 512], gf_bc[:, i4 * 512:(i4 + 1) * 512], [], [Bgf], par=(i4 > 0))
                firsty = True
                for tb in range(NTB):
                    ts_ = slice(tb * 128, (tb + 1) * 128)
                    xt, xtb = xin.get()
                    for i4 in range(4):
                        O.dma("sync", xt[:, i4 * 512:(i4 + 1) * 512], x1d[ts_, i4 * 512:(i4 + 1) * 512], [Bx1d], [xtb], par=(i4 > 0))
                    a_ = acc[:, tb, :]
                    O.tt("vector", a_, a_, gate2[:], ALU.mult, [Bacc[tb], Bg2], [Bacc[tb]])
                    O.tt("gpsimd", a_, a_, xt[:], ALU.add, [Bacc[tb], xtb], [Bacc[tb]])
                    st, stb = sm.get()
                    O.memset("gpsimd", st[:, 0:1], 0.0, [stb])
                    O.act(jk[:], a_, AF.Square, [Bacc[tb]], [Bjk, stb], accum_out=st[:, 0:1])
                    O.act(st[:, 1:2], st[:, 0:1], AF.Sqrt, [stb], [stb], scale=1.0 / D, bias=EPS)
                    O.recip(st[:, 2:3], st[:, 1:2], [stb], [stb])
                    O.stt(a_, a_, st[:, 2:3], gft[:], ALU.mult, ALU.mult, [Bacc[tb], stb, Bgf], [Bacc[tb]])
                    for i4 in range(4):
                        O.dma("sync", y[ts_, i4 * 512:(i4 + 1) * 512], acc[:, tb, i4 * 512:(i4 + 1) * 512], [Bacc[tb]], [By], par=(not firsty))
                        firsty = False
                S.wait_all("sync", [By])
                S.emit()


def _prep_inputs(inp):
    f = np.float32
    x = np.asarray(inp["x"], f)
    c = np.asarray(inp["c"], f)
    w_in = np.asarray(inp["w_in"], f)[0]
    shared = {}
    shared["w_ada"] = np.ascontiguousarray(np.asarray(inp["w_ada"], f)[0])
    shared["bada_bc"] = np.ascontiguousarray(np.broadcast_to(np.asarray(inp["b_ada"], f)[0][None, :], (128, 6 * D)))
    shared["g1_bc"] = np.ascontiguousarray(np.broadcast_to(np.asarray(inp["norm1_g"], f)[0][None, :], (128, D)))
    shared["g2_bc"] = np.ascontiguousarray(np.broadcast_to(np.asarray(inp["norm2_g"], f)[0][None, :], (128, D)))
    shared["gf_bc"] = np.ascontiguousarray(np.broadcast_to(np.asarray(inp["final_norm_g"], f)[None, :], (128, D)))
    O_G, O_X, O_Z, O_XBC, O_DT = 0, 2048, 4096, 6144, 9216
    wins, wout_rows = [], []
    P = {k: [] for k in ("lru_cw", "lru_cb", "lru_wa", "lru_wi", "lru_ba", "lru_bi", "lru_lam", "ssd_cw", "ssd_cb",
                         "ssd_dtb", "ssd_alog", "ssd_d", "ssd_ng")}
    lcw = np.asarray(inp["lru_conv_w"], f)[0]
    lcb = np.asarray(inp["lru_conv_b"], f)[0]
    lwa = np.asarray(inp["lru_w_a"], f)[0]
    lwi = np.asarray(inp["lru_w_i"], f)[0]
    lba = np.asarray(inp["lru_b_a"], f)[0]
    lbi = np.asarray(inp["lru_b_i"], f)[0]
    lam = np.asarray(inp["lru_lambda"], f)[0]
    scw = np.asarray(inp["ssd_conv_w"], f)[0]
    scb = np.asarray(inp["ssd_conv_b"], f)[0]
    dtb = np.asarray(inp["ssd_dt_bias"], f)[0]
    alog = np.asarray(inp["ssd_a_log"], f)[0]
    sd = np.asarray(inp["ssd_d"], f)[0]
    sng = np.asarray(inp["ssd_norm_g"], f)[0]
    for j in range(4):
        cols = np.concatenate([
            O_G + j * 512 + np.arange(512), O_X + j * 512 + np.arange(512), O_Z + j * 512 + np.arange(512),
            O_XBC + j * 512 + np.arange(512), O_XBC + 2048 + j * 128 + np.arange(128),
            O_XBC + 2560 + j * 128 + np.arange(128), O_DT + j * 8 + np.arange(8)])
        wins.append(w_in[:, cols])
        lch = j * 512 + np.arange(512)
        P["lru_cw"].append(lcw[:, lch].reshape(4, 4, 128).transpose(2, 1, 0).reshape(128, 16))
        P["lru_cb"].append(lcb[lch].reshape(4, 128).T)
        P["lru_wa"].append(lwa[4 * j:4 * j + 4])
        P["lru_wi"].append(lwi[4 * j:4 * j + 4])
        P["lru_ba"].append(lba[4 * j:4 * j + 4].T)
        P["lru_bi"].append(lbi[4 * j:4 * j + 4].T)
        P["lru_lam"].append(lam[lch].reshape(4, 128).T)
        sch = np.concatenate([j * 512 + np.arange(512), 2048 + j * 128 + np.arange(128), 2560 + j * 128 + np.arange(128)])
        P["ssd_cw"].append(scw[:, sch].reshape(4, 6, 128).transpose(2, 1, 0).reshape(128, 24))
        P["ssd_cb"].append(scb[sch].reshape(6, 128).T)
        P["ssd_dtb"].append(np.broadcast_to(dtb[8 * j:8 * j + 8][None, :], (128, 8)))
        P["ssd_alog"].append(np.broadcast_to(alog[8 * j:8 * j + 8][None, :], (128, 8)))
        P["ssd_d"].append(np.broadcast_to(np.repeat(sd[8 * j:8 * j + 8], 64)[None, :], (128, 512)))
        P["ssd_ng"].append(np.broadcast_to(sng[j * 512:(j + 1) * 512][None, :], (128, 512)))
        wout_rows.append(np.concatenate([j * 512 + np.arange(512), 2048 + j * 512 + np.arange(512)]))
    shared["win"] = np.ascontiguousarray(np.stack(wins))
    for k, v in P.items():
        shared[k] = np.ascontiguousarray(np.stack([np.asarray(a, f) for a in v]))
    shared["wout"] = np.ascontiguousarray(np.asarray(inp["w_out"], f)[0][np.concatenate(wout_rows)])
    shared["wq"] = np.ascontiguousarray(np.asarray(inp["peer_w_q"], f)[0])
    sk = np.asarray(inp["peer_sub_keys"], f)[0]
    shared["keysT"] = np.ascontiguousarray(sk.reshape(16, 128, 128).transpose(0, 2, 1))
    shared["UT"] = np.ascontiguousarray(np.asarray(inp["peer_u"], f)[0].T)
    shared["V"] = np.ascontiguousarray(np.asarray(inp["peer_v"], f)[0])
    maps = []
    for core in range(8):
        b, q = core // 4, core % 4
        m = dict(shared)
        m["xb"] = np.ascontiguousarray(x[b])
        m["xq"] = np.ascontiguousarray(x[b, q * 1024:(q + 1) * 1024])
        m["cfm"] = np.ascontiguousarray(c[b].reshape(16, 128).T)
        s = np.zeros((128, 4), f)
        s[:, q] = 1.0
        m["sel"] = s
        maps.append(m)
    return maps


def kernel(**inputs):
    maps = _prep_inputs(inputs)
    nc = build()
    res = run_bass_kernel_spmd(nc, maps, core_ids=list(range(8)))
    out = np.zeros((2, SEQ, D), np.float32)
    for core in range(8):
        b, q = core // 4, core % 4
        out[b, q * 1024:(q + 1) * 1024] = res.results[core]["y"]
    return out
```

```python
import numpy as np
import concourse.bass as bass
import concourse.mybir as mybir
from concourse.bass_utils import run_bass_kernel_spmd
from contextlib import ExitStack

F32 = mybir.dt.float32
BF16 = mybir.dt.bfloat16
U32 = mybir.dt.uint32
AF = mybir.ActivationFunctionType
ALU = mybir.AluOpType
AX = mybir.AxisListType

D = 2048
SEQ = 4096
NSL = 4
TT = 512
NTILE = SEQ // TT
WSL = 2312
G0, X0, Z0, SX0, B0, C0, DT0 = 0, 512, 1024, 1536, 2048, 2176, 2304
EPS = 1e-6
NEG = -1.0e30
GC1 = 0.044715
GC2 = 1.5957691216057308
DEBUG = {}


class Buf:
    __slots__ = ("name", "w", "r")

    def __init__(self, name=""):
        self.name = name
        self.w = []
        self.r = []


class Sched:
    ENG = ("tensor", "vector", "scalar", "gpsimd", "sync")

    def __init__(self, nc, es, ndma=14):
        self.nc = nc
        self.es = es
        self.q = {e: [] for e in self.ENG}
        self.sems = []
        self.cur = {}
        self.cnt = {}
        for e in self.ENG:
            self._new_eng_sem(e)
        self.seen = {e: {} for e in self.ENG}
        self.dpool = {}
        for e in ("sync", "gpsimd", "scalar"):
            self.dpool[e] = [[self._sem("d%s%d" % (e, i)), 0] for i in range(ndma)]
        self.dnext = {e: 0 for e in self.dpool}
        self.nops = {e: 0 for e in self.ENG}

    def _sem(self, name):
        s = self.es.enter_context(self.nc.semaphore(name))
        self.sems.append(s)
        return len(self.sems) - 1

    def _new_eng_sem(self, e):
        self.cur[e] = self._sem("e%s%d" % (e, len(self.sems)))
        self.cnt[e] = 0

    def _waits(self, eng, reads, writes, extra=(), par=False):
        deps = {}

        def add(tok):
            if tok is None:
                return
            k, v = tok
            if deps.get(k, 0) < v:
                deps[k] = v
        for b in reads:
            for t in b.w:
                add(t)
        for b in writes:
            if not par:
                for t in b.w:
                    add(t)
            for t in b.r:
                add(t)
        for t in extra:
            add(t)
        for k, v in deps.items():
            if eng == "tensor" and k == self.cur["tensor"]:
                continue
            if self.seen[eng].get(k, 0) >= v:
                continue
            self.seen[eng][k] = v
            sem = self.sems[k]
            self.q[eng].append(lambda e, sem=sem, v=v: e.wait_ge(sem, v))

    def _mark(self, tok, reads, writes, par=False):
        for b in reads:
            if len(b.r) > 24:
                mx = {}
                for k, v in b.r:
                    if mx.get(k, 0) < v:
                        mx[k] = v
                b.r = list(mx.items())
            b.r.append(tok)
        for b in writes:
            if par:
                if len(b.w) > 24:
                    mx = {}
                    for k, v in b.w:
                        if mx.get(k, 0) < v:
                            mx[k] = v
                    b.w = list(mx.items())
                b.w.append(tok)
            else:
                b.w = [tok]
            b.r = []

    def begin_chain(self):
        self.rec = []

    def end_chain(self):
        r = self.rec
        self.rec = None
        return r

    def interleave(self, chains):
        n = max(len(c) for c in chains) if chains else 0
        for i in range(n):
            for c in chains:
                if i < len(c):
                    kind, eng, fn, reads, writes, par = c[i]
                    if kind == "op":
                        self.op(eng, fn, reads, writes)
                    else:
                        self.dma(eng, fn, reads, writes, par)

    def op(self, eng, fn, reads=(), writes=()):
        if getattr(self, "rec", None) is not None:
            self.rec.append(("op", eng, fn, tuple(reads), tuple(writes), False))
            return None
        self._waits(eng, reads, writes)
        if self.cnt[eng] >= 30000:
            self._new_eng_sem(eng)
        self.cnt[eng] += 1
        self.nops[eng] += 1
        k = self.cur[eng]
        tok = (k, self.cnt[eng])
        sem = self.sems[k]
        self.q[eng].append(lambda e, fn=fn, sem=sem: fn(e).then_inc(sem, 1))
        self._mark(tok, reads, writes)
        return tok

    def dma(self, eng, fn, reads=(), writes=(), par=False):
        if getattr(self, "rec", None) is not None:
            self.rec.append(("dma", eng, fn, tuple(reads), tuple(writes), par))
            return None
        pool = self.dpool[eng]
        i = self.dnext[eng]
        self.dnext[eng] = (i + 1) % len(pool)
        k, val = pool[i]
        extra = [(k, val)] if val > 0 else []
        self._waits(eng, reads, writes, extra, par)
        pool[i][1] = val + 16
        tok = (k, val + 16)
        sem = self.sems[k]
        self.q[eng].append(lambda e, fn=fn, sem=sem: fn(e).then_inc(sem, 16))
        self._mark(tok, reads, writes, par)
        return tok

    def wait_all(self, eng, bufs):
        self._waits(eng, bufs, bufs)

    def emit(self, name=None):
        self.nemit = getattr(self, "nemit", 0) + 1
        with self.nc.named_scope(name or ("ph%d" % self.nemit)), self.nc.Block() as block:
            for e in self.ENG:
                lst = self.q[e]

                def body(engine, lst=lst):
                    for f in lst:
                        f(engine)
                getattr(block, e)(body)
        self.q = {e: [] for e in self.ENG}


class Ops:
    def __init__(self, S):
        self.S = S

    def tt(self, eng, out, in0, in1, op, r, w):
        self.S.op(eng, lambda e: e.tensor_tensor(out, in0, in1, op), r, w)

    def ts(self, eng, out, in0, s1, s2, op0, op1, r, w):
        if op1 is None:
            self.S.op(eng, lambda e: e.tensor_scalar(out, in0, s1, None, op0), r, w)
        else:
            self.S.op(eng, lambda e: e.tensor_scalar(out, in0, s1, s2, op0, op1), r, w)

    def stt(self, out, in0, sc, in1, op0, op1, r, w):
        self.S.op("vector", lambda e: e.scalar_tensor_tensor(out, in0, sc, in1, op0, op1), r, w)

    def act(self, out, in_, func, r, w, **kw):
        self.S.op("scalar", lambda e: e.activation(out, in_, func, **kw), r, w)

    def cp(self, eng, out, in_, r, w):
        if eng == "scalar":
            self.S.op(eng, lambda e: e.activation(out, in_, AF.Identity), r, w)
        else:
            self.S.op(eng, lambda e: e.tensor_copy(out, in_), r, w)

    def mm(self, out, lhsT, rhs, start, stop, r, w):
        self.S.op("tensor", lambda e: e.matmul(out, lhsT=lhsT, rhs=rhs, start=start, stop=stop), r, w)

    def tr(self, out, in_, ident, r, w):
        self.S.op("tensor", lambda e: e.transpose(out, in_, ident), r, w)

    def dma(self, eng, out, in_, r, w, par=False):
        self.S.dma(eng, lambda e: e.dma_start(out=out, in_=in_), r, w, par)

    def memset(self, eng, out, val, w):
        self.S.op(eng, lambda e: e.memset(out, val), (), w)

    def red(self, out, in_, r, w):
        self.S.op("vector", lambda e: e.reduce_sum(out, in_, axis=AX.X), r, w)

    def recip(self, out, in_, r, w):
        self.S.op("vector", lambda e: e.reciprocal(out, in_), r, w)

    def scan(self, out, d0, d1, init, r, w):
        self.S.op("vector", lambda e: e.tensor_tensor_scan(out, d0, d1, init, ALU.mult, ALU.add), r, w)


def mkap(base, dims):
    return bass.AP(base.tensor, base.offset, [list(base.ap[0])] + [list(d) for d in dims])


class Pool:
    def __init__(self, nc, es, name, n, shape, dt, psum=False):
        self.t = []
        self.b = []
        for i in range(n):
            if psum:
                t = es.enter_context(nc.psum_tensor("%s%d" % (name, i), shape, dt))
            else:
                t = es.enter_context(nc.sbuf_tensor("%s%d" % (name, i), shape, dt))
            self.t.append(t)
            self.b.append(Buf("%s%d" % (name, i)))
        self.i = 0

    def get(self):
        i = self.i
        self.i = (i + 1) % len(self.t)
        return self.t[i], self.b[i]


class SubPool:
    def __init__(self, pool, idxs):
        self.t = [pool.t[i] for i in idxs]
        self.b = [pool.b[i] for i in idxs]
        self.i = 0

    def get(self):
        i = self.i
        self.i = (i + 1) % len(self.t)
        return self.t[i], self.b[i]


def build():
    nc = bass.Bass("TRN2", target_bir_lowering=False)

    def din(name, shape, dt=F32):
        return nc.dram_tensor(name, list(shape), dt, kind="ExternalInput").ap()

    xb = din("xb", [SEQ, D])
    xq = din("xq", [1024, D])
    cfm = din("cfm", [128, 16])
    w_ada = din("w_ada", [D, 6 * D])
    bada_bc = din("bada_bc", [128, 6 * D])
    g1_bc = din("g1_bc", [128, D])
    g2_bc = din("g2_bc", [128, D])
    gf_bc = din("gf_bc", [128, D])
    sel = din("sel", [128, 4])
    win = din("win", [NSL, D, WSL])
    lru_cw = din("lru_cw", [NSL, 128, 16])
    lru_cb = din("lru_cb", [NSL, 128, 4])
    lru_wa = din("lru_wa", [NSL, 4, 128, 128])
    lru_wi = din("lru_wi", [NSL, 4, 128, 128])
    lru_ba = din("lru_ba", [NSL, 128, 4])
    lru_bi = din("lru_bi", [NSL, 128, 4])
    lru_lam = din("lru_lam", [NSL, 128, 4])
    ssd_cw = din("ssd_cw", [NSL, 128, 24])
    ssd_cb = din("ssd_cb", [NSL, 128, 6])
    ssd_dtb = din("ssd_dtb", [NSL, 128, 8])
    ssd_alog = din("ssd_alog", [NSL, 128, 8])
    ssd_d = din("ssd_d", [NSL, 128, 512])
    ssd_ng = din("ssd_ng", [NSL, 128, 512])
    wout = din("wout", [4096, D])
    wq = din("wq", [D, D])
    keysT = din("keysT", [16, 128, 128])
    UT = din("UT", [D, 16384])
    Vt = din("V", [16384, D])
    y = nc.dram_tensor("y", [1024, D], F32, kind="ExternalOutput").ap()
    dbg_out = None
    if DEBUG.get("shape"):
        dbg_out = nc.dram_tensor("dbg", list(DEBUG["shape"]), F32, kind="ExternalOutput").ap()

    modsc = nc.dram_tensor("modsc", [128, 6 * D], F32).ap()
    mixd = nc.dram_tensor("mixd", [4, 8, 128, SEQ], BF16).ap()
    x1d = nc.dram_tensor("x1d", [1024, D], F32).ap()
    Gd = nc.dram_tensor("Gd", [128, 128, 1024], BF16).ap()

    stop_after = DEBUG.get("stop_after", 99)

    with ExitStack() as ges:
        S = Sched(nc, ges)
        O = Ops(S)

        def gsb(name, shape, dt):
            return ges.enter_context(nc.sbuf_tensor(name, shape, dt))

        ident32 = gsb("ident32", [128, 128], F32); Bc = Buf("consts")
        identb = gsb("identb", [128, 128], BF16)
        ones32 = gsb("ones32", [128, 128], F32)
        tri32 = gsb("tri32", [128, 128], F32)
        ustr32 = gsb("ustr32", [128, 128], F32)
        iotaf = gsb("iotaf", [128, 128], F32)
        fm4 = gsb("fm4", [128, 4, 16], F32); Bfm4 = Buf("fm4")
        selt = gsb("selt", [128, 4], F32)
        dfl = gsb("dfl", [128, 128], F32); Bdfl = Buf("dfl")

        O.S.op("gpsimd", lambda e: e.iota(dfl[:], pattern=[[1, 128]], base=0, channel_multiplier=-1,
                                          allow_small_or_imprecise_dtypes=True), (), [Bdfl])
        O.S.op("gpsimd", lambda e: e.iota(iotaf[:], pattern=[[1, 128]], base=0, channel_multiplier=0,
                                          allow_small_or_imprecise_dtypes=True), (), [Bc])
        O.ts("vector", ident32[:], dfl[:], 0.0, None, ALU.is_equal, None, [Bdfl], [Bc])
        O.ts("vector", tri32[:], dfl[:], 0.0, None, ALU.is_ge, None, [Bdfl], [Bc])
        O.ts("vector", ustr32[:], dfl[:], 0.0, None, ALU.is_lt, None, [Bdfl], [Bc])
        O.cp("vector", identb[:], ident32[:], [Bc], [Bc])
        O.memset("vector", ones32[:], 1.0, [Bc])
        O.dma("sync", selt[:], sel, [], [Bc])

        with ExitStack() as es:
            def sb(name, shape, dt):
                return es.enter_context(nc.sbuf_tensor(name, shape, dt))
            psp = Pool(nc, es, "p0ps", 4, [128, 512], F32, psum=True)
            wa = Pool(nc, es, "wa", 2, [128, 16, 512], F32)
            bad = Pool(nc, es, "bad", 2, [128, 512], F32)
            modbc = sb("modbc", [128, 6 * D], F32); Bmod = Buf("modbc")
            g12 = sb("g12", [128, D], F32); Bg12 = Buf("g12")
            crep = sb("crep", [128, 16, 128], F32); Bcrep = Buf("crep")
            cin = sb("cin", [128, 16], F32); Bcin = Buf("cin")
            cact = sb("cact", [128, 16], F32); Bcact = Buf("cact")
            tmpd = Pool(nc, es, "tmpd", 2, [128, 128], F32)

            O.dma("sync", cin[:], cfm, [], [Bcin])
            O.act(cact[:], cin[:], AF.Silu, [Bcin], [Bcact])
            for k in range(16):
                O.ts("vector", crep[:, k, :], ones32[:], cact[:, k:k + 1], None, ALU.mult, None, [Bc, Bcact], [Bcrep])
            wav = w_ada.rearrange("(k p) e -> p k e", p=128)
            for cb in range(24):
                wt, wb_ = wa.get()
                bt_, bb_ = bad.get()
                cs = slice(cb * 512, (cb + 1) * 512)
                for k in range(16):
                    O.dma("sync", wt[:, k, :], w_ada[k * 128:(k + 1) * 128, cs], [], [wb_], par=(k > 0))
                O.dma("sync", bt_[:], bada_bc[:, cs], [], [bb_])
                pt, pb = psp.get()
                for k in range(16):
                    O.mm(pt[:], crep[:, k, :], wt[:, k, :], k == 0, k == 15, [Bcrep, wb_], [pb])
                O.tt("vector", modbc[:, cs], pt[:], bt_[:], ALU.add, [pb, bb_], [Bmod])
            O.dma("sync", g12[:], g1_bc, [], [Bg12])
            O.stt(modbc[:, 2048:4096], modbc[:, 2048:4096], 1.0, g12[:], ALU.add, ALU.mult, [Bmod, Bg12], [Bmod])
            O.dma("sync", g12[:], g2_bc, [Bg12], [Bg12])
            O.stt(modbc[:, 8192:10240], modbc[:, 8192:10240], 1.0, g12[:], ALU.add, ALU.mult, [Bmod, Bg12], [Bmod])
            for v, sec in enumerate((0, 1, 3, 4)):
                for k in range(16):
                    td, tb_ = tmpd.get()
                    c0 = sec * 2048 + k * 128
                    O.tt("vector", td[:], modbc[:, c0:c0 + 128], ident32[:], ALU.mult, [Bmod, Bc], [tb_])
                    O.red(fm4[:, v, k:k + 1], td[:], [tb_], [Bfm4])
            Bmodsc = Buf("modsc")
            for i6 in range(6):
                O.dma("sync", modsc[:, i6 * 2048:(i6 + 1) * 2048], modbc[:, i6 * 2048:(i6 + 1) * 2048], [Bmod], [Bmodsc])
            if stop_after == 0:
                for i6 in range(6):
                    O.dma("sync", dbg_out[:, i6 * 2048:(i6 + 1) * 2048], modbc[:, i6 * 2048:(i6 + 1) * 2048], [Bmod], [Bmodsc])
            S.wait_all("sync", [Bmodsc])
            S.emit()

        Bmix = Buf("mixd")
        if stop_after >= 1:
            for j in range(DEBUG.get("nsl", NSL)):
                phase1_slice(nc, S, O, j, locals())
        if stop_after == 1:
            with ExitStack() as es:
                dt_ = es.enter_context(nc.sbuf_tensor("dbgt", [128, 4096], BF16)); bdt = Buf()
                df_ = es.enter_context(nc.sbuf_tensor("dbgf", [128, 4096], F32)); bdf = Buf()
                for cc in range(8):
                    O.dma("sync", dt_[:], mixd[0, cc], [Bmix], [bdt])
                    O.cp("vector", df_[:], dt_[:], [bdt], [bdf])
                    O.dma("sync", dbg_out[cc * 128:(cc + 1) * 128, :], df_[:], [bdf], [Bmix])
                S.wait_all("sync", [Bmix])
                S.emit()
        if stop_after >= 2:
            phase2(nc, S, O, locals())
    return nc


def phase1_slice(nc, S, O, j, G):
    xb, win, mixd = G["xb"], G["win"], G["mixd"]
    ident32, identb, ones32, tri32, ustr32 = G["ident32"], G["identb"], G["ones32"], G["tri32"], G["ustr32"]
    fm4, Bc, Bfm4, Bmix = G["fm4"], G["Bc"], G["Bfm4"], G["Bmix"]
    ntile = DEBUG.get("ntile", NTILE)
    with ExitStack() as es:
        def sb(name, shape, dt):
            return es.enter_context(nc.sbuf_tensor("%s_%d" % (name, j), shape, dt))
        psp = Pool(nc, es, "p1ps%d" % j, 8, [128, 512], F32, psum=True)
        tmp = Pool(nc, es, "p1t%d" % j, 16, [128, 512], F32)
        sm = Pool(nc, es, "p1s%d" % j, 24, [128, 16], F32)
        smw = Pool(nc, es, "p1w%d" % j, 8, [128, 64], F32)
        s16 = Pool(nc, es, "p1h%d" % j, 6, [128, 128], BF16)
        pgen, pyP, pzP, phg = SubPool(psp, [0, 1, 2]), SubPool(psp, [3, 4]), SubPool(psp, [5]), SubPool(psp, [6, 7])
        W = sb("W", [128, 16, WSL], BF16); BW = Buf("W")
        xin = Pool(nc, es, "xin%d" % j, 1, [128, D], F32)
        xn = Pool(nc, es, "xn%d" % j, 1, [128, D], BF16)
        hbuf = [sb("h0", [128, 16, TT], BF16), sb("h1", [128, 16, TT], BF16)]
        Bhs = [Buf("h0"), Buf("h1")]
        h, Bh = hbuf[0], Bhs[0]
        xh = [sb("xh%d" % c, [128, TT + 3], F32) for c in range(10)]
        Bxh = [Buf("xh%d" % c) for c in range(10)]
        hst = sb("hst", [128, 4], F32); Bhst = [Buf() for _ in range(4)]
        prm = sb("prm", [128, 16 + 4 + 4 + 4 + 4 + 24 + 6 + 8 + 8], F32); Bprm = Buf("prm")
        cvec = sb("cvec", [128, 8], F32); Bcv = Buf("cvec")
        abc = sb("abc", [128, 8], F32); Babc = Buf("abc")
        dbc = sb("dbc", [128, 512], F32)
        ngb = sb("ngb", [128, 512], F32)
        wab = sb("wab", [128, 8, 128], BF16); Bwab = Buf("wab")
        xsf = [sb("xsf%d" % c, [128, TT], F32) for c in range(4)]
        Bxsf = [Buf() for _ in range(4)]
        BCf = sb("BCf", [128, 2, TT], BF16); BBC = [Buf(), Buf()]
        Lh = Pool(nc, es, "Lh%d" % j, 1, [128, 8, 128], F32)
        prev = sb("prev", [128, 512], F32); Bprev = Buf("prev")
        prevb = sb("prevb", [128, 512], BF16); Bprevb = Buf("prevb")
        mos = sb("mos", [128, 4, TT], BF16); Bmos = Buf("mos")
        P_LCW, P_LCB, P_LBA, P_LBI, P_LAM, P_SCW, P_SCB, P_DTB, P_ALOG = 0, 16, 20, 24, 28, 32, 56, 62, 70

        for k in range(16):
            O.dma("gpsimd", W[:, k, :], win[j, k * 128:(k + 1) * 128, :], [], [BW], par=(k > 0))
        for h_ in range(4):
            O.dma("gpsimd", wab[:, h_, :], G["lru_wa"][j, h_], [], [Bwab])
            O.dma("gpsimd", wab[:, 4 + h_, :], G["lru_wi"][j, h_], [], [Bwab])
        for (off, n, src) in ((P_LCW, 16, "lru_cw"), (P_LCB, 4, "lru_cb"), (P_LBA, 4, "lru_ba"), (P_LBI, 4, "lru_bi"),
                              (P_LAM, 4, "lru_lam"), (P_SCW, 24, "ssd_cw"), (P_SCB, 6, "ssd_cb"), (P_DTB, 8, "ssd_dtb"),
                              (P_ALOG, 8, "ssd_alog")):
            O.dma("sync", prm[:, off:off + n], G[src][j], [], [Bprm])
        O.dma("sync", dbc[:], G["ssd_d"][j], [], [Bprm])
        O.dma("sync", ngb[:], G["ssd_ng"][j], [], [Bprm])
        O.act(cvec[:, 0:4], prm[:, P_LAM:P_LAM + 4], AF.Exp, [Bprm], [Bcv], scale=-1.0)
        O.act(cvec[:, 0:4], cvec[:, 0:4], AF.Ln, [Bcv], [Bcv], bias=1.0)
        O.ts("vector", cvec[:, 4:8], cvec[:, 0:4], -16.0, None, ALU.mult, None, [Bcv], [Bcv])
        O.ts("vector", cvec[:, 0:4], cvec[:, 0:4], -8.0, None, ALU.mult, None, [Bcv], [Bcv])
        O.act(abc[:], prm[:, P_ALOG:P_ALOG + 8], AF.Exp, [Bprm], [Babc])
        O.ts("vector", abc[:], abc[:], -1.0, None, ALU.mult, None, [Babc], [Babc])
        for c in range(10):
            O.memset("gpsimd", xh[c][:, 0:3], 0.0, [Bxh[c]])
        O.memset("gpsimd", prev[:], 0.0, [Bprev])
        O.memset("gpsimd", prevb[:], 0.0, [Bprevb])

        def inproj(col, n=128):
            pt, pb = psp.get()
            for k in range(16):
                O.mm(pt[0:n, :], W[:, k, col:col + n], h[:, k, :], k == 0, k == 15, [BW, Bh], [pb])
            return pt, pb

        def conv(c, pt, pb, wcol, bcol):
            O.cp("scalar", xh[c][:, 3:TT + 3], pt[:], [pb], [Bxh[c]])
            r, rb = tmp.get()
            O.ts("vector", r[:], xh[c][:, 0:TT], prm[:, wcol:wcol + 1], prm[:, bcol:bcol + 1], ALU.mult, ALU.add,
                 [Bxh[c], Bprm], [rb])
            for k in range(1, 4):
                O.stt(r[:], xh[c][:, k:k + TT], prm[:, wcol + k:wcol + k + 1], r[:], ALU.mult, ALU.add,
                      [Bxh[c], Bprm, rb], [rb])
            O.cp("gpsimd", xh[c][:, 0:3], xh[c][:, TT:TT + 3], [Bxh[c]], [Bxh[c]])
            return r, rb

        def hgen(tix):
            hh_, Bhh_ = hbuf[tix % 2], Bhs[tix % 2]
            tt0 = tix * TT
            for tb in range(4):
                xt, xtb = xin.get()
                xnt, xnb = xn.get()
                O.dma("sync", xt[:], xb[tt0 + tb * 128:tt0 + (tb + 1) * 128, :], [], [xtb])
                st, stb = sm.get()
                O.memset("gpsimd", st[:, 0:1], 0.0, [stb])
                O.act(xnt[:], xt[:], AF.Square, [xtb], [xnb, stb], accum_out=st[:, 0:1])
                O.act(st[:, 1:2], st[:, 0:1], AF.Ln, [stb], [stb], scale=1.0 / D, bias=EPS)
                O.act(st[:, 2:3], st[:, 1:2], AF.Exp, [stb], [stb], scale=-0.5)
                O.ts("vector", xnt[:], xt[:], st[:, 2:3], None, ALU.mult, None, [xtb, stb], [xnb])
                for half in range(2):
                    pt, pb = phg.get()
                    ptb = pt[:].bitcast(BF16)
                    for kk in range(8):
                        k = half * 8 + kk
                        O.tr(ptb[:, kk * 128:(kk + 1) * 128], xnt[:, k * 128:(k + 1) * 128], identb[:], [xnb, Bc], [pb])
                    for kk in range(8):
                        k = half * 8 + kk
                        dst = hh_[:, k, tb * 128:(tb + 1) * 128]
                        src = ptb[:, kk * 128:(kk + 1) * 128]
                        if kk % 2 == 0:
                            O.act(dst, src, AF.Identity, [pb, Bfm4], [Bhh_], scale=fm4[:, 1, k:k + 1], bias=fm4[:, 0, k:k + 1])
                        else:
                            O.ts("vector", dst, src, fm4[:, 1, k:k + 1], fm4[:, 0, k:k + 1], ALU.mult, ALU.add,
                                 [pb, Bfm4], [Bhh_])

        hgen(0)
        for ti in range(ntile):
            t0 = ti * TT
            h, Bh = hbuf[ti % 2], Bhs[ti % 2]
            def lru_chain(cc):
                pg, pgb = inproj(G0 + cc * 128)
                px, pxb = inproj(X0 + cc * 128)
                sA, sAb = tmp.get()
                sB, sBb = tmp.get()
                O.act(sB[:], pg[:], AF.Gelu_apprx_tanh, [pgb], [sBb])
                xc, xcb_ = conv(cc, px, pxb, P_LCW + cc * 4, P_LCB + cc)
                xb16v = sA[:].bitcast(BF16)[:, 0:TT]
                O.cp("scalar", xb16v, xc[:], [xcb_], [sAb])
                pa, pab = psp.get()
                pi, pib = psp.get()
                O.mm(pa[:], wab[:, cc, :], xb16v, True, True, [Bwab, sAb], [pab])
                O.mm(pi[:], wab[:, 4 + cc, :], xb16v, True, True, [Bwab, sAb], [pib])
                r_, rb_ = tmp.get()
                i_, ib_ = tmp.get()
                m_, mb_ = tmp.get()
                a_, ab_ = sA, sAb
                O.act(r_[:], pa[:], AF.Sigmoid, [pab, Bprm], [rb_], bias=prm[:, P_LBA + cc:P_LBA + cc + 1])
                O.act(i_[:], pi[:], AF.Sigmoid, [pib, Bprm], [ib_], bias=prm[:, P_LBI + cc:P_LBI + cc + 1])
                O.act(a_[:], r_[:], AF.Exp, [rb_, Bcv], [ab_], scale=cvec[:, cc:cc + 1])
                O.act(m_[:], r_[:], AF.Exp, [rb_, Bcv], [mb_], scale=cvec[:, 4 + cc:5 + cc])
                O.act(m_[:], m_[:], AF.Ln, [mb_], [mb_], scale=-1.0, bias=1.0)
                O.act(m_[:], m_[:], AF.Exp, [mb_], [mb_], scale=0.5)
                O.tt("vector", i_[:], i_[:], xc[:], ALU.mult, [ib_, xcb_], [ib_])
                O.tt("vector", i_[:], i_[:], m_[:], ALU.mult, [ib_, mb_], [ib_])
                init = 0.0 if ti == 0 else hst[:, cc:cc + 1]
                O.scan(r_[:], a_[:], i_[:], init, [ab_, ib_, Bhst[cc]], [rb_])
                O.cp("vector", hst[:, cc:cc + 1], r_[:, TT - 1:TT], [rb_], [Bhst[cc]])
                mov = xc[:].bitcast(BF16)[:, 0:TT]
                O.tt("vector", mov, sB[:], r_[:], ALU.mult, [sBb, rb_, xcb_], [xcb_])
                O.dma("sync", mixd[j, cc, :, t0:t0 + TT], mov, [xcb_], [Bmix], par=True)

            for pr in range(2):
                chains = []
                for cc in (2 * pr, 2 * pr + 1):
                    S.begin_chain()
                    lru_chain(cc)
                    chains.append(S.end_chain())
                S.interleave(chains)
            chains = []
            for c in range(6):
                S.begin_chain()
                col = SX0 + c * 128 if c < 4 else (B0 if c == 4 else C0)
                pp, ppb = inproj(col)
                cv, cvb = conv(4 + c, pp, ppb, P_SCW + c * 4, P_SCB + c)
                if c < 4:
                    O.act(xsf[c][:], cv[:], AF.Silu, [cvb], [Bxsf[c]])
                else:
                    O.act(BCf[:, c - 4, :], cv[:], AF.Silu, [cvb], [BBC[c - 4]])
                chains.append(S.end_chain())
            S.interleave(chains)
            pd, pdb = psp.get()
            for tb in range(4):
                ts_ = slice(tb * 128, (tb + 1) * 128)
                for k in range(16):
                    O.mm(pd[:, tb * 8:(tb + 1) * 8], h[:, k, ts_], W[:, k, DT0:DT0 + 8], k == 0, k == 15, [Bh, BW], [pdb])
            q, qb = smw.get()
            O.tt("vector", mkap(q[:, 0:1], [[8, 4], [1, 8]]), mkap(pd[:, 0:1], [[8, 4], [1, 8]]),
                 mkap(prm[:, P_DTB:P_DTB + 1], [[0, 4], [1, 8]]), ALU.add, [pdb, Bprm], [qb])
            O.act(q[:, 0:32], q[:, 0:32], AF.Exp, [qb], [qb])
            O.act(q[:, 0:32], q[:, 0:32], AF.Ln, [qb], [qb], bias=1.0)
            O.tt("vector", mkap(q[:, 32:33], [[8, 4], [1, 8]]), mkap(q[:, 0:1], [[8, 4], [1, 8]]),
                 mkap(abc[:, 0:1], [[0, 4], [1, 8]]), ALU.mult, [qb, Babc], [qb])
            pa2, pa2b = psp.get()
            O.mm(pa2[:, 0:32], tri32[:], q[:, 32:64], True, True, [Bc, qb], [pa2b])
            O.mm(pa2[:, 32:64], ones32[:], q[:, 32:64], True, True, [Bc, qb], [pa2b])
            e_, eb_ = smw.get()
            f_, fb_ = smw.get()
            g_, gb_ = smw.get()
            O.cp("scalar", e_[:], pa2[:, 0:64], [pa2b], [eb_])
            O.act(f_[:], e_[:], AF.Exp, [eb_], [fb_])
            O.tt("vector", g_[:, 0:32], e_[:, 32:64], e_[:, 0:32], ALU.subtract, [eb_], [gb_])
            O.act(g_[:, 0:32], g_[:, 0:32], AF.Exp, [gb_], [gb_])
            O.tt("vector", g_[:, 0:32], g_[:, 0:32], q[:, 0:32], ALU.mult, [gb_, qb], [gb_])

            def stage1(tb):
                ts_ = slice(tb * 128, (tb + 1) * 128)
                o8 = tb * 8
                pxs, pxsb = pgen.get()
                for c in range(4):
                    O.tr(pxs[:, c * 128:(c + 1) * 128], xsf[c][:, ts_], ident32[:], [Bxsf[c], Bc], [pxsb])
                xstm, xstmb = tmp.get()
                O.cp("scalar", xstm[:], pxs[:], [pxsb], [xstmb])
                pbt, pbtb = pgen.get()
                pbtv = pbt[:].bitcast(BF16)
                O.tr(pbtv[:, 0:128], BCf[:, 0, ts_], identb[:], [BBC[0], Bc], [pbtb])
                btm, btmb = s16.get()
                O.cp("vector", btm[:], pbtv[:, 0:128], [pbtb], [btmb])
                xcx, xcxb = tmp.get()
                xcv = xcx[:].bitcast(BF16)
                x3 = mkap(xstm[:, 0:1], [[64, 8], [1, 64]])
                O.tt("vector", mkap(xcv[:, 0:1], [[64, 8], [1, 64]]), x3, mkap(q[:, o8:o8 + 1], [[1, 8], [0, 64]]), ALU.mult,
                     [xstmb, qb], [xcxb])
                O.tt("gpsimd", mkap(xcv[:, 512:513], [[64, 8], [1, 64]]), x3, mkap(g_[:, o8:o8 + 1], [[1, 8], [0, 64]]), ALU.mult,
                     [xstmb, gb_], [xcxb])
                lh, lhb = Lh.get()
                O.tt("vector", lh[:], mkap(ustr32[:, 0:1], [[0, 8], [1, 128]]), mkap(q[:, 32 + o8:33 + o8], [[1, 8], [0, 128]]),
                     ALU.mult, [Bc, qb], [lhb])
                ps1, ps1b = pgen.get()
                ps2, ps2b = pgen.get()
                for hh in range(8):
                    pdst = (ps1 if hh < 4 else ps2)
                    pbuf = (ps1b if hh < 4 else ps2b)
                    O.mm(pdst[:, (hh % 4) * 128:(hh % 4 + 1) * 128], lh[:, hh, :], tri32[:], True, True, [lhb, Bc], [pbuf])
                E, Eb = tmp.get()
                Ev = E[:].bitcast(BF16)
                O.act(Ev[:, 0:512], ps1[:], AF.Exp, [ps1b], [Eb])
                O.act(Ev[:, 512:1024], ps2[:], AF.Exp, [ps2b], [Eb])
                pc, pcb = pgen.get()
                O.mm(pc[:, 0:128], BCf[:, 0, ts_], BCf[:, 1, ts_], True, True, [BBC[0], BBC[1]], [pcb])
                cbm, cbmb = s16.get()
                O.tt("vector", cbm[:], pc[:, 0:128], tri32[:], ALU.mult, [pcb, Bc], [cbmb])
                O.tt("vector", mkap(Ev[:, 0:1], [[128, 8], [1, 128]]), mkap(Ev[:, 0:1], [[128, 8], [1, 128]]),
                     mkap(cbm[:, 0:1], [[0, 8], [1, 128]]), ALU.mult, [Eb, cbmb], [Eb])
                py, pyb = pyP.get()
                for hh in range(8):
                    O.mm(py[:, hh * 64:(hh + 1) * 64], Ev[:, hh * 128:(hh + 1) * 128], xcv[:, hh * 64:(hh + 1) * 64],
                         True, True, [Eb, xcxb], [pyb])
                pz, pzb = pzP.get()
                for k in range(16):
                    O.mm(pz[:], h[:, k, ts_], W[:, k, Z0:Z0 + 512], k == 0, k == 15, [Bh, BW], [pzb])
                sgz, sgzb = tmp.get()
                O.act(sgz[:], pz[:], AF.Silu, [pzb], [sgzb])
                return dict(xstm=xstm, xstmb=xstmb, btm=btm, btmb=btmb, xcv=xcv, xcxb=xcxb, py=py, pyb=pyb, sgz=sgz, sgzb=sgzb)

            def stage2(tb, st):
                ts_ = slice(tb * 128, (tb + 1) * 128)
                o8 = tb * 8
                xstm, xstmb, xcv, xcxb = st["xstm"], st["xstmb"], st["xcv"], st["xcxb"]
                py, pyb, sgz, sgzb = st["py"], st["pyb"], st["sgz"], st["sgzb"]
                pyo, pyob = pgen.get()
                O.mm(pyo[:], BCf[:, 1, ts_], prevb[:], True, True, [BBC[1], Bprevb], [pyob])
                yv, yvb = tmp.get()
                O.tt("vector", mkap(yv[:, 0:1], [[64, 8], [1, 64]]), mkap(pyo[:, 0:1], [[64, 8], [1, 64]]),
                     mkap(f_[:, o8:o8 + 1], [[1, 8], [0, 64]]), ALU.mult, [pyob, fb_], [yvb])
                O.tt("vector", yv[:], yv[:], py[:], ALU.add, [yvb, pyb], [yvb])
                t2, t2b = tmp.get()
                O.tt("gpsimd", t2[:], xstm[:], dbc[:], ALU.mult, [xstmb, Bprm], [t2b])
                O.tt("vector", yv[:], yv[:], t2[:], ALU.add, [yvb, t2b], [yvb])
                pst, pstb = pgen.get()
                O.mm(pst[:], st["btm"][:], xcv[:, 512:1024], True, True, [st["btmb"], xcxb], [pstb])
                O.tt("vector", mkap(prev[:, 0:1], [[64, 8], [1, 64]]), mkap(prev[:, 0:1], [[64, 8], [1, 64]]),
                     mkap(f_[:, 32 + o8:33 + o8], [[1, 8], [0, 64]]), ALU.mult, [Bprev, fb_], [Bprev])
                O.tt("vector", prev[:], prev[:], pst[:], ALU.add, [Bprev, pstb], [Bprev])
                O.cp("scalar", prevb[:], prev[:], [Bprev], [Bprevb])
                O.tt("vector", yv[:], yv[:], sgz[:], ALU.mult, [yvb, sgzb], [yvb])
                n_, nb_ = sm.get()
                O.memset("gpsimd", n_[:, 0:1], 0.0, [nb_])
                O.act(sgz[:], yv[:], AF.Square, [yvb], [sgzb, nb_], accum_out=n_[:, 0:1])
                O.act(n_[:, 1:2], n_[:, 0:1], AF.Ln, [nb_], [nb_], scale=1.0 / 512, bias=EPS)
                O.act(n_[:, 2:3], n_[:, 1:2], AF.Exp, [nb_], [nb_], scale=-0.5)
                ob, obb = tmp.get()
                obv = ob[:].bitcast(BF16)[:, 0:512]
                O.stt(obv, yv[:], n_[:, 2:3], ngb[:], ALU.mult, ALU.mult, [yvb, nb_, Bprm], [obb])
                pot, potb = pgen.get()
                potv = pot[:].bitcast(BF16)
                for c in range(4):
                    O.tr(potv[:, c * 128:(c + 1) * 128], obv[:, c * 128:(c + 1) * 128], identb[:], [obb, Bc], [potb])
                O.cp("scalar", mos[:, :, ts_], mkap(potv[:, 0:1], [[128, 4], [1, 128]]), [potb], [Bmos])

            S.begin_chain()
            sts = {0: stage1(0)}
            for tb in range(4):
                if tb + 1 < 4:
                    sts[tb + 1] = stage1(tb + 1)
                stage2(tb, sts.pop(tb))
            for c in range(4):
                O.dma("sync", mixd[j, 4 + c, :, t0:t0 + TT], mos[:, c, :], [Bmos], [Bmix], par=True)
            chT = S.end_chain()
            chains = [chT]
            if ti + 1 < ntile:
                S.begin_chain()
                hgen(ti + 1)
                chains.append(S.end_chain())
            S.interleave(chains)
        S.wait_all("sync", [Bmix])
        S.emit()


def phase2(nc, S, O, G):
    mixd, modsc, x1d, Gd, xq, y = G["mixd"], G["modsc"], G["x1d"], G["Gd"], G["xq"], G["y"]
    ident32, identb, iotaf, fm4, selt = G["ident32"], G["identb"], G["iotaf"], G["fm4"], G["selt"]
    Bc, Bfm4, Bmix = G["Bc"], G["Bfm4"], G["Bmix"]
    wout, wq, keysT, UT, Vt, gf_bc = G["wout"], G["wq"], G["keysT"], G["UT"], G["Vt"], G["gf_bc"]
    dbg_out = G["dbg_out"]
    stop_after = G["stop_after"]
    NTB = 8
    Bx1d = Buf("x1d")
    BGd = Buf("Gd")
    By = Buf("y")

    with ExitStack() as es:
        def sb(name, shape, dt):
            return es.enter_context(nc.sbuf_tensor(name, shape, dt))
        psp = Pool(nc, es, "p2aps", 4, [128, 512], F32, psum=True)
        msel = sb("msel", [128, 32, 1024], BF16); Bmsel = [Buf() for _ in range(32)]
        tq = Pool(nc, es, "tq", 8, [128, 1024], BF16)
        wo = Pool(nc, es, "wo", 2, [128, 32, 512], BF16)
        gate1 = sb("gate1", [128, D], F32); Bg1 = Buf()
        xqb = Pool(nc, es, "xqb", 3, [128, 512], F32)
        x1b = Pool(nc, es, "x1b", 3, [128, 512], F32)
        for i4 in range(4):
            O.dma("sync", gate1[:, i4 * 512:(i4 + 1) * 512], modsc[:, 4096 + i4 * 512:4096 + (i4 + 1) * 512], [], [Bg1], par=(i4 > 0))
        for ch in range(32):
            jj, cc = ch // 8, ch % 8
            tqs = []
            for Q in range(4):
                t_, tb_ = tq.get()
                O.dma("sync", t_[:], mixd[jj, cc, :, Q * 1024:(Q + 1) * 1024], [Bmix], [tb_])
                tqs.append((t_, tb_))
            O.ts("vector", msel[:, ch, :], tqs[0][0][:], selt[:, 0:1], None, ALU.mult, None, [tqs[0][1], Bc], [Bmsel[ch]])
            for Q in range(1, 4):
                O.stt(msel[:, ch, :], tqs[Q][0][:], selt[:, Q:Q + 1], msel[:, ch, :], ALU.mult, ALU.add,
                      [tqs[Q][1], Bc, Bmsel[ch]], [Bmsel[ch]])
        first = True
        for cb in range(4):
            cs = slice(cb * 512, (cb + 1) * 512)
            wt, wb_ = wo.get()
            for ch in range(32):
                O.dma("gpsimd", wt[:, ch, :], wout[ch * 128:(ch + 1) * 128, cs], [], [wb_], par=(ch > 0))
            for tb in range(NTB):
                ts_ = slice(tb * 128, (tb + 1) * 128)
                pt, pb = psp.get()
                for ch in range(32):
                    O.mm(pt[:], msel[:, ch, ts_], wt[:, ch, :], ch == 0, ch == 31, [Bmsel[ch], wb_], [pb])
                xt, xtb = xqb.get()
                O.dma("sync", xt[:], xq[ts_, cs], [], [xtb])
                ot, otb = x1b.get()
                O.tt("vector", ot[:], pt[:], gate1[:, cs], ALU.mult, [pb, Bg1], [otb])
                O.tt("gpsimd", ot[:], ot[:], xt[:], ALU.add, [otb, xtb], [otb])
                O.dma("sync", x1d[ts_, cs], ot[:], [otb], [Bx1d], par=(not first))
                first = False
        S.wait_all("sync", [Bx1d])
        S.emit()
    if stop_after == 2:
        with ExitStack() as es:
            df_ = es.enter_context(nc.sbuf_tensor("dbgf2", [128, D], F32)); bdf = Buf()
            bo = Buf()
            for tb in range(NTB):
                O.dma("sync", df_[:], x1d[tb * 128:(tb + 1) * 128, :], [Bx1d], [bdf])
                O.dma("sync", dbg_out[tb * 128:(tb + 1) * 128, :], df_[:], [bdf], [bo])
            S.wait_all("sync", [bo])
            S.emit()
        return

    with ExitStack() as hes:
        h2 = hes.enter_context(nc.sbuf_tensor("h2", [128, 16, 1024], BF16)); Bh2 = Buf("h2")
        ies = ExitStack()
        ITt = ies.enter_context(nc.sbuf_tensor("ITt", [128, 3, 1024], F32)); BIT = Buf("IT")
        with ExitStack() as es:
            def sb(name, shape, dt):
                return es.enter_context(nc.sbuf_tensor(name, shape, dt))
            psp = Pool(nc, es, "p2bps", 8, [128, 512], F32, psum=True)
            sm = Pool(nc, es, "p2bs", 8, [128, 16], F32)
            xin = Pool(nc, es, "x1in", 1, [128, D], F32)
            xn = Pool(nc, es, "x1n", 1, [128, D], BF16)
            wqb = sb("wqb", [128, 16, D], BF16); Bwq = Buf()
            qfm = sb("qfm", [128, 16, 512], F32); Bqf = Buf()
            kT = sb("kT", [128, 16, 128], F32); BkT = Buf()
            sub = Pool(nc, es, "sub", 1, [128, D], F32)
            wk = Pool(nc, es, "wk", 2, [128, 256], F32)
            v16 = sb("v16", [128, 16, 16], F32); Bv16 = Buf()
            idx = sb("idx", [128, 16, 16], U32); Bidx = Buf()
            idxf = sb("idxf", [128, 16, 16], F32); Bidxf = Buf()
            cand = sb("cand", [128, 8, 256], F32); Bcand = Buf()
            tv = sb("tv", [128, 8, 16], F32); Btv = Buf()
            pos = sb("pos", [128, 8, 16], U32); Bpos = Buf()
            posf = sb("posf", [128, 8, 16], F32); Bposf = Buf()
            thr = sb("thr", [128, 16], F32); Bthr = Buf()
            big = Pool(nc, es, "big", 2, [128, 2048], F32)
            d1 = sb("d1", [128, 8, 16], F32); Bd1 = Buf()
            IJW = sb("IJW", [128, 3, 128], F32); BIJW = Buf()
            asel = sb("asel", [128, 128], F32); Basel = Buf()
            bsel = sb("bsel", [128, 128], F32); Bbsel = Buf()
            ew = sb("ew", [128, 128], F32); Bew = Buf()

            for k in range(16):
                O.dma("gpsimd", wqb[:, k, :], wq[k * 128:(k + 1) * 128, :], [], [Bwq], par=(k > 0))
                O.dma("sync", kT[:, k, :], keysT[k], [], [BkT], par=(k > 0))
            O.ts("vector", thr[:], iotaf[:, 0:16], 16.0, None, ALU.mult, None, [Bc], [Bthr])
            for tb in range(NTB):
                ts_ = slice(tb * 128, (tb + 1) * 128)
                xt, xtb = xin.get()
                xnt, xnb = xn.get()
                for i4 in range(4):
                    O.dma("sync", xt[:, i4 * 512:(i4 + 1) * 512], x1d[ts_, i4 * 512:(i4 + 1) * 512], [Bx1d], [xtb], par=(i4 > 0))
                st, stb = sm.get()
                O.memset("gpsimd", st[:, 0:1], 0.0, [stb])
                O.act(xnt[:], xt[:], AF.Square, [xtb], [xnb, stb], accum_out=st[:, 0:1])
                O.act(st[:, 1:2], st[:, 0:1], AF.Sqrt, [stb], [stb], scale=1.0 / D, bias=EPS)
                O.recip(st[:, 2:3], st[:, 1:2], [stb], [stb])
                O.ts("vector", xnt[:], xt[:], st[:, 2:3], None, ALU.mult, None, [xtb, stb], [xnb])
                for half in range(2):
                    pt, pb = psp.get()
                    ptb = pt[:].bitcast(BF16)
                    for kk in range(8):
                        k = half * 8 + kk
                        O.tr(ptb[:, kk * 128:(kk + 1) * 128], xnt[:, k * 128:(k + 1) * 128], identb[:], [xnb, Bc], [pb])
                    for kk in range(8):
                        k = half * 8 + kk
                        dst = h2[:, k, ts_]
                        src = ptb[:, kk * 128:(kk + 1) * 128]
                        if kk % 2 == 0:
                            O.act(dst, src, AF.Identity, [pb, Bfm4], [Bh2], scale=fm4[:, 3, k:k + 1], bias=fm4[:, 2, k:k + 1])
                        else:
                            O.ts("vector", dst, src, fm4[:, 3, k:k + 1], fm4[:, 2, k:k + 1], ALU.mult, ALU.add,
                                 [pb, Bfm4], [Bh2])
            for half in range(2):
              for qc in range(16):
                pt, pb = psp.get()
                for k in range(16):
                    O.mm(pt[:], wqb[:, k, qc * 128:(qc + 1) * 128], h2[:, k, half * 512:(half + 1) * 512],
                         k == 0, k == 15, [Bwq, Bh2], [pb])
                O.cp("scalar" if qc % 2 == 0 else "vector", qfm[:, qc, :], pt[:], [pb], [Bqf])
              for tb in range(half * 4, half * 4 + 4):
                ts_ = slice(tb * 128, (tb + 1) * 128)
                tl_ = slice((tb % 4) * 128, (tb % 4 + 1) * 128)
                sbt, sbb = sub.get()
                for g4 in range(4):
                    pt, pb = psp.get()
                    for q4 in range(4):
                        qc = g4 * 4 + q4
                        O.mm(pt[:, q4 * 128:(q4 + 1) * 128], qfm[:, qc, tl_], kT[:, qc, :], True, True, [Bqf, BkT], [pb])
                    O.cp("scalar", sbt[:, g4 * 512:(g4 + 1) * 512], pt[:], [pb], [sbb])
                for qc in range(16):
                    sv = sbt[:, qc * 128:(qc + 1) * 128]
                    w_, wb2 = wk.get()
                    S.op("vector", lambda e, o=v16[:, qc, 0:8], i=sv: e.max(out=o, in_=i), [sbb], [Bv16])
                    S.op("vector", lambda e, o=w_[:, 0:128], r=v16[:, qc, 0:8], i=sv: e.match_replace(out=o, in_to_replace=r, in_values=i, imm_value=NEG),
                         [sbb, Bv16], [wb2])
                    S.op("vector", lambda e, o=v16[:, qc, 8:16], i=w_[:, 0:128]: e.max(out=o, in_=i), [wb2], [Bv16])
                    S.op("vector", lambda e, o=idx[:, qc, 0:8], m=v16[:, qc, 0:8], i=sv: e.max_index(out=o, in_max=m, in_values=i), [sbb, Bv16], [Bidx])
                    S.op("vector", lambda e, o=idx[:, qc, 8:16], m=v16[:, qc, 8:16], i=sv: e.max_index(out=o, in_max=m, in_values=i), [sbb, Bv16], [Bidx])
                O.cp("vector", idxf[:], idx[:], [Bidx], [Bidxf])
                O.tt("vector", mkap(cand[:, 0, 0:1], [[256, 8], [16, 16], [1, 16]]),
                     mkap(v16[:, 0, 0:1], [[32, 8], [1, 16], [0, 16]]),
                     mkap(v16[:, 1, 0:1], [[32, 8], [0, 16], [1, 16]]), ALU.add, [Bv16], [Bcand])
                for hh in range(8):
                    cvw = cand[:, hh, :]
                    w_, wb2 = wk.get()
                    S.op("vector", lambda e, o=tv[:, hh, 0:8], i=cvw: e.max(out=o, in_=i), [Bcand], [Btv])
                    S.op("vector", lambda e, o=w_[:], r=tv[:, hh, 0:8], i=cvw: e.match_replace(out=o, in_to_replace=r, in_values=i, imm_value=NEG),
                         [Bcand, Btv], [wb2])
                    S.op("vector", lambda e, o=tv[:, hh, 8:16], i=w_[:]: e.max(out=o, in_=i), [wb2], [Btv])
                    S.op("vector", lambda e, o=pos[:, hh, 0:8], m=tv[:, hh, 0:8], i=cvw: e.max_index(out=o, in_max=m, in_values=i), [Bcand, Btv], [Bpos])
                    S.op("vector", lambda e, o=pos[:, hh, 8:16], m=tv[:, hh, 8:16], i=cvw: e.max_index(out=o, in_max=m, in_values=i), [Bcand, Btv], [Bpos])
                O.cp("vector", posf[:], pos[:], [Bpos], [Bposf])
                ge, geb = big.get()
                pr, prb = big.get()
                A4 = [[16, 8], [1, 16], [0, 16]]
                O.tt("vector", mkap(ge[:, 0:1], [[256, 8], [16, 16], [1, 16]]), mkap(posf[:, 0, 0:1], A4),
                     mkap(thr[:, 0:1], [[0, 8], [0, 16], [1, 16]]), ALU.is_ge, [Bposf, Bthr], [geb])
                O.cp("vector", mkap(d1[:, 0, 0:1], [[16, 8], [1, 1]]), mkap(idxf[:, 0, 0:1], [[32, 8], [1, 1]]), [Bidxf], [Bd1])
                O.tt("vector", mkap(d1[:, 0, 1:2], [[16, 8], [1, 15]]), mkap(idxf[:, 0, 1:2], [[32, 8], [1, 15]]),
                     mkap(idxf[:, 0, 0:1], [[32, 8], [1, 15]]), ALU.subtract, [Bidxf], [Bd1])
                O.tt("vector", mkap(pr[:, 0:1], [[256, 8], [16, 16], [1, 16]]), mkap(ge[:, 0:1], [[256, 8], [16, 16], [1, 16]]),
                     mkap(d1[:, 0, 0:1], [[16, 8], [0, 16], [1, 16]]), ALU.mult, [geb, Bd1], [prb])
                O.red(IJW[:, 0, :], mkap(pr[:, 0:1], [[16, 128], [1, 16]]), [prb], [BIJW])
                O.red(asel[:], mkap(ge[:, 0:1], [[16, 128], [1, 16]]), [geb], [Basel])
                O.ts("vector", asel[:], asel[:], -16.0, 16.0, ALU.mult, ALU.add, [Basel], [Basel])
                O.tt("vector", bsel[:], asel[:], mkap(posf[:, 0, 0:1], [[1, 128]]), ALU.add, [Basel, Bposf], [Bbsel])
                eq, eqb = big.get()
                O.tt("vector", mkap(eq[:, 0:1], [[256, 8], [16, 16], [1, 16]]), mkap(bsel[:, 0:1], A4),
                     mkap(iotaf[:, 0:1], [[0, 8], [0, 16], [1, 16]]), ALU.is_equal, [Bbsel, Bc], [eqb])
                O.tt("vector", mkap(eq[:, 0:1], [[256, 8], [16, 16], [1, 16]]), mkap(eq[:, 0:1], [[256, 8], [16, 16], [1, 16]]),
                     mkap(idxf[:, 1, 0:1], [[32, 8], [0, 16], [1, 16]]), ALU.mult, [eqb, Bidxf], [eqb])
                O.red(IJW[:, 1, :], mkap(eq[:, 0:1], [[16, 128], [1, 16]]), [eqb], [BIJW])
                O.tt("vector", mkap(ew[:, 0:1], [[16, 8], [1, 16]]), mkap(tv[:, 0, 0:1], [[16, 8], [1, 16]]),
                     mkap(tv[:, 0, 0:1], [[16, 8], [0, 16]]), ALU.subtract, [Btv], [Bew])
                O.act(ew[:], ew[:], AF.Exp, [Bew], [Bew])
                z_, zb_ = sm.get()
                O.red(z_[:, 0:8], mkap(ew[:, 0:1], [[16, 8], [1, 16]]), [Bew], [zb_])
                O.recip(z_[:, 8:16], z_[:, 0:8], [zb_], [zb_])
                O.tt("vector", mkap(IJW[:, 2, 0:1], [[16, 8], [1, 16]]), mkap(ew[:, 0:1], [[16, 8], [1, 16]]),
                     mkap(z_[:, 8:9], [[1, 8], [0, 16]]), ALU.mult, [Bew, zb_], [BIJW])
                pt, pb = psp.get()
                for i3 in range(3):
                    O.tr(pt[:, i3 * 128:(i3 + 1) * 128], IJW[:, i3, :], ident32[:], [BIJW, Bc], [pb])
                O.cp("scalar", ITt[:, :, ts_], mkap(pt[:, 0:1], [[128, 3], [1, 128]]), [pb], [BIT])
            S.emit()
        if stop_after == 3:
            with ExitStack() as es:
                bo = Buf()
                for i3 in range(3):
                    O.dma("sync", dbg_out[:, i3 * 1024:(i3 + 1) * 1024], ITt[:, i3, :], [BIT], [bo])
                S.wait_all("sync", [bo])
                S.emit()
            ies.close()
            return
        with ExitStack() as es:
            psp = Pool(nc, es, "p2cps", 4, [128, 512], F32, psum=True)
            Gs = Pool(nc, es, "Gs", 2, [128, 128, 128], BF16)
            EJ = Pool(nc, es, "EJ", 6, [128, 128], BF16)
            WI = Pool(nc, es, "WI", 6, [128, 128], BF16)
            firstg = True
            for tb in range(NTB):
                gs, gsb = Gs.get()
                for t4 in range(32):
                    pt, pb = psp.get()
                    for tl in range(4):
                        t = tb * 128 + t4 * 4 + tl
                        ej, ejb = EJ.get()
                        wi, wib = WI.get()
                        O.ts("vector", ej[:], iotaf[:], ITt[:, 1, t:t + 1], None, ALU.is_equal, None, [Bc, BIT], [ejb])
                        O.ts("vector", wi[:], iotaf[:], ITt[:, 0, t:t + 1], ITt[:, 2, t:t + 1], ALU.is_equal, ALU.mult,
                             [Bc, BIT], [wib])
                        O.mm(pt[:, tl * 128:(tl + 1) * 128], ej[:], wi[:], True, True, [ejb, wib], [pb])
                    dst = mkap(gs[:, 0, t4 * 4:t4 * 4 + 1], [[128, 128], [1, 4]])
                    src = mkap(pt[:, 0:1], [[1, 128], [128, 4]])
                    if t4 % 2 == 0:
                        O.cp("scalar", dst, src, [pb], [gsb])
                    else:
                        O.cp("vector", dst, src, [pb], [gsb])
                for g16 in range(16):
                    O.dma("sync", Gd[g16 * 8:(g16 + 1) * 8, :, tb * 128:(tb + 1) * 128].rearrange("i j t -> j i t"),
                          gs[:, g16 * 8:(g16 + 1) * 8, :], [gsb], [BGd], par=(not firstg))
                    firstg = False
            S.wait_all("sync", [BGd])
            S.emit()
        ies.close()
        with ExitStack() as aes:
            acc = aes.enter_context(nc.sbuf_tensor("acc", [128, 8, D], F32)); Bacc = [Buf() for _ in range(8)]
            with ExitStack() as es:
                psp = Pool(nc, es, "p2dps", 8, [128, 512], F32, psum=True)
                UTp = Pool(nc, es, "UTp", 2, [128, 16, 512], BF16)
                Vp = Pool(nc, es, "Vp", 2, [128, 4, D], BF16)
                Gp = Pool(nc, es, "Gp", 2, [128, 4, 1024], BF16)
                ATp = Pool(nc, es, "ATp", 2, [128, 4, 1024], BF16)
                tmp = Pool(nc, es, "p2dt", 4, [128, 512], F32)
                NG = DEBUG.get("ngroups", 32)
                for g in range(NG):
                    e0 = g * 512
                    ut, utb = UTp.get()
                    vt, vtb = Vp.get()
                    gg, ggb = Gp.get()
                    at, atb = ATp.get()
                    for k in range(16):
                        O.dma("gpsimd", ut[:, k, :], UT[k * 128:(k + 1) * 128, e0:e0 + 512], [], [utb], par=(k > 0))
                    for ii in range(4):
                        O.dma("sync", gg[:, ii, :], Gd[4 * g + ii], [BGd], [ggb], par=(ii > 0))
                    for ii in range(4):
                        r0 = (4 * g + ii) * 128
                        for hf in range(2):
                            O.dma("gpsimd", vt[:, ii, hf * 1024:(hf + 1) * 1024], Vt[r0:r0 + 128, hf * 1024:(hf + 1) * 1024], [], [vtb],
                                  par=(ii > 0 or hf > 0))
                    for ii in range(4):
                        for half in range(2):
                            hs = slice(half * 512, (half + 1) * 512)
                            ps_, psb = psp.get()
                            for k in range(16):
                                O.mm(ps_[:], ut[:, k, ii * 128:(ii + 1) * 128], h2[:, k, hs], k == 0, k == 15, [utb, Bh2], [psb])
                            t1_, t1b = tmp.get()
                            O.act(t1_[:], ps_[:], AF.Gelu_apprx_tanh, [psb], [t1b])
                            O.tt("gpsimd", at[:, ii, hs], t1_[:], gg[:, ii, hs], ALU.mult, [t1b, ggb], [atb])
                    for tb in range(NTB):
                        ts_ = slice(tb * 128, (tb + 1) * 128)
                        for cb in range(4):
                            cs = slice(cb * 512, (cb + 1) * 512)
                            pv, pvb = psp.get()
                            for ii in range(4):
                                O.mm(pv[:], at[:, ii, ts_], vt[:, ii, cs], ii == 0, ii == 3, [atb, vtb], [pvb])
                            if g == 0:
                                O.cp("vector", acc[:, tb, cs], pv[:], [pvb], [Bacc[tb]])
                            else:
                                O.tt("vector", acc[:, tb, cs], acc[:, tb, cs], pv[:], ALU.add, [Bacc[tb], pvb], [Bacc[tb]])
                S.emit()
            with ExitStack() as es:
                def sb(name, shape, dt):
                    return es.enter_context(nc.sbuf_tensor(name, shape, dt))
                sm = Pool(nc, es, "p2es", 4, [128, 16], F32)
                xin = Pool(nc, es, "x1f", 2, [128, D], F32)
                gate2 = sb("gate2", [128, D], F32); Bg2 = Buf()
                gft = sb("gft", [128, D], F32); Bgf = Buf()
                jk = sb("jk", [128, D], BF16); Bjk = Buf()
                for i4 in range(4):
                    O.dma("sync", gate2[:, i4 * 512:(i4 + 1) * 512], modsc[:, 10240 + i4 * 512:10240 + (i4 + 1) * 512], [], [Bg2], par=(i4 > 0))
                    O.dma("sync", gft[:, i4 * 512:(i4 + 1) * 512], gf_bc[:, i4 * 512:(i4 + 1) * 512], [], [Bgf], par=(i4 > 0))
                firsty = True
                for tb in range(NTB):
                    ts_ = slice(tb * 128, (tb + 1) * 128)
                    xt, xtb = xin.get()
                    for i4 in range(4):
                        O.dma("sync", xt[:, i4 * 512:(i4 + 1) * 512], x1d[ts_, i4 * 512:(i4 + 1) * 512], [Bx1d], [xtb], par=(i4 > 0))
                    a_ = acc[:, tb, :]
                    O.tt("vector", a_, a_, gate2[:], ALU.mult, [Bacc[tb], Bg2], [Bacc[tb]])
                    O.tt("gpsimd", a_, a_, xt[:], ALU.add, [Bacc[tb], xtb], [Bacc[tb]])
                    st, stb = sm.get()
                    O.memset("gpsimd", st[:, 0:1], 0.0, [stb])
                    O.act(jk[:], a_, AF.Square, [Bacc[tb]], [Bjk, stb], accum_out=st[:, 0:1])
                    O.act(st[:, 1:2], st[:, 0:1], AF.Sqrt, [stb], [stb], scale=1.0 / D, bias=EPS)
                    O.recip(st[:, 2:3], st[:, 1:2], [stb], [stb])
                    O.stt(a_, a_, st[:, 2:3], gft[:], ALU.mult, ALU.mult, [Bacc[tb], stb, Bgf], [Bacc[tb]])
                    for i4 in range(4):
                        O.dma("sync", y[ts_, i4 * 512:(i4 + 1) * 512], acc[:, tb, i4 * 512:(i4 + 1) * 512], [Bacc[tb]], [By], par=(not firsty))
                        firsty = False
                S.wait_all("sync", [By])
                S.emit()


def _prep_inputs(inp):
    f = np.float32
    x = np.asarray(inp["x"], f)
    c = np.asarray(inp["c"], f)
    w_in = np.asarray(inp["w_in"], f)[0]
    shared = {}
    shared["w_ada"] = np.ascontiguousarray(np.asarray(inp["w_ada"], f)[0])
    shared["bada_bc"] = np.ascontiguousarray(np.broadcast_to(np.asarray(inp["b_ada"], f)[0][None, :], (128, 6 * D)))
    shared["g1_bc"] = np.ascontiguousarray(np.broadcast_to(np.asarray(inp["norm1_g"], f)[0][None, :], (128, D)))
    shared["g2_bc"] = np.ascontiguousarray(np.broadcast_to(np.asarray(inp["norm2_g"], f)[0][None, :], (128, D)))
    shared["gf_bc"] = np.ascontiguousarray(np.broadcast_to(np.asarray(inp["final_norm_g"], f)[None, :], (128, D)))
    O_G, O_X, O_Z, O_XBC, O_DT = 0, 2048, 4096, 6144, 9216
    wins, wout_rows = [], []
    P = {k: [] for k in ("lru_cw", "lru_cb", "lru_wa", "lru_wi", "lru_ba", "lru_bi", "lru_lam", "ssd_cw", "ssd_cb",
                         "ssd_dtb", "ssd_alog", "ssd_d", "ssd_ng")}
    lcw = np.asarray(inp["lru_conv_w"], f)[0]
    lcb = np.asarray(inp["lru_conv_b"], f)[0]
    lwa = np.asarray(inp["lru_w_a"], f)[0]
    lwi = np.asarray(inp["lru_w_i"], f)[0]
    lba = np.asarray(inp["lru_b_a"], f)[0]
    lbi = np.asarray(inp["lru_b_i"], f)[0]
    lam = np.asarray(inp["lru_lambda"], f)[0]
    scw = np.asarray(inp["ssd_conv_w"], f)[0]
    scb = np.asarray(inp["ssd_conv_b"], f)[0]
    dtb = np.asarray(inp["ssd_dt_bias"], f)[0]
    alog = np.asarray(inp["ssd_a_log"], f)[0]
    sd = np.asarray(inp["ssd_d"], f)[0]
    sng = np.asarray(inp["ssd_norm_g"], f)[0]
    for j in range(4):
        cols = np.concatenate([
            O_G + j * 512 + np.arange(512), O_X + j * 512 + np.arange(512), O_Z + j * 512 + np.arange(512),
            O_XBC + j * 512 + np.arange(512), O_XBC + 2048 + j * 128 + np.arange(128),
            O_XBC + 2560 + j * 128 + np.arange(128), O_DT + j * 8 + np.arange(8)])
        wins.append(w_in[:, cols])
        lch = j * 512 + np.arange(512)
        P["lru_cw"].append(lcw[:, lch].reshape(4, 4, 128).transpose(2, 1, 0).reshape(128, 16))
        P["lru_cb"].append(lcb[lch].reshape(4, 128).T)
        P["lru_wa"].append(lwa[4 * j:4 * j + 4])
        P["lru_wi"].append(lwi[4 * j:4 * j + 4])
        P["lru_ba"].append(lba[4 * j:4 * j + 4].T)
        P["lru_bi"].append(lbi[4 * j:4 * j + 4].T)
        P["lru_lam"].append(lam[lch].reshape(4, 128).T)
        sch = np.concatenate([j * 512 + np.arange(512), 2048 + j * 128 + np.arange(128), 2560 + j * 128 + np.arange(128)])
        P["ssd_cw"].append(scw[:, sch].reshape(4, 6, 128).transpose(2, 1, 0).reshape(128, 24))
        P["ssd_cb"].append(scb[sch].reshape(6, 128).T)
        P["ssd_dtb"].append(np.broadcast_to(dtb[8 * j:8 * j + 8][None, :], (128, 8)))
        P["ssd_alog"].append(np.broadcast_to(alog[8 * j:8 * j + 8][None, :], (128, 8)))
        P["ssd_d"].append(np.broadcast_to(np.repeat(sd[8 * j:8 * j + 8], 64)[None, :], (128, 512)))
        P["ssd_ng"].append(np.broadcast_to(sng[j * 512:(j + 1) * 512][None, :], (128, 512)))
        wout_rows.append(np.concatenate([j * 512 + np.arange(512), 2048 + j * 512 + np.arange(512)]))
    shared["win"] = np.ascontiguousarray(np.stack(wins))
    for k, v in P.items():
        shared[k] = np.ascontiguousarray(np.stack([np.asarray(a, f) for a in v]))
    shared["wout"] = np.ascontiguousarray(np.asarray(inp["w_out"], f)[0][np.concatenate(wout_rows)])
    shared["wq"] = np.ascontiguousarray(np.asarray(inp["peer_w_q"], f)[0])
    sk = np.asarray(inp["peer_sub_keys"], f)[0]
    shared["keysT"] = np.ascontiguousarray(sk.reshape(16, 128, 128).transpose(0, 2, 1))
    shared["UT"] = np.ascontiguousarray(np.asarray(inp["peer_u"], f)[0].T)
    shared["V"] = np.ascontiguousarray(np.asarray(inp["peer_v"], f)[0])
    maps = []
    for core in range(8):
        b, q = core // 4, core % 4
        m = dict(shared)
        m["xb"] = np.ascontiguousarray(x[b])
        m["xq"] = np.ascontiguousarray(x[b, q * 1024:(q + 1) * 1024])
        m["cfm"] = np.ascontiguousarray(c[b].reshape(16, 128).T)
        s = np.zeros((128, 4), f)
        s[:, q] = 1.0
        m["sel"] = s
        maps.append(m)
    return maps


def kernel(**inputs):
    maps = _prep_inputs(inputs)
    nc = build()
    res = run_bass_kernel_spmd(nc, maps, core_ids=list(range(8)))
    out = np.zeros((2, SEQ, D), np.float32)
    for core in range(8):
        b, q = core // 4, core % 4
        out[b, q * 1024:(q + 1) * 1024] = res.results[core]["y"]
    return out
```

```python
import numpy as np
import concourse.bass as bass
import concourse.mybir as mybir
from concourse.bass_utils import run_bass_kernel_spmd
from contextlib import ExitStack

F32 = mybir.dt.float32
BF16 = mybir.dt.bfloat16
U32 = mybir.dt.uint32
AF = mybir.ActivationFunctionType
ALU = mybir.AluOpType
AX = mybir.AxisListType

D = 2048
SEQ = 4096
NSL = 4
TT = 512
NTILE = SEQ // TT
WSL = 2312
G0, X0, Z0, SX0, B0, C0, DT0 = 0, 512, 1024, 1536, 2048, 2176, 2304
EPS = 1e-6
NEG = -1.0e30
GC1 = 0.044715
GC2 = 1.5957691216057308
DEBUG = {}


class Buf:
    __slots__ = ("name", "w", "r")

    def __init__(self, name=""):
        self.name = name
        self.w = []
        self.r = []


class Sched:
    ENG = ("tensor", "vector", "scalar", "gpsimd", "sync")

    def __init__(self, nc, es, ndma=14):
        self.nc = nc
        self.es = es
        self.q = {e: [] for e in self.ENG}
        self.sems = []
        self.cur = {}
        self.cnt = {}
        for e in self.ENG:
            self._new_eng_sem(e)
        self.seen = {e: {} for e in self.ENG}
        self.dpool = {}
        for e in ("sync", "gpsimd", "scalar"):
            self.dpool[e] = [[self._sem("d%s%d" % (e, i)), 0] for i in range(ndma)]
        self.dnext = {e: 0 for e in self.dpool}
        self.nops = {e: 0 for e in self.ENG}

    def _sem(self, name):
        s = self.es.enter_context(self.nc.semaphore(name))
        self.sems.append(s)
        return len(self.sems) - 1

    def _new_eng_sem(self, e):
        self.cur[e] = self._sem("e%s%d" % (e, len(self.sems)))
        self.cnt[e] = 0

    def _waits(self, eng, reads, writes, extra=(), par=False):
        deps = {}

        def add(tok):
            if tok is None:
                return
            k, v = tok
            if deps.get(k, 0) < v:
                deps[k] = v
        for b in reads:
            for t in b.w:
                add(t)
        for b in writes:
            if not par:
                for t in b.w:
                    add(t)
            for t in b.r:
                add(t)
        for t in extra:
            add(t)
        for k, v in deps.items():
            if eng == "tensor" and k == self.cur["tensor"]:
                continue
            if self.seen[eng].get(k, 0) >= v:
                continue
            self.seen[eng][k] = v
            sem = self.sems[k]
            self.q[eng].append(lambda e, sem=sem, v=v: e.wait_ge(sem, v))

    def _mark(self, tok, reads, writes, par=False):
        for b in reads:
            if len(b.r) > 24:
                mx = {}
                for k, v in b.r:
                    if mx.get(k, 0) < v:
                        mx[k] = v
                b.r = list(mx.items())
            b.r.append(tok)
        for b in writes:
            if par:
                if len(b.w) > 24:
                    mx = {}
                    for k, v in b.w:
                        if mx.get(k, 0) < v:
                            mx[k] = v
                    b.w = list(mx.items())
                b.w.append(tok)
            else:
                b.w = [tok]
            b.r = []

    def begin_chain(self):
        self.rec = []

    def end_chain(self):
        r = self.rec
        self.rec = None
        return r

    def interleave(self, chains):
        n = max(len(c) for c in chains) if chains else 0
        for i in range(n):
            for c in chains:
                if i < len(c):
                    kind, eng, fn, reads, writes, par = c[i]
                    if kind == "op":
                        self.op(eng, fn, reads, writes)
                    else:
                        self.dma(eng, fn, reads, writes, par)

    def op(self, eng, fn, reads=(), writes=()):
        if getattr(self, "rec", None) is not None:
            self.rec.append(("op", eng, fn, tuple(reads), tuple(writes), False))
            return None
        self._waits(eng, reads, writes)
        if self.cnt[eng] >= 30000:
            self._new_eng_sem(eng)
        self.cnt[eng] += 1
        self.nops[eng] += 1
        k = self.cur[eng]
        tok = (k, self.cnt[eng])
        sem = self.sems[k]
        self.q[eng].append(lambda e, fn=fn, sem=sem: fn(e).then_inc(sem, 1))
        self._mark(tok, reads, writes)
        return tok

    def dma(self, eng, fn, reads=(), writes=(), par=False):
        if getattr(self, "rec", None) is not None:
            self.rec.append(("dma", eng, fn, tuple(reads), tuple(writes), par))
            return None
        pool = self.dpool[eng]
        i = self.dnext[eng]
        self.dnext[eng] = (i + 1) % len(pool)
        k, val = pool[i]
        extra = [(k, val)] if val > 0 else []
        self._waits(eng, reads, writes, extra, par)
        pool[i][1] = val + 16
        tok = (k, val + 16)
        sem = self.sems[k]
        self.q[eng].append(lambda e, fn=fn, sem=sem: fn(e).then_inc(sem, 16))
        self._mark(tok, reads, writes, par)
        return tok

    def wait_all(self, eng, bufs):
        self._waits(eng, bufs, bufs)

    def emit(self, name=None):
        self.nemit = getattr(self, "nemit", 0) + 1
        with self.nc.named_scope(name or ("ph%d" % self.nemit)), self.nc.Block() as block:
            for e in self.ENG:
                lst = self.q[e]

                def body(engine, lst=lst):
                    for f in lst:
                        f(engine)
                getattr(block, e)(body)
        self.q = {e: [] for e in self.ENG}


class Ops:
    def __init__(self, S):
        self.S = S

    def tt(self, eng, out, in0, in1, op, r, w):
        self.S.op(eng, lambda e: e.tensor_tensor(out, in0, in1, op), r, w)

    def ts(self, eng, out, in0, s1, s2, op0, op1, r, w):
        if op1 is None:
            self.S.op(eng, lambda e: e.tensor_scalar(out, in0, s1, None, op0), r, w)
        else:
            self.S.op(eng, lambda e: e.tensor_scalar(out, in0, s1, s2, op0, op1), r, w)

    def stt(self, out, in0, sc, in1, op0, op1, r, w):
        self.S.op("vector", lambda e: e.scalar_tensor_tensor(out, in0, sc, in1, op0, op1), r, w)

    def act(self, out, in_, func, r, w, **kw):
        self.S.op("scalar", lambda e: e.activation(out, in_, func, **kw), r, w)

    def cp(self, eng, out, in_, r, w):
        if eng == "scalar":
            self.S.op(eng, lambda e: e.activation(out, in_, AF.Identity), r, w)
        else:
            self.S.op(eng, lambda e: e.tensor_copy(out, in_), r, w)

    def mm(self, out, lhsT, rhs, start, stop, r, w):
        self.S.op("tensor", lambda e: e.matmul(out, lhsT=lhsT, rhs=rhs, start=start, stop=stop), r, w)

    def tr(self, out, in_, ident, r, w):
        self.S.op("tensor", lambda e: e.transpose(out, in_, ident), r, w)

    def dma(self, eng, out, in_, r, w, par=False):
        self.S.dma(eng, lambda e: e.dma_start(out=out, in_=in_), r, w, par)

    def memset(self, eng, out, val, w):
        self.S.op(eng, lambda e: e.memset(out, val), (), w)

    def red(self, out, in_, r, w):
        self.S.op("vector", lambda e: e.reduce_sum(out, in_, axis=AX.X), r, w)

    def recip(self, out, in_, r, w):
        self.S.op("vector", lambda e: e.reciprocal(out, in_), r, w)

    def scan(self, out, d0, d1, init, r, w):
        self.S.op("vector", lambda e: e.tensor_tensor_scan(out, d0, d1, init, ALU.mult, ALU.add), r, w)


def mkap(base, dims):
    return bass.AP(base.tensor, base.offset, [list(base.ap[0])] + [list(d) for d in dims])


class Pool:
    def __init__(self, nc, es, name, n, shape, dt, psum=False):
        self.t = []
        self.b = []
        for i in range(n):
            if psum:
                t = es.enter_context(nc.psum_tensor("%s%d" % (name, i), shape, dt))
            else:
                t = es.enter_context(nc.sbuf_tensor("%s%d" % (name, i), shape, dt))
            self.t.append(t)
            self.b.append(Buf("%s%d" % (name, i)))
        self.i = 0

    def get(self):
        i = self.i
        self.i = (i + 1) % len(self.t)
        return self.t[i], self.b[i]


class SubPool:
    def __init__(self, pool, idxs):
        self.t = [pool.t[i] for i in idxs]
        self.b = [pool.b[i] for i in idxs]
        self.i = 0

    def get(self):
        i = self.i
        self.i = (i + 1) % len(self.t)
        return self.t[i], self.b[i]


def build():
    nc = bass.Bass("TRN2", target_bir_lowering=False)

    def din(name, shape, dt=F32):
        return nc.dram_tensor(name, list(shape), dt, kind="ExternalInput").ap()

    xb = din("xb", [SEQ, D])
    xq = din("xq", [1024, D])
    cfm = din("cfm", [128, 16])
    w_ada = din("w_ada", [D, 6 * D])
    bada_bc = din("bada_bc", [128, 6 * D])
    g1_bc = din("g1_bc", [128, D])
    g2_bc = din("g2_bc", [128, D])
    gf_bc = din("gf_bc", [128, D])
    sel = din("sel", [128, 4])
    win = din("win", [NSL, D, WSL])
    lru_cw = din("lru_cw", [NSL, 128, 16])
    lru_cb = din("lru_cb", [NSL, 128, 4])
    lru_wa = din("lru_wa", [NSL, 4, 128, 128])
    lru_wi = din("lru_wi", [NSL, 4, 128, 128])
    lru_ba = din("lru_ba", [NSL, 128, 4])
    lru_bi = din("lru_bi", [NSL, 128, 4])
    lru_lam = din("lru_lam", [NSL, 128, 4])
    ssd_cw = din("ssd_cw", [NSL, 128, 24])
    ssd_cb = din("ssd_cb", [NSL, 128, 6])
    ssd_dtb = din("ssd_dtb", [NSL, 128, 8])
    ssd_alog = din("ssd_alog", [NSL, 128, 8])
    ssd_d = din("ssd_d", [NSL, 128, 512])
    ssd_ng = din("ssd_ng", [NSL, 128, 512])
    wout = din("wout", [4096, D])
    wq = din("wq", [D, D])
    keysT = din("keysT", [16, 128, 128])
    UT = din("UT", [D, 16384])
    Vt = din("V", [16384, D])
    y = nc.dram_tensor("y", [1024, D], F32, kind="ExternalOutput").ap()
    dbg_out = None
    if DEBUG.get("shape"):
        dbg_out = nc.dram_tensor("dbg", list(DEBUG["shape"]), F32, kind="ExternalOutput").ap()

    modsc = nc.dram_tensor("modsc", [128, 6 * D], F32).ap()
    mixd = nc.dram_tensor("mixd", [4, 8, 128, SEQ], BF16).ap()
    x1d = nc.dram_tensor("x1d", [1024, D], F32).ap()
    Gd = nc.dram_tensor("Gd", [128, 128, 1024], BF16).ap()

    stop_after = DEBUG.get("stop_after", 99)

    with ExitStack() as ges:
        S = Sched(nc, ges)
        O = Ops(S)

        def gsb(name, shape, dt):
            return ges.enter_context(nc.sbuf_tensor(name, shape, dt))

        ident32 = gsb("ident32", [128, 128], F32); Bc = Buf("consts")
        identb = gsb("identb", [128, 128], BF16)
        ones32 = gsb("ones32", [128, 128], F32)
        tri32 = gsb("tri32", [128, 128], F32)
        ustr32 = gsb("ustr32", [128, 128], F32)
        iotaf = gsb("iotaf", [128, 128], F32)
        fm4 = gsb("fm4", [128, 4, 16], F32); Bfm4 = Buf("fm4")
        selt = gsb("selt", [128, 4], F32)
        dfl = gsb("dfl", [128, 128], F32); Bdfl = Buf("dfl")

        O.S.op("gpsimd", lambda e: e.iota(dfl[:], pattern=[[1, 128]], base=0, channel_multiplier=-1,
                                          allow_small_or_imprecise_dtypes=True), (), [Bdfl])
        O.S.op("gpsimd", lambda e: e.iota(iotaf[:], pattern=[[1, 128]], base=0, channel_multiplier=0,
                                          allow_small_or_imprecise_dtypes=True), (), [Bc])
        O.ts("vector", ident32[:], dfl[:], 0.0, None, ALU.is_equal, None, [Bdfl], [Bc])
        O.ts("vector", tri32[:], dfl[:], 0.0, None, ALU.is_ge, None, [Bdfl], [Bc])
        O.ts("vector", ustr32[:], dfl[:], 0.0, None, ALU.is_lt, None, [Bdfl], [Bc])
        O.cp("vector", identb[:], ident32[:], [Bc], [Bc])
        O.memset("vector", ones32[:], 1.0, [Bc])
        O.dma("sync", selt[:], sel, [], [Bc])

        with ExitStack() as es:
            def sb(name, shape, dt):
                return es.enter_context(nc.sbuf_tensor(name, shape, dt))
            psp = Pool(nc, es, "p0ps", 4, [128, 512], F32, psum=True)
            wa = Pool(nc, es, "wa", 2, [128, 16, 512], F32)
            bad = Pool(nc, es, "bad", 2, [128, 512], F32)
            modbc = sb("modbc", [128, 6 * D], F32); Bmod = Buf("modbc")
            g12 = sb("g12", [128, D], F32); Bg12 = Buf("g12")
            crep = sb("crep", [128, 16, 128], F32); Bcrep = Buf("crep")
            cin = sb("cin", [128, 16], F32); Bcin = Buf("cin")
            cact = sb("cact", [128, 16], F32); Bcact = Buf("cact")
            tmpd = Pool(nc, es, "tmpd", 2, [128, 128], F32)

            O.dma("sync", cin[:], cfm, [], [Bcin])
            O.act(cact[:], cin[:], AF.Silu, [Bcin], [Bcact])
            for k in range(16):
                O.ts("vector", crep[:, k, :], ones32[:], cact[:, k:k + 1], None, ALU.mult, None, [Bc, Bcact], [Bcrep])
            wav = w_ada.rearrange("(k p) e -> p k e", p=128)
            for cb in range(24):
                wt, wb_ = wa.get()
                bt_, bb_ = bad.get()
                cs = slice(cb * 512, (cb + 1) * 512)
                for k in range(16):
                    O.dma("sync", wt[:, k, :], w_ada[k * 128:(k + 1) * 128, cs], [], [wb_], par=(k > 0))
                O.dma("sync", bt_[:], bada_bc[:, cs], [], [bb_])
                pt, pb = psp.get()
                for k in range(16):
                    O.mm(pt[:], crep[:, k, :], wt[:, k, :], k == 0, k == 15, [Bcrep, wb_], [pb])
                O.tt("vector", modbc[:, cs], pt[:], bt_[:], ALU.add, [pb, bb_], [Bmod])
            O.dma("sync", g12[:], g1_bc, [], [Bg12])
            O.stt(modbc[:, 2048:4096], modbc[:, 2048:4096], 1.0, g12[:], ALU.add, ALU.mult, [Bmod, Bg12], [Bmod])
            O.dma("sync", g12[:], g2_bc, [Bg12], [Bg12])
            O.stt(modbc[:, 8192:10240], modbc[:, 8192:10240], 1.0, g12[:], ALU.add, ALU.mult, [Bmod, Bg12], [Bmod])
            for v, sec in enumerate((0, 1, 3, 4)):
                for k in range(16):
                    td, tb_ = tmpd.get()
                    c0 = sec * 2048 + k * 128
                    O.tt("vector", td[:], modbc[:, c0:c0 + 128], ident32[:], ALU.mult, [Bmod, Bc], [tb_])
                    O.red(fm4[:, v, k:k + 1], td[:], [tb_], [Bfm4])
            Bmodsc = Buf("modsc")
            for i6 in range(6):
                O.dma("sync", modsc[:, i6 * 2048:(i6 + 1) * 2048], modbc[:, i6 * 2048:(i6 + 1) * 2048], [Bmod], [Bmodsc])
            if stop_after == 0:
                for i6 in range(6):
                    O.dma("sync", dbg_out[:, i6 * 2048:(i6 + 1) * 2048], modbc[:, i6 * 2048:(i6 + 1) * 2048], [Bmod], [Bmodsc])
            S.wait_all("sync", [Bmodsc])
            S.emit()

        Bmix = Buf("mixd")
        if stop_after >= 1:
            for j in range(DEBUG.get("nsl", NSL)):
                phase1_slice(nc, S, O, j, locals())
        if stop_after == 1:
            with ExitStack() as es:
                dt_ = es.enter_context(nc.sbuf_tensor("dbgt", [128, 4096], BF16)); bdt = Buf()
                df_ = es.enter_context(nc.sbuf_tensor("dbgf", [128, 4096], F32)); bdf = Buf()
                for cc in range(8):
                    O.dma("sync", dt_[:], mixd[0, cc], [Bmix], [bdt])
                    O.cp("vector", df_[:], dt_[:], [bdt], [bdf])
                    O.dma("sync", dbg_out[cc * 128:(cc + 1) * 128, :], df_[:], [bdf], [Bmix])
                S.wait_all("sync", [Bmix])
                S.emit()
        if stop_after >= 2:
            phase2(nc, S, O, locals())
    return nc


def phase1_slice(nc, S, O, j, G):
    xb, win, mixd = G["xb"], G["win"], G["mixd"]
    ident32, identb, ones32, tri32, ustr32 = G["ident32"], G["identb"], G["ones32"], G["tri32"], G["ustr32"]
    fm4, Bc, Bfm4, Bmix = G["fm4"], G["Bc"], G["Bfm4"], G["Bmix"]
    ntile = DEBUG.get("ntile", NTILE)
    with ExitStack() as es:
        def sb(name, shape, dt):
            return es.enter_context(nc.sbuf_tensor("%s_%d" % (name, j), shape, dt))
        psp = Pool(nc, es, "p1ps%d" % j, 8, [128, 512], F32, psum=True)
        tmp = Pool(nc, es, "p1t%d" % j, 16, [128, 512], F32)
        sm = Pool(nc, es, "p1s%d" % j, 24, [128, 16], F32)
        smw = Pool(nc, es, "p1w%d" % j, 8, [128, 64], F32)
        s16 = Pool(nc, es, "p1h%d" % j, 6, [128, 128], BF16)
        pgen, pyP, pzP, phg = SubPool(psp, [0, 1, 2]), SubPool(psp, [3, 4]), SubPool(psp, [5]), SubPool(psp, [6, 7])
        W = sb("W", [128, 16, WSL], BF16); BW = Buf("W")
        xin = Pool(nc, es, "xin%d" % j, 1, [128, D], F32)
        xn = Pool(nc, es, "xn%d" % j, 1, [128, D], BF16)
        hbuf = [sb("h0", [128, 16, TT], BF16), sb("h1", [128, 16, TT], BF16)]
        Bhs = [Buf("h0"), Buf("h1")]
        h, Bh = hbuf[0], Bhs[0]
        xh = [sb("xh%d" % c, [128, TT + 3], F32) for c in range(10)]
        Bxh = [Buf("xh%d" % c) for c in range(10)]
        hst = sb("hst", [128, 4], F32); Bhst = [Buf() for _ in range(4)]
        prm = sb("prm", [128, 16 + 4 + 4 + 4 + 4 + 24 + 6 + 8 + 8], F32); Bprm = Buf("prm")
        cvec = sb("cvec", [128, 8], F32); Bcv = Buf("cvec")
        abc = sb("abc", [128, 8], F32); Babc = Buf("abc")
        dbc = sb("dbc", [128, 512], F32)
        ngb = sb("ngb", [128, 512], F32)
        wab = sb("wab", [128, 8, 128], BF16); Bwab = Buf("wab")
        xsf = [sb("xsf%d" % c, [128, TT], F32) for c in range(4)]
        Bxsf = [Buf() for _ in range(4)]
        BCf = sb("BCf", [128, 2, TT], BF16); BBC = [Buf(), Buf()]
        Lh = Pool(nc, es, "Lh%d" % j, 1, [128, 8, 128], F32)
        prev = sb("prev", [128, 512], F32); Bprev = Buf("prev")
        prevb = sb("prevb", [128, 512], BF16); Bprevb = Buf("prevb")
        mos = sb("mos", [128, 4, TT], BF16); Bmos = Buf("mos")
        P_LCW, P_LCB, P_LBA, P_LBI, P_LAM, P_SCW, P_SCB, P_DTB, P_ALOG = 0, 16, 20, 24, 28, 32, 56, 62, 70

        for k in range(16):
            O.dma("gpsimd", W[:, k, :], win[j, k * 128:(k + 1) * 128, :], [], [BW], par=(k > 0))
        for h_ in range(4):
            O.dma("gpsimd", wab[:, h_, :], G["lru_wa"][j, h_], [], [Bwab])
            O.dma("gpsimd", wab[:, 4 + h_, :], G["lru_wi"][j, h_], [], [Bwab])
        for (off, n, src) in ((P_LCW, 16, "lru_cw"), (P_LCB, 4, "lru_cb"), (P_LBA, 4, "lru_ba"), (P_LBI, 4, "lru_bi"),
                              (P_LAM, 4, "lru_lam"), (P_SCW, 24, "ssd_cw"), (P_SCB, 6, "ssd_cb"), (P_DTB, 8, "ssd_dtb"),
                              (P_ALOG, 8, "ssd_alog")):
            O.dma("sync", prm[:, off:off + n], G[src][j], [], [Bprm])
        O.dma("sync", dbc[:], G["ssd_d"][j], [], [Bprm])
        O.dma("sync", ngb[:], G["ssd_ng"][j], [], [Bprm])
        O.act(cvec[:, 0:4], prm[:, P_LAM:P_LAM + 4], AF.Exp, [Bprm], [Bcv], scale=-1.0)
        O.act(cvec[:, 0:4], cvec[:, 0:4], AF.Ln, [Bcv], [Bcv], bias=1.0)
        O.ts("vector", cvec[:, 4:8], cvec[:, 0:4], -16.0, None, ALU.mult, None, [Bcv], [Bcv])
        O.ts("vector", cvec[:, 0:4], cvec[:, 0:4], -8.0, None, ALU.mult, None, [Bcv], [Bcv])
        O.act(abc[:], prm[:, P_ALOG:P_ALOG + 8], AF.Exp, [Bprm], [Babc])
        O.ts("vector", abc[:], abc[:], -1.0, None, ALU.mult, None, [Babc], [Babc])
        for c in range(10):
            O.memset("gpsimd", xh[c][:, 0:3], 0.0, [Bxh[c]])
        O.memset("gpsimd", prev[:], 0.0, [Bprev])
        O.memset("gpsimd", prevb[:], 0.0, [Bprevb])

        def inproj(col, n=128, pool=None):
            pt, pb = (pool or psp).get()
            for k in range(16):
                O.mm(pt[0:n, :], W[:, k, col:col + n], h[:, k, :], k == 0, k == 15, [BW, Bh], [pb])
            return pt, pb

        def conv(c, pt, pb, wcol, bcol):
            O.cp("scalar", xh[c][:, 3:TT + 3], pt[:], [pb], [Bxh[c]])
            r, rb = tmp.get()
            O.ts("vector", r[:], xh[c][:, 0:TT], prm[:, wcol:wcol + 1], prm[:, bcol:bcol + 1], ALU.mult, ALU.add,
                 [Bxh[c], Bprm], [rb])
            for k in range(1, 4):
                O.stt(r[:], xh[c][:, k:k + TT], prm[:, wcol + k:wcol + k + 1], r[:], ALU.mult, ALU.add,
                      [Bxh[c], Bprm, rb], [rb])
            O.cp("gpsimd", xh[c][:, 0:3], xh[c][:, TT:TT + 3], [Bxh[c]], [Bxh[c]])
            return r, rb

        def hgen(tix):
            hh_, Bhh_ = hbuf[tix % 2], Bhs[tix % 2]
            tt0 = tix * TT
            for tb in range(4):
                xt, xtb = xin.get()
                xnt, xnb = xn.get()
                O.dma("sync", xt[:], xb[tt0 + tb * 128:tt0 + (tb + 1) * 128, :], [], [xtb])
                st, stb = sm.get()
                O.memset("gpsimd", st[:, 0:1], 0.0, [stb])
                O.act(xnt[:], xt[:], AF.Square, [xtb], [xnb, stb], accum_out=st[:, 0:1])
                O.act(st[:, 1:2], st[:, 0:1], AF.Ln, [stb], [stb], scale=1.0 / D, bias=EPS)
                O.act(st[:, 2:3], st[:, 1:2], AF.Exp, [stb], [stb], scale=-0.5)
                O.ts("vector", xnt[:], xt[:], st[:, 2:3], None, ALU.mult, None, [xtb, stb], [xnb])
                for half in range(2):
                    pt, pb = phg.get()
                    ptb = pt[:].bitcast(BF16)
                    for kk in range(8):
                        k = half * 8 + kk
                        O.tr(ptb[:, kk * 128:(kk + 1) * 128], xnt[:, k * 128:(k + 1) * 128], identb[:], [xnb, Bc], [pb])
                    for kk in range(8):
                        k = half * 8 + kk
                        dst = hh_[:, k, tb * 128:(tb + 1) * 128]
                        src = ptb[:, kk * 128:(kk + 1) * 128]
                        if kk % 2 == 0:
                            O.act(dst, src, AF.Identity, [pb, Bfm4], [Bhh_], scale=fm4[:, 1, k:k + 1], bias=fm4[:, 0, k:k + 1])
                        else:
                            O.ts("vector", dst, src, fm4[:, 1, k:k + 1], fm4[:, 0, k:k + 1], ALU.mult, ALU.add,
                                 [pb, Bfm4], [Bhh_])

        hgen(0)
        for ti in range(ntile):
            t0 = ti * TT
            h, Bh = hbuf[ti % 2], Bhs[ti % 2]
            def lru_chain(cc, pp_):
                pg, pgb = inproj(G0 + cc * 128, pool=pp_)
                px, pxb = inproj(X0 + cc * 128, pool=pp_)
                sA, sAb = tmp.get()
                sB, sBb = tmp.get()
                O.act(sB[:], pg[:], AF.Gelu_apprx_tanh, [pgb], [sBb])
                xc, xcb_ = conv(cc, px, pxb, P_LCW + cc * 4, P_LCB + cc)
                xb16v = sA[:].bitcast(BF16)[:, 0:TT]
                O.cp("scalar", xb16v, xc[:], [xcb_], [sAb])
                pa, pab = pp_.get()
                pi, pib = pp_.get()
                O.mm(pa[:], wab[:, cc, :], xb16v, True, True, [Bwab, sAb], [pab])
                O.mm(pi[:], wab[:, 4 + cc, :], xb16v, True, True, [Bwab, sAb], [pib])
                r_, rb_ = tmp.get()
                i_, ib_ = tmp.get()
                m_, mb_ = tmp.get()
                a_, ab_ = sA, sAb
                O.act(r_[:], pa[:], AF.Sigmoid, [pab, Bprm], [rb_], bias=prm[:, P_LBA + cc:P_LBA + cc + 1])
                O.act(i_[:], pi[:], AF.Sigmoid, [pib, Bprm], [ib_], bias=prm[:, P_LBI + cc:P_LBI + cc + 1])
                O.act(a_[:], r_[:], AF.Exp, [rb_, Bcv], [ab_], scale=cvec[:, cc:cc + 1])
                O.act(m_[:], r_[:], AF.Exp, [rb_, Bcv], [mb_], scale=cvec[:, 4 + cc:5 + cc])
                O.act(m_[:], m_[:], AF.Ln, [mb_], [mb_], scale=-1.0, bias=1.0)
                O.act(m_[:], m_[:], AF.Exp, [mb_], [mb_], scale=0.5)
                O.tt("vector", i_[:], i_[:], xc[:], ALU.mult, [ib_, xcb_], [ib_])
                O.tt("vector", i_[:], i_[:], m_[:], ALU.mult, [ib_, mb_], [ib_])
                init = 0.0 if ti == 0 else hst[:, cc:cc + 1]
                O.scan(r_[:], a_[:], i_[:], init, [ab_, ib_, Bhst[cc]], [rb_])
                O.cp("vector", hst[:, cc:cc + 1], r_[:, TT - 1:TT], [rb_], [Bhst[cc]])
                mov = xc[:].bitcast(BF16)[:, 0:TT]
                O.tt("vector", mov, sB[:], r_[:], ALU.mult, [sBb, rb_, xcb_], [xcb_])
                O.dma("sync", mixd[j, cc, :, t0:t0 + TT], mov, [xcb_], [Bmix], par=True)

            lpools = (SubPool(psp, [0, 1, 2]), SubPool(psp, [3, 4, 5]))
            cpool = SubPool(psp, [6, 7])

            def conv_chain(cs_):
                for c in cs_:
                    col = SX0 + c * 128 if c < 4 else (B0 if c == 4 else C0)
                    pp, ppb = inproj(col, pool=cpool)
                    cv, cvb = conv(4 + c, pp, ppb, P_SCW + c * 4, P_SCB + c)
                    if c < 4:
                        O.act(xsf[c][:], cv[:], AF.Silu, [cvb], [Bxsf[c]])
                    else:
                        O.act(BCf[:, c - 4, :], cv[:], AF.Silu, [cvb], [BBC[c - 4]])

            for pr in range(2):
                chains = []
                for ci_, cc in enumerate((2 * pr, 2 * pr + 1)):
                    S.begin_chain()
                    lru_chain(cc, lpools[ci_])
                    chains.append(S.end_chain())
                S.begin_chain()
                conv_chain(range(3 * pr, 3 * pr + 3))
                chains.append(S.end_chain())
                S.interleave(chains)
            pd, pdb = psp.get()
            for tb in range(4):
                ts_ = slice(tb * 128, (tb + 1) * 128)
                for k in range(16):
                    O.mm(pd[:, tb * 8:(tb + 1) * 8], h[:, k, ts_], W[:, k, DT0:DT0 + 8], k == 0, k == 15, [Bh, BW], [pdb])
            q, qb = smw.get()
            O.tt("vector", mkap(q[:, 0:1], [[8, 4], [1, 8]]), mkap(pd[:, 0:1], [[8, 4], [1, 8]]),
                 mkap(prm[:, P_DTB:P_DTB + 1], [[0, 4], [1, 8]]), ALU.add, [pdb, Bprm], [qb])
            O.act(q[:, 0:32], q[:, 0:32], AF.Exp, [qb], [qb])
            O.act(q[:, 0:32], q[:, 0:32], AF.Ln, [qb], [qb], bias=1.0)
            O.tt("vector", mkap(q[:, 32:33], [[8, 4], [1, 8]]), mkap(q[:, 0:1], [[8, 4], [1, 8]]),
                 mkap(abc[:, 0:1], [[0, 4], [1, 8]]), ALU.mult, [qb, Babc], [qb])
            pa2, pa2b = psp.get()
            O.mm(pa2[:, 0:32], tri32[:], q[:, 32:64], True, True, [Bc, qb], [pa2b])
            O.mm(pa2[:, 32:64], ones32[:], q[:, 32:64], True, True, [Bc, qb], [pa2b])
            e_, eb_ = smw.get()
            f_, fb_ = smw.get()
            g_, gb_ = smw.get()
            O.cp("scalar", e_[:], pa2[:, 0:64], [pa2b], [eb_])
            O.act(f_[:], e_[:], AF.Exp, [eb_], [fb_])
            O.tt("vector", g_[:, 0:32], e_[:, 32:64], e_[:, 0:32], ALU.subtract, [eb_], [gb_])
            O.act(g_[:, 0:32], g_[:, 0:32], AF.Exp, [gb_], [gb_])
            O.tt("vector", g_[:, 0:32], g_[:, 0:32], q[:, 0:32], ALU.mult, [gb_, qb], [gb_])

            def stage1(tb):
                ts_ = slice(tb * 128, (tb + 1) * 128)
                o8 = tb * 8
                pxs, pxsb = pgen.get()
                for c in range(4):
                    O.tr(pxs[:, c * 128:(c + 1) * 128], xsf[c][:, ts_], ident32[:], [Bxsf[c], Bc], [pxsb])
                xstm, xstmb = tmp.get()
                O.cp("scalar", xstm[:], pxs[:], [pxsb], [xstmb])
                pbt, pbtb = pgen.get()
                pbtv = pbt[:].bitcast(BF16)
                O.tr(pbtv[:, 0:128], BCf[:, 0, ts_], identb[:], [BBC[0], Bc], [pbtb])
                btm, btmb = s16.get()
                O.cp("vector", btm[:], pbtv[:, 0:128], [pbtb], [btmb])
                xcx, xcxb = tmp.get()
                xcv = xcx[:].bitcast(BF16)
                x3 = mkap(xstm[:, 0:1], [[64, 8], [1, 64]])
                O.tt("vector", mkap(xcv[:, 0:1], [[64, 8], [1, 64]]), x3, mkap(q[:, o8:o8 + 1], [[1, 8], [0, 64]]), ALU.mult,
                     [xstmb, qb], [xcxb])
                O.tt("gpsimd", mkap(xcv[:, 512:513], [[64, 8], [1, 64]]), x3, mkap(g_[:, o8:o8 + 1], [[1, 8], [0, 64]]), ALU.mult,
                     [xstmb, gb_], [xcxb])
                lh, lhb = Lh.get()
                O.tt("vector", lh[:], mkap(ustr32[:, 0:1], [[0, 8], [1, 128]]), mkap(q[:, 32 + o8:33 + o8], [[1, 8], [0, 128]]),
                     ALU.mult, [Bc, qb], [lhb])
                ps1, ps1b = pgen.get()
                ps2, ps2b = pgen.get()
                for hh in range(8):
                    pdst = (ps1 if hh < 4 else ps2)
                    pbuf = (ps1b if hh < 4 else ps2b)
                    O.mm(pdst[:, (hh % 4) * 128:(hh % 4 + 1) * 128], lh[:, hh, :], tri32[:], True, True, [lhb, Bc], [pbuf])
                E, Eb = tmp.get()
                Ev = E[:].bitcast(BF16)
                O.act(Ev[:, 0:512], ps1[:], AF.Exp, [ps1b], [Eb])
                O.act(Ev[:, 512:1024], ps2[:], AF.Exp, [ps2b], [Eb])
                pc, pcb = pgen.get()
                O.mm(pc[:, 0:128], BCf[:, 0, ts_], BCf[:, 1, ts_], True, True, [BBC[0], BBC[1]], [pcb])
                cbm, cbmb = s16.get()
                O.tt("vector", cbm[:], pc[:, 0:128], tri32[:], ALU.mult, [pcb, Bc], [cbmb])
                O.tt("vector", mkap(Ev[:, 0:1], [[128, 8], [1, 128]]), mkap(Ev[:, 0:1], [[128, 8], [1, 128]]),
                     mkap(cbm[:, 0:1], [[0, 8], [1, 128]]), ALU.mult, [Eb, cbmb], [Eb])
                py, pyb = pyP.get()
                for hh in range(8):
                    O.mm(py[:, hh * 64:(hh + 1) * 64], Ev[:, hh * 128:(hh + 1) * 128], xcv[:, hh * 64:(hh + 1) * 64],
                         True, True, [Eb, xcxb], [pyb])
                pz, pzb = pzP.get()
                for k in range(16):
                    O.mm(pz[:], h[:, k, ts_], W[:, k, Z0:Z0 + 512], k == 0, k == 15, [Bh, BW], [pzb])
                sgz, sgzb = tmp.get()
                O.act(sgz[:], pz[:], AF.Silu, [pzb], [sgzb])
                return dict(xstm=xstm, xstmb=xstmb, btm=btm, btmb=btmb, xcv=xcv, xcxb=xcxb, py=py, pyb=pyb, sgz=sgz, sgzb=sgzb)

            def stage2(tb, st):
                ts_ = slice(tb * 128, (tb + 1) * 128)
                o8 = tb * 8
                xstm, xstmb, xcv, xcxb = st["xstm"], st["xstmb"], st["xcv"], st["xcxb"]
                py, pyb, sgz, sgzb = st["py"], st["pyb"], st["sgz"], st["sgzb"]
                pyo, pyob = pgen.get()
                O.mm(pyo[:], BCf[:, 1, ts_], prevb[:], True, True, [BBC[1], Bprevb], [pyob])
                yv, yvb = tmp.get()
                O.tt("vector", mkap(yv[:, 0:1], [[64, 8], [1, 64]]), mkap(pyo[:, 0:1], [[64, 8], [1, 64]]),
                     mkap(f_[:, o8:o8 + 1], [[1, 8], [0, 64]]), ALU.mult, [pyob, fb_], [yvb])
                O.tt("vector", yv[:], yv[:], py[:], ALU.add, [yvb, pyb], [yvb])
                t2, t2b = tmp.get()
                O.tt("gpsimd", t2[:], xstm[:], dbc[:], ALU.mult, [xstmb, Bprm], [t2b])
                O.tt("vector", yv[:], yv[:], t2[:], ALU.add, [yvb, t2b], [yvb])
                pst, pstb = pgen.get()
                O.mm(pst[:], st["btm"][:], xcv[:, 512:1024], True, True, [st["btmb"], xcxb], [pstb])
                O.tt("vector", mkap(prev[:, 0:1], [[64, 8], [1, 64]]), mkap(prev[:, 0:1], [[64, 8], [1, 64]]),
                     mkap(f_[:, 32 + o8:33 + o8], [[1, 8], [0, 64]]), ALU.mult, [Bprev, fb_], [Bprev])
                O.tt("vector", prev[:], prev[:], pst[:], ALU.add, [Bprev, pstb], [Bprev])
                O.cp("scalar", prevb[:], prev[:], [Bprev], [Bprevb])
                O.tt("vector", yv[:], yv[:], sgz[:], ALU.mult, [yvb, sgzb], [yvb])
                n_, nb_ = sm.get()
                O.memset("gpsimd", n_[:, 0:1], 0.0, [nb_])
                O.act(sgz[:], yv[:], AF.Square, [yvb], [sgzb, nb_], accum_out=n_[:, 0:1])
                O.act(n_[:, 1:2], n_[:, 0:1], AF.Ln, [nb_], [nb_], scale=1.0 / 512, bias=EPS)
                O.act(n_[:, 2:3], n_[:, 1:2], AF.Exp, [nb_], [nb_], scale=-0.5)
                ob, obb = tmp.get()
                obv = ob[:].bitcast(BF16)[:, 0:512]
                O.stt(obv, yv[:], n_[:, 2:3], ngb[:], ALU.mult, ALU.mult, [yvb, nb_, Bprm], [obb])
                pot, potb = pgen.get()
                potv = pot[:].bitcast(BF16)
                for c in range(4):
                    O.tr(potv[:, c * 128:(c + 1) * 128], obv[:, c * 128:(c + 1) * 128], identb[:], [obb, Bc], [potb])
                O.cp("scalar", mos[:, :, ts_], mkap(potv[:, 0:1], [[128, 4], [1, 128]]), [potb], [Bmos])

            S.begin_chain()
            sts = {0: stage1(0)}
            for tb in range(4):
                if tb + 1 < 4:
                    sts[tb + 1] = stage1(tb + 1)
                stage2(tb, sts.pop(tb))
            for c in range(4):
                O.dma("sync", mixd[j, 4 + c, :, t0:t0 + TT], mos[:, c, :], [Bmos], [Bmix], par=True)
            chT = S.end_chain()
            chains = [chT]
            if ti + 1 < ntile:
                S.begin_chain()
                hgen(ti + 1)
                chains.append(S.end_chain())
            S.interleave(chains)
        S.wait_all("sync", [Bmix])
        S.emit()


def phase2(nc, S, O, G):
    mixd, modsc, x1d, Gd, xq, y = G["mixd"], G["modsc"], G["x1d"], G["Gd"], G["xq"], G["y"]
    ident32, identb, iotaf, fm4, selt = G["ident32"], G["identb"], G["iotaf"], G["fm4"], G["selt"]
    Bc, Bfm4, Bmix = G["Bc"], G["Bfm4"], G["Bmix"]
    wout, wq, keysT, UT, Vt, gf_bc = G["wout"], G["wq"], G["keysT"], G["UT"], G["Vt"], G["gf_bc"]
    dbg_out = G["dbg_out"]
    stop_after = G["stop_after"]
    NTB = 8
    Bx1d = Buf("x1d")
    BGd = Buf("Gd")
    By = Buf("y")

    with ExitStack() as es:
        def sb(name, shape, dt):
            return es.enter_context(nc.sbuf_tensor(name, shape, dt))
        psp = Pool(nc, es, "p2aps", 4, [128, 512], F32, psum=True)
        msel = sb("msel", [128, 32, 1024], BF16); Bmsel = [Buf() for _ in range(32)]
        tq = Pool(nc, es, "tq", 8, [128, 1024], BF16)
        wo = Pool(nc, es, "wo", 2, [128, 32, 512], BF16)
        gate1 = sb("gate1", [128, D], F32); Bg1 = Buf()
        xqb = Pool(nc, es, "xqb", 3, [128, 512], F32)
        x1b = Pool(nc, es, "x1b", 3, [128, 512], F32)
        for i4 in range(4):
            O.dma("sync", gate1[:, i4 * 512:(i4 + 1) * 512], modsc[:, 4096 + i4 * 512:4096 + (i4 + 1) * 512], [], [Bg1], par=(i4 > 0))
        for ch in range(32):
            jj, cc = ch // 8, ch % 8
            tqs = []
            for Q in range(4):
                t_, tb_ = tq.get()
                O.dma("sync", t_[:], mixd[jj, cc, :, Q * 1024:(Q + 1) * 1024], [Bmix], [tb_])
                tqs.append((t_, tb_))
            O.ts("vector", msel[:, ch, :], tqs[0][0][:], selt[:, 0:1], None, ALU.mult, None, [tqs[0][1], Bc], [Bmsel[ch]])
            for Q in range(1, 4):
                O.stt(msel[:, ch, :], tqs[Q][0][:], selt[:, Q:Q + 1], msel[:, ch, :], ALU.mult, ALU.add,
                      [tqs[Q][1], Bc, Bmsel[ch]], [Bmsel[ch]])
        first = True
        for cb in range(4):
            cs = slice(cb * 512, (cb + 1) * 512)
            wt, wb_ = wo.get()
            for ch in range(32):
                O.dma("gpsimd", wt[:, ch, :], wout[ch * 128:(ch + 1) * 128, cs], [], [wb_], par=(ch > 0))
            for tb in range(NTB):
                ts_ = slice(tb * 128, (tb + 1) * 128)
                pt, pb = psp.get()
                for ch in range(32):
                    O.mm(pt[:], msel[:, ch, ts_], wt[:, ch, :], ch == 0, ch == 31, [Bmsel[ch], wb_], [pb])
                xt, xtb = xqb.get()
                O.dma("sync", xt[:], xq[ts_, cs], [], [xtb])
                ot, otb = x1b.get()
                O.tt("vector", ot[:], pt[:], gate1[:, cs], ALU.mult, [pb, Bg1], [otb])
                O.tt("gpsimd", ot[:], ot[:], xt[:], ALU.add, [otb, xtb], [otb])
                O.dma("sync", x1d[ts_, cs], ot[:], [otb], [Bx1d], par=(not first))
                first = False
        S.wait_all("sync", [Bx1d])
        S.emit()
    if stop_after == 2:
        with ExitStack() as es:
            df_ = es.enter_context(nc.sbuf_tensor("dbgf2", [128, D], F32)); bdf = Buf()
            bo = Buf()
            for tb in range(NTB):
                O.dma("sync", df_[:], x1d[tb * 128:(tb + 1) * 128, :], [Bx1d], [bdf])
                O.dma("sync", dbg_out[tb * 128:(tb + 1) * 128, :], df_[:], [bdf], [bo])
            S.wait_all("sync", [bo])
            S.emit()
        return

    with ExitStack() as hes:
        h2 = hes.enter_context(nc.sbuf_tensor("h2", [128, 16, 1024], BF16)); Bh2 = Buf("h2")
        ies = ExitStack()
        ITt = ies.enter_context(nc.sbuf_tensor("ITt", [128, 3, 1024], F32)); BIT = Buf("IT")
        with ExitStack() as es:
            def sb(name, shape, dt):
                return es.enter_context(nc.sbuf_tensor(name, shape, dt))
            psp = Pool(nc, es, "p2bps", 8, [128, 512], F32, psum=True)
            sm = Pool(nc, es, "p2bs", 8, [128, 16], F32)
            xin = Pool(nc, es, "x1in", 1, [128, D], F32)
            xn = Pool(nc, es, "x1n", 1, [128, D], BF16)
            wqb = sb("wqb", [128, 16, D], BF16); Bwq = Buf()
            qfm = sb("qfm", [128, 16, 512], F32); Bqf = Buf()
            kT = sb("kT", [128, 16, 128], F32); BkT = Buf()
            sub = Pool(nc, es, "sub", 1, [128, D], F32)
            wk = Pool(nc, es, "wk", 2, [128, 256], F32)
            v16 = sb("v16", [128, 16, 16], F32); Bv16 = Buf()
            idx = sb("idx", [128, 16, 16], U32); Bidx = Buf()
            idxf = sb("idxf", [128, 16, 16], F32); Bidxf = Buf()
            cand = sb("cand", [128, 8, 256], F32); Bcand = Buf()
            tv = sb("tv", [128, 8, 16], F32); Btv = Buf()
            pos = sb("pos", [128, 8, 16], U32); Bpos = Buf()
            posf = sb("posf", [128, 8, 16], F32); Bposf = Buf()
            thr = sb("thr", [128, 16], F32); Bthr = Buf()
            big = Pool(nc, es, "big", 2, [128, 2048], F32)
            d1 = sb("d1", [128, 8, 16], F32); Bd1 = Buf()
            IJW = sb("IJW", [128, 3, 128], F32); BIJW = Buf()
            asel = sb("asel", [128, 128], F32); Basel = Buf()
            bsel = sb("bsel", [128, 128], F32); Bbsel = Buf()
            ew = sb("ew", [128, 128], F32); Bew = Buf()

            for k in range(16):
                O.dma("gpsimd", wqb[:, k, :], wq[k * 128:(k + 1) * 128, :], [], [Bwq], par=(k > 0))
                O.dma("sync", kT[:, k, :], keysT[k], [], [BkT], par=(k > 0))
            O.ts("vector", thr[:], iotaf[:, 0:16], 16.0, None, ALU.mult, None, [Bc], [Bthr])
            for tb in range(NTB):
                ts_ = slice(tb * 128, (tb + 1) * 128)
                xt, xtb = xin.get()
                xnt, xnb = xn.get()
                for i4 in range(4):
                    O.dma("sync", xt[:, i4 * 512:(i4 + 1) * 512], x1d[ts_, i4 * 512:(i4 + 1) * 512], [Bx1d], [xtb], par=(i4 > 0))
                st, stb = sm.get()
                O.memset("gpsimd", st[:, 0:1], 0.0, [stb])
                O.act(xnt[:], xt[:], AF.Square, [xtb], [xnb, stb], accum_out=st[:, 0:1])
                O.act(st[:, 1:2], st[:, 0:1], AF.Sqrt, [stb], [stb], scale=1.0 / D, bias=EPS)
                O.recip(st[:, 2:3], st[:, 1:2], [stb], [stb])
                O.ts("vector", xnt[:], xt[:], st[:, 2:3], None, ALU.mult, None, [xtb, stb], [xnb])
                for half in range(2):
                    pt, pb = psp.get()
                    ptb = pt[:].bitcast(BF16)
                    for kk in range(8):
                        k = half * 8 + kk
                        O.tr(ptb[:, kk * 128:(kk + 1) * 128], xnt[:, k * 128:(k + 1) * 128], identb[:], [xnb, Bc], [pb])
                    for kk in range(8):
                        k = half * 8 + kk
                        dst = h2[:, k, ts_]
                        src = ptb[:, kk * 128:(kk + 1) * 128]
                        if kk % 2 == 0:
                            O.act(dst, src, AF.Identity, [pb, Bfm4], [Bh2], scale=fm4[:, 3, k:k + 1], bias=fm4[:, 2, k:k + 1])
                        else:
                            O.ts("vector", dst, src, fm4[:, 3, k:k + 1], fm4[:, 2, k:k + 1], ALU.mult, ALU.add,
                                 [pb, Bfm4], [Bh2])
            for half in range(2):
              for qc in range(16):
                pt, pb = psp.get()
                for k in range(16):
                    O.mm(pt[:], wqb[:, k, qc * 128:(qc + 1) * 128], h2[:, k, half * 512:(half + 1) * 512],
                         k == 0, k == 15, [Bwq, Bh2], [pb])
                O.cp("scalar" if qc % 2 == 0 else "vector", qfm[:, qc, :], pt[:], [pb], [Bqf])
              for tb in range(half * 4, half * 4 + 4):
                ts_ = slice(tb * 128, (tb + 1) * 128)
                tl_ = slice((tb % 4) * 128, (tb % 4 + 1) * 128)
                sbt, sbb = sub.get()
                for g4 in range(4):
                    pt, pb = psp.get()
                    for q4 in range(4):
                        qc = g4 * 4 + q4
                        O.mm(pt[:, q4 * 128:(q4 + 1) * 128], qfm[:, qc, tl_], kT[:, qc, :], True, True, [Bqf, BkT], [pb])
                    O.cp("scalar", sbt[:, g4 * 512:(g4 + 1) * 512], pt[:], [pb], [sbb])
                for qc in range(16):
                    sv = sbt[:, qc * 128:(qc + 1) * 128]
                    w_, wb2 = wk.get()
                    S.op("vector", lambda e, o=v16[:, qc, 0:8], i=sv: e.max(out=o, in_=i), [sbb], [Bv16])
                    S.op("vector", lambda e, o=w_[:, 0:128], r=v16[:, qc, 0:8], i=sv: e.match_replace(out=o, in_to_replace=r, in_values=i, imm_value=NEG),
                         [sbb, Bv16], [wb2])
                    S.op("vector", lambda e, o=v16[:, qc, 8:16], i=w_[:, 0:128]: e.max(out=o, in_=i), [wb2], [Bv16])
                    S.op("vector", lambda e, o=idx[:, qc, 0:8], m=v16[:, qc, 0:8], i=sv: e.max_index(out=o, in_max=m, in_values=i), [sbb, Bv16], [Bidx])
                    S.op("vector", lambda e, o=idx[:, qc, 8:16], m=v16[:, qc, 8:16], i=sv: e.max_index(out=o, in_max=m, in_values=i), [sbb, Bv16], [Bidx])
                O.cp("vector", idxf[:], idx[:], [Bidx], [Bidxf])
                O.tt("vector", mkap(cand[:, 0, 0:1], [[256, 8], [16, 16], [1, 16]]),
                     mkap(v16[:, 0, 0:1], [[32, 8], [1, 16], [0, 16]]),
                     mkap(v16[:, 1, 0:1], [[32, 8], [0, 16], [1, 16]]), ALU.add, [Bv16], [Bcand])
                for hh in range(8):
                    cvw = cand[:, hh, :]
                    w_, wb2 = wk.get()
                    S.op("vector", lambda e, o=tv[:, hh, 0:8], i=cvw: e.max(out=o, in_=i), [Bcand], [Btv])
                    S.op("vector", lambda e, o=w_[:], r=tv[:, hh, 0:8], i=cvw: e.match_replace(out=o, in_to_replace=r, in_values=i, imm_value=NEG),
                         [Bcand, Btv], [wb2])
                    S.op("vector", lambda e, o=tv[:, hh, 8:16], i=w_[:]: e.max(out=o, in_=i), [wb2], [Btv])
                    S.op("vector", lambda e, o=pos[:, hh, 0:8], m=tv[:, hh, 0:8], i=cvw: e.max_index(out=o, in_max=m, in_values=i), [Bcand, Btv], [Bpos])
                    S.op("vector", lambda e, o=pos[:, hh, 8:16], m=tv[:, hh, 8:16], i=cvw: e.max_index(out=o, in_max=m, in_values=i), [Bcand, Btv], [Bpos])
                O.cp("vector", posf[:], pos[:], [Bpos], [Bposf])
                ge, geb = big.get()
                pr, prb = big.get()
                A4 = [[16, 8], [1, 16], [0, 16]]
                O.tt("vector", mkap(ge[:, 0:1], [[256, 8], [16, 16], [1, 16]]), mkap(posf[:, 0, 0:1], A4),
                     mkap(thr[:, 0:1], [[0, 8], [0, 16], [1, 16]]), ALU.is_ge, [Bposf, Bthr], [geb])
                O.cp("vector", mkap(d1[:, 0, 0:1], [[16, 8], [1, 1]]), mkap(idxf[:, 0, 0:1], [[32, 8], [1, 1]]), [Bidxf], [Bd1])
                O.tt("vector", mkap(d1[:, 0, 1:2], [[16, 8], [1, 15]]), mkap(idxf[:, 0, 1:2], [[32, 8], [1, 15]]),
                     mkap(idxf[:, 0, 0:1], [[32, 8], [1, 15]]), ALU.subtract, [Bidxf], [Bd1])
                O.tt("vector", mkap(pr[:, 0:1], [[256, 8], [16, 16], [1, 16]]), mkap(ge[:, 0:1], [[256, 8], [16, 16], [1, 16]]),
                     mkap(d1[:, 0, 0:1], [[16, 8], [0, 16], [1, 16]]), ALU.mult, [geb, Bd1], [prb])
                O.red(IJW[:, 0, :], mkap(pr[:, 0:1], [[16, 128], [1, 16]]), [prb], [BIJW])
                O.red(asel[:], mkap(ge[:, 0:1], [[16, 128], [1, 16]]), [geb], [Basel])
                O.ts("vector", asel[:], asel[:], -16.0, 16.0, ALU.mult, ALU.add, [Basel], [Basel])
                O.tt("vector", bsel[:], asel[:], mkap(posf[:, 0, 0:1], [[1, 128]]), ALU.add, [Basel, Bposf], [Bbsel])
                eq, eqb = big.get()
                O.tt("vector", mkap(eq[:, 0:1], [[256, 8], [16, 16], [1, 16]]), mkap(bsel[:, 0:1], A4),
                     mkap(iotaf[:, 0:1], [[0, 8], [0, 16], [1, 16]]), ALU.is_equal, [Bbsel, Bc], [eqb])
                O.tt("vector", mkap(eq[:, 0:1], [[256, 8], [16, 16], [1, 16]]), mkap(eq[:, 0:1], [[256, 8], [16, 16], [1, 16]]),
                     mkap(idxf[:, 1, 0:1], [[32, 8], [0, 16], [1, 16]]), ALU.mult, [eqb, Bidxf], [eqb])
                O.red(IJW[:, 1, :], mkap(eq[:, 0:1], [[16, 128], [1, 16]]), [eqb], [BIJW])
                O.tt("vector", mkap(ew[:, 0:1], [[16, 8], [1, 16]]), mkap(tv[:, 0, 0:1], [[16, 8], [1, 16]]),
                     mkap(tv[:, 0, 0:1], [[16, 8], [0, 16]]), ALU.subtract, [Btv], [Bew])
                O.act(ew[:], ew[:], AF.Exp, [Bew], [Bew])
                z_, zb_ = sm.get()
                O.red(z_[:, 0:8], mkap(ew[:, 0:1], [[16, 8], [1, 16]]), [Bew], [zb_])
                O.recip(z_[:, 8:16], z_[:, 0:8], [zb_], [zb_])
                O.tt("vector", mkap(IJW[:, 2, 0:1], [[16, 8], [1, 16]]), mkap(ew[:, 0:1], [[16, 8], [1, 16]]),
                     mkap(z_[:, 8:9], [[1, 8], [0, 16]]), ALU.mult, [Bew, zb_], [BIJW])
                pt, pb = psp.get()
                for i3 in range(3):
                    O.tr(pt[:, i3 * 128:(i3 + 1) * 128], IJW[:, i3, :], ident32[:], [BIJW, Bc], [pb])
                O.cp("scalar", ITt[:, :, ts_], mkap(pt[:, 0:1], [[128, 3], [1, 128]]), [pb], [BIT])
            S.emit()
        if stop_after == 3:
            with ExitStack() as es:
                bo = Buf()
                for i3 in range(3):
                    O.dma("sync", dbg_out[:, i3 * 1024:(i3 + 1) * 1024], ITt[:, i3, :], [BIT], [bo])
                S.wait_all("sync", [bo])
                S.emit()
            ies.close()
            return
        with ExitStack() as es:
            psp = Pool(nc, es, "p2cps", 4, [128, 512], F32, psum=True)
            Gs = Pool(nc, es, "Gs", 2, [128, 128, 128], BF16)
            EJ = Pool(nc, es, "EJ", 6, [128, 128], BF16)
            WI = Pool(nc, es, "WI", 6, [128, 128], BF16)
            firstg = True
            for tb in range(NTB):
                gs, gsb = Gs.get()
                for t4 in range(32):
                    pt, pb = psp.get()
                    for tl in range(4):
                        t = tb * 128 + t4 * 4 + tl
                        ej, ejb = EJ.get()
                        wi, wib = WI.get()
                        O.ts("vector", ej[:], iotaf[:], ITt[:, 1, t:t + 1], None, ALU.is_equal, None, [Bc, BIT], [ejb])
                        O.ts("vector", wi[:], iotaf[:], ITt[:, 0, t:t + 1], ITt[:, 2, t:t + 1], ALU.is_equal, ALU.mult,
                             [Bc, BIT], [wib])
                        O.mm(pt[:, tl * 128:(tl + 1) * 128], ej[:], wi[:], True, True, [ejb, wib], [pb])
                    dst = mkap(gs[:, 0, t4 * 4:t4 * 4 + 1], [[128, 128], [1, 4]])
                    src = mkap(pt[:, 0:1], [[1, 128], [128, 4]])
                    O.cp("scalar", dst, src, [pb], [gsb])
                for g16 in range(16):
                    O.dma("sync", Gd[g16 * 8:(g16 + 1) * 8, :, tb * 128:(tb + 1) * 128].rearrange("i j t -> j i t"),
                          gs[:, g16 * 8:(g16 + 1) * 8, :], [gsb], [BGd], par=(not firstg))
                    firstg = False
            S.wait_all("sync", [BGd])
            S.emit()
        ies.close()
        with ExitStack() as aes:
            acc = aes.enter_context(nc.sbuf_tensor("acc", [128, 8, D], F32)); Bacc = [Buf() for _ in range(8)]
            with ExitStack() as es:
                psp = Pool(nc, es, "p2dps", 8, [128, 512], F32, psum=True)
                UTp = Pool(nc, es, "UTp", 2, [128, 16, 512], BF16)
                Vp = Pool(nc, es, "Vp", 2, [128, 4, D], BF16)
                Gp = Pool(nc, es, "Gp", 2, [128, 4, 1024], BF16)
                ATp = Pool(nc, es, "ATp", 2, [128, 4, 1024], BF16)
                tmp = Pool(nc, es, "p2dt", 4, [128, 512], F32)
                NG = DEBUG.get("ngroups", 32)
                for g in range(NG):
                    e0 = g * 512
                    ut, utb = UTp.get()
                    vt, vtb = Vp.get()
                    gg, ggb = Gp.get()
                    at, atb = ATp.get()
                    for k in range(16):
                        O.dma("gpsimd", ut[:, k, :], UT[k * 128:(k + 1) * 128, e0:e0 + 512], [], [utb], par=(k > 0))
                    for ii in range(4):
                        O.dma("sync", gg[:, ii, :], Gd[4 * g + ii], [BGd], [ggb], par=(ii > 0))
                    for ii in range(4):
                        r0 = (4 * g + ii) * 128
                        for hf in range(2):
                            O.dma("gpsimd", vt[:, ii, hf * 1024:(hf + 1) * 1024], Vt[r0:r0 + 128, hf * 1024:(hf + 1) * 1024], [], [vtb],
                                  par=(ii > 0 or hf > 0))
                    for ii in range(4):
                        for half in range(2):
                            hs = slice(half * 512, (half + 1) * 512)
                            ps_, psb = psp.get()
                            for k in range(16):
                                O.mm(ps_[:], ut[:, k, ii * 128:(ii + 1) * 128], h2[:, k, hs], k == 0, k == 15, [utb, Bh2], [psb])
                            t1_, t1b = tmp.get()
                            O.act(t1_[:], ps_[:], AF.Gelu_apprx_tanh, [psb], [t1b])
                            O.tt("gpsimd", at[:, ii, hs], t1_[:], gg[:, ii, hs], ALU.mult, [t1b, ggb], [atb])
                    for tb in range(NTB):
                        ts_ = slice(tb * 128, (tb + 1) * 128)
                        for cb in range(4):
                            cs = slice(cb * 512, (cb + 1) * 512)
                            pv, pvb = psp.get()
                            for ii in range(4):
                                O.mm(pv[:], at[:, ii, ts_], vt[:, ii, cs], ii == 0, ii == 3, [atb, vtb], [pvb])
                            if g == 0:
                                O.cp("vector", acc[:, tb, cs], pv[:], [pvb], [Bacc[tb]])
                            else:
                                O.tt("vector", acc[:, tb, cs], acc[:, tb, cs], pv[:], ALU.add, [Bacc[tb], pvb], [Bacc[tb]])
                S.emit()
            with ExitStack() as es:
                def sb(name, shape, dt):
                    return es.enter_context(nc.sbuf_tensor(name, shape, dt))
                sm = Pool(nc, es, "p2es", 4, [128, 16], F32)
                xin = Pool(nc, es, "x1f", 2, [128, D], F32)
                gate2 = sb("gate2", [128, D], F32); Bg2 = Buf()
                gft = sb("gft", [128, D], F32); Bgf = Buf()
                jk = sb("jk", [128, D], BF16); Bjk = Buf()
                for i4 in range(4):
                    O.dma("sync", gate2[:, i4 * 512:(i4 + 1) * 512], modsc[:, 10240 + i4 * 512:10240 + (i4 + 1) * 512], [], [Bg2], par=(i4 > 0))
                    O.dma("sync", gft[:, i4 * 512:(i4 + 1) * 512], gf_bc[:, i4 * 512:(i4 + 1) * 512], [], [Bgf], par=(i4 > 0))
                firsty = True
                for tb in range(NTB):
                    ts_ = slice(tb * 128, (tb + 1) * 128)
                    xt, xtb = xin.get()
                    for i4 in range(4):
                        O.dma("sync", xt[:, i4 * 512:(i4 + 1) * 512], x1d[ts_, i4 * 512:(i4 + 1) * 512], [Bx1d], [xtb], par=(i4 > 0))
                    a_ = acc[:, tb, :]
                    O.tt("vector", a_, a_, gate2[:], ALU.mult, [Bacc[tb], Bg2], [Bacc[tb]])
                    O.tt("gpsimd", a_, a_, xt[:], ALU.add, [Bacc[tb], xtb], [Bacc[tb]])
                    st, stb = sm.get()
                    O.memset("gpsimd", st[:, 0:1], 0.0, [stb])
                    O.act(jk[:], a_, AF.Square, [Bacc[tb]], [Bjk, stb], accum_out=st[:, 0:1])
                    O.act(st[:, 1:2], st[:, 0:1], AF.Sqrt, [stb], [stb], scale=1.0 / D, bias=EPS)
                    O.recip(st[:, 2:3], st[:, 1:2], [stb], [stb])
                    O.stt(a_, a_, st[:, 2:3], gft[:], ALU.mult, ALU.mult, [Bacc[tb], stb, Bgf], [Bacc[tb]])
                    for i4 in range(4):
                        O.dma("sync", y[ts_, i4 * 512:(i4 + 1) * 512], acc[:, tb, i4 * 512:(i4 + 1) * 512], [Bacc[tb]], [By], par=(not firsty))
                        firsty = False
                S.wait_all("sync", [By])
                S.emit()


def _prep_inputs(inp):
    f = np.float32
    x = np.asarray(inp["x"], f)
    c = np.asarray(inp["c"], f)
    w_in = np.asarray(inp["w_in"], f)[0]
    shared = {}
    shared["w_ada"] = np.ascontiguousarray(np.asarray(inp["w_ada"], f)[0])
    shared["bada_bc"] = np.ascontiguousarray(np.broadcast_to(np.asarray(inp["b_ada"], f)[0][None, :], (128, 6 * D)))
    shared["g1_bc"] = np.ascontiguousarray(np.broadcast_to(np.asarray(inp["norm1_g"], f)[0][None, :], (128, D)))
    shared["g2_bc"] = np.ascontiguousarray(np.broadcast_to(np.asarray(inp["norm2_g"], f)[0][None, :], (128, D)))
    shared["gf_bc"] = np.ascontiguousarray(np.broadcast_to(np.asarray(inp["final_norm_g"], f)[None, :], (128, D)))
    O_G, O_X, O_Z, O_XBC, O_DT = 0, 2048, 4096, 6144, 9216
    wins, wout_rows = [], []
    P = {k: [] for k in ("lru_cw", "lru_cb", "lru_wa", "lru_wi", "lru_ba", "lru_bi", "lru_lam", "ssd_cw", "ssd_cb",
                         "ssd_dtb", "ssd_alog", "ssd_d", "ssd_ng")}
    lcw = np.asarray(inp["lru_conv_w"], f)[0]
    lcb = np.asarray(inp["lru_conv_b"], f)[0]
    lwa = np.asarray(inp["lru_w_a"], f)[0]
    lwi = np.asarray(inp["lru_w_i"], f)[0]
    lba = np.asarray(inp["lru_b_a"], f)[0]
    lbi = np.asarray(inp["lru_b_i"], f)[0]
    lam = np.asarray(inp["lru_lambda"], f)[0]
    scw = np.asarray(inp["ssd_conv_w"], f)[0]
    scb = np.asarray(inp["ssd_conv_b"], f)[0]
    dtb = np.asarray(inp["ssd_dt_bias"], f)[0]
    alog = np.asarray(inp["ssd_a_log"], f)[0]
    sd = np.asarray(inp["ssd_d"], f)[0]
    sng = np.asarray(inp["ssd_norm_g"], f)[0]
    for j in range(4):
        cols = np.concatenate([
            O_G + j * 512 + np.arange(512), O_X + j * 512 + np.arange(512), O_Z + j * 512 + np.arange(512),
            O_XBC + j * 512 + np.arange(512), O_XBC + 2048 + j * 128 + np.arange(128),
            O_XBC + 2560 + j * 128 + np.arange(128), O_DT + j * 8 + np.arange(8)])
        wins.append(w_in[:, cols])
        lch = j * 512 + np.arange(512)
        P["lru_cw"].append(lcw[:, lch].reshape(4, 4, 128).transpose(2, 1, 0).reshape(128, 16))
        P["lru_cb"].append(lcb[lch].reshape(4, 128).T)
        P["lru_wa"].append(lwa[4 * j:4 * j + 4])
        P["lru_wi"].append(lwi[4 * j:4 * j + 4])
        P["lru_ba"].append(lba[4 * j:4 * j + 4].T)
        P["lru_bi"].append(lbi[4 * j:4 * j + 4].T)
        P["lru_lam"].append(lam[lch].reshape(4, 128).T)
        sch = np.concatenate([j * 512 + np.arange(512), 2048 + j * 128 + np.arange(128), 2560 + j * 128 + np.arange(128)])
        P["ssd_cw"].append(scw[:, sch].reshape(4, 6, 128).transpose(2, 1, 0).reshape(128, 24))
        P["ssd_cb"].append(scb[sch].reshape(6, 128).T)
        P["ssd_dtb"].append(np.broadcast_to(dtb[8 * j:8 * j + 8][None, :], (128, 8)))
        P["ssd_alog"].append(np.broadcast_to(alog[8 * j:8 * j + 8][None, :], (128, 8)))
        P["ssd_d"].append(np.broadcast_to(np.repeat(sd[8 * j:8 * j + 8], 64)[None, :], (128, 512)))
        P["ssd_ng"].append(np.broadcast_to(sng[j * 512:(j + 1) * 512][None, :], (128, 512)))
        wout_rows.append(np.concatenate([j * 512 + np.arange(512), 2048 + j * 512 + np.arange(512)]))
    shared["win"] = np.ascontiguousarray(np.stack(wins))
    for k, v in P.items():
        shared[k] = np.ascontiguousarray(np.stack([np.asarray(a, f) for a in v]))
    shared["wout"] = np.ascontiguousarray(np.asarray(inp["w_out"], f)[0][np.concatenate(wout_rows)])
    shared["wq"] = np.ascontiguousarray(np.asarray(inp["peer_w_q"], f)[0])
    sk = np.asarray(inp["peer_sub_keys"], f)[0]
    shared["keysT"] = np.ascontiguousarray(sk.reshape(16, 128, 128).transpose(0, 2, 1))
    shared["UT"] = np.ascontiguousarray(np.asarray(inp["peer_u"], f)[0].T)
    shared["V"] = np.ascontiguousarray(np.asarray(inp["peer_v"], f)[0])
    maps = []
    for core in range(8):
        b, q = core // 4, core % 4
        m = dict(shared)
        m["xb"] = np.ascontiguousarray(x[b])
        m["xq"] = np.ascontiguousarray(x[b, q * 1024:(q + 1) * 1024])
        m["cfm"] = np.ascontiguousarray(c[b].reshape(16, 128).T)
        s = np.zeros((128, 4), f)
        s[:, q] = 1.0
        m["sel"] = s
        maps.append(m)
    return maps


def kernel(**inputs):
    maps = _prep_inputs(inputs)
    nc = build()
    res = run_bass_kernel_spmd(nc, maps, core_ids=list(range(8)))
    out = np.zeros((2, SEQ, D), np.float32)
    for core in range(8):
        b, q = core // 4, core % 4
        out[b, q * 1024:(q + 1) * 1024] = res.results[core]["y"]
    return out
```
